# Optimizing a Trainium2 kernel written in Bass

```python
import math
import jax
import jax.numpy as jnp
from jax import lax
import numpy as np

D_MODEL = 2048
BATCH = 4
SEQ = 2048
DEPTH = 4
DEC_BATCH = 128
DEC_SEQ = 4
PAST_LEN = 16384
PAGE_SIZE = 128

N_META = 16
GROUP_WIDTH = D_MODEL // 4
EPS = 1e-6
R_HEAD_DIM = 64
R_HEADS = GROUP_WIDTH // R_HEAD_DIM
R_WIDTH = R_HEADS * R_HEAD_DIM
R_DECAY_RANK = 64
R_ICL_RANK = 64
R_GATE_RANK = 128
R_COLS = 3 * R_WIDTH + R_DECAY_RANK + R_ICL_RANK + R_GATE_RANK
RWKV_DECAY_SCALE = 0.606531
RWKV_GN_EPS = 64e-5
S5_GROUP_CH = 16
S5_GROUPS = GROUP_WIDTH // S5_GROUP_CH
S5_WIDTH = S5_GROUPS * S5_GROUP_CH
S5_STATE = 64
S5_COLS = S5_WIDTH
H_HEAD_DIM = 128
H_HEADS = GROUP_WIDTH // H_HEAD_DIM
H_WIDTH = H_HEADS * H_HEAD_DIM
H_COLS = 4 * H_WIDTH
HGRN_MAX_INPUT = 1.0 - 1e-4
G_VAL_DIM = 128
G_HEADS = GROUP_WIDTH // G_VAL_DIM
G_KEY_DIM = G_VAL_DIM // 2
G_WIDTH = G_HEADS * G_VAL_DIM
G_QK = G_HEADS * G_KEY_DIM
G_GATE_RANK = 16
G_COLS = 2 * G_QK + G_WIDTH + G_GATE_RANK + G_WIDTH
GLA_GATE_NORM = 16.0

MIX_WIDTH = R_WIDTH + S5_WIDTH + H_WIDTH + G_WIDTH
IN_COLS = R_COLS + S5_COLS + H_COLS + G_COLS
CHUNK = 16
D_FF = ((8 * D_MODEL // 3 + 127) // 128) * 128
CONV_W = 3

kernel_name = 'hybrid_rwkv7_s5_hgrn2_gla_decode_step'


def _rmsnorm(x, g):
    xf = x.astype(jnp.float32)
    y = xf * lax.rsqrt(jnp.mean(xf * xf, axis=-1, keepdims=True) + EPS)
    return (y * g.astype(jnp.float32)).astype(x.dtype)


def _head_rmsnorm(o, g):
    y = o * lax.rsqrt(jnp.mean(o * o, axis=-1, keepdims=True) + EPS)
    return y * g.astype(jnp.float32).reshape(o.shape[-2:])


def _rwkv7_mix(p, shift_prev, S0, mu, w0, w_up, a0, a_up, g_up, k_k, k_a, r_k, ln_g):
    f32 = jnp.float32
    Bn, T, _ = p.shape
    pf = p.astype(f32)
    p_prev = jnp.concatenate([shift_prev.astype(f32)[:, None, :], pf[:, :-1]], axis=1)
    ps = pf + (p_prev - pf) * mu.astype(f32)
    W = R_WIDTH
    r, k, v, w_lo, a_lo, g_lo = jnp.split(
        ps, [W, 2 * W, 3 * W, 3 * W + R_DECAY_RANK, 3 * W + R_DECAY_RANK + R_ICL_RANK], axis=-1)
    log_w = -RWKV_DECAY_SCALE * jax.nn.sigmoid(w0.astype(f32) + jnp.tanh(w_lo) @ w_up.astype(f32))
    a = jax.nn.sigmoid(a0.astype(f32) + a_lo @ a_up.astype(f32))
    g = jax.nn.sigmoid(g_lo) @ g_up.astype(f32)
    hs = lambda t: t.reshape(Bn, T, R_HEADS, R_HEAD_DIM)
    r, k, v, w, a = hs(r), hs(k), hs(v), hs(jnp.exp(log_w)), hs(a)
    kk = k * k_k.astype(f32).reshape(R_HEADS, R_HEAD_DIM)
    kk = kk / jnp.maximum(jnp.sqrt(jnp.sum(kk * kk, axis=-1, keepdims=True)), 1e-12)
    k = k * (1.0 + (a - 1.0) * k_a.astype(f32).reshape(R_HEADS, R_HEAD_DIM))
    ka = kk * a

    def step(S, inp):
        r_t, w_t, k_t, v_t, kk_t, ka_t = inp
        sa = jnp.einsum('bhvk,bhk->bhv', S, kk_t)
        S = (S * w_t[:, :, None, :] - sa[..., None] * ka_t[:, :, None, :]
             + v_t[..., None] * k_t[:, :, None, :])
        return S, jnp.einsum('bhvk,bhk->bhv', S, r_t)

    tm = lambda t: jnp.moveaxis(t, 1, 0)
    S, y = lax.scan(step, S0.astype(f32), (tm(r), tm(w), tm(k), tm(v), tm(kk), tm(ka)))
    y = jnp.moveaxis(y, 0, 1)
    mean = jnp.mean(y, axis=-1, keepdims=True)
    var = jnp.mean((y - mean) ** 2, axis=-1, keepdims=True)
    y = (y - mean) * lax.rsqrt(var + RWKV_GN_EPS) * ln_g.astype(f32).reshape(R_HEADS, R_HEAD_DIM)
    y = y + jnp.sum(r * k * r_k.astype(f32), axis=-1, keepdims=True) * v
    out = y.reshape(Bn, T, R_WIDTH) * g
    return out.astype(p.dtype), p[:, -1].astype(shift_prev.dtype), S.astype(S0.dtype)


def _complex_affine_combine(e1, e2):
    a1r, a1i, b1r, b1i = e1
    a2r, a2i, b2r, b2i = e2
    return (a2r * a1r - a2i * a1i,
            a2r * a1i + a2i * a1r,
            a2r * b1r - a2i * b1i + b2r,
            a2r * b1i + a2i * b1r + b2i)


def _s5_mix(u, h0_re, h0_im, A_re, A_im, log_dt, B_re, B_im, C_re, C_im, D, w_glu, b_glu):
    f32 = jnp.float32
    Bn, T, _ = u.shape
    uf = u.astype(f32)
    ug = uf.reshape(Bn, T, S5_GROUPS, S5_GROUP_CH)
    A_re = A_re.astype(f32)
    A_im = A_im.astype(f32)
    dt = jnp.exp(log_dt.astype(f32))[:, None]
    mag = jnp.exp(A_re * dt)
    ab_re = mag * jnp.cos(A_im * dt)
    ab_im = mag * jnp.sin(A_im * dt)
    den = A_re * A_re + A_im * A_im
    n_re = ab_re - 1.0
    co_re = (n_re * A_re + ab_im * A_im) / den
    co_im = (ab_im * A_re - n_re * A_im) / den
    B_re = B_re.astype(f32)
    B_im = B_im.astype(f32)
    bb_re = co_re[..., None] * B_re - co_im[..., None] * B_im
    bb_im = co_re[..., None] * B_im + co_im[..., None] * B_re
    bu_re = jnp.einsum('btgc,gpc->btgp', ug, bb_re)
    bu_im = jnp.einsum('btgc,gpc->btgp', ug, bb_im)
    a_re = jnp.broadcast_to(ab_re, bu_re.shape)
    a_im = jnp.broadcast_to(ab_im, bu_im.shape)
    cum_re, cum_im, h_re, h_im = lax.associative_scan(
        _complex_affine_combine, (a_re, a_im, bu_re, bu_im), axis=1)
    h0r = h0_re.astype(f32)[:, None]
    h0i = h0_im.astype(f32)[:, None]
    h_re = h_re + cum_re * h0r - cum_im * h0i
    h_im = h_im + cum_re * h0i + cum_im * h0r
    y = (jnp.einsum('btgp,gcp->btgc', h_re, C_re.astype(f32))
         - jnp.einsum('btgp,gcp->btgc', h_im, C_im.astype(f32)))
    y = y.reshape(Bn, T, S5_WIDTH) + D.astype(f32) * uf
    y = jax.nn.gelu(y)
    y = y * jax.nn.sigmoid(y @ w_glu.astype(f32) + b_glu.astype(f32))
    return (y.astype(u.dtype), h_re[:, -1].astype(h0_re.dtype), h_im[:, -1].astype(h0_im.dtype))


def _chunked_gla(q, k, v, log_f, S0):
    f32 = jnp.float32
    Bn, T, H, _ = q.shape
    V = v.shape[-1]
    n_chunks = -(-T // CHUNK)
    pad = n_chunks * CHUNK - T

    def to_chunks(a):
        a = jnp.pad(a.astype(f32), ((0, 0), (0, pad), (0, 0), (0, 0)))
        return jnp.moveaxis(a.reshape(Bn, n_chunks, CHUNK, H, a.shape[-1]), 1, 0)

    causal = jnp.tril(jnp.ones((CHUNK, CHUNK), dtype=bool))[None, :, :, None, None]

    def step(S, inp):
        qi, ki, vi, gi = inp
        b = jnp.cumsum(gi, axis=1)
        b_last = b[:, -1]
        o_inter = jnp.einsum('bchk,bhkv->bchv', qi * jnp.exp(b), S)
        diff = jnp.where(causal, b[:, :, None] - b[:, None, :], 0.0)
        decay = jnp.where(causal, jnp.exp(diff), 0.0)
        att = jnp.einsum('bihk,bjhk,bijhk->bhij', qi, ki, decay)
        o_intra = jnp.einsum('bhij,bjhv->bihv', att, vi)
        S_new = (S * jnp.exp(b_last)[..., None]
                 + jnp.einsum('bchk,bchv->bhkv', ki * jnp.exp(b_last[:, None] - b), vi))
        return S_new, o_inter + o_intra

    S, o = lax.scan(step, S0.astype(f32),
                    (to_chunks(q), to_chunks(k), to_chunks(v), to_chunks(log_f)))
    o = jnp.moveaxis(o, 0, 1).reshape(Bn, n_chunks * CHUNK, H, V)[:, :T]
    return o, S


def _hgrn2_mix(p, S0, lb, norm_g):
    f32 = jnp.float32
    Bn, T, _ = p.shape
    q, f_raw, i, g = jnp.split(p.astype(f32), 4, axis=-1)
    lb = lb.astype(f32)
    k = jnp.minimum((1.0 - lb) * jax.nn.sigmoid(-f_raw), HGRN_MAX_INPUT)
    log_f = jnp.log1p(-k)
    hs = lambda t: t.reshape(Bn, T, H_HEADS, H_HEAD_DIM)
    o, S = _chunked_gla(hs(jax.nn.silu(q)), hs(k), hs(i), hs(log_f), S0)
    o = _head_rmsnorm(o, norm_g) * hs(jax.nn.silu(g))
    return o.reshape(Bn, T, H_WIDTH).astype(p.dtype), S.astype(S0.dtype)


def _gla_mix(p, S0, gk_up, gk_b, norm_g):
    f32 = jnp.float32
    Bn, T, _ = p.shape
    q, k, v, gk_lo, gate = jnp.split(
        p.astype(f32), [G_QK, 2 * G_QK, 2 * G_QK + G_WIDTH, 2 * G_QK + G_WIDTH + G_GATE_RANK], axis=-1)
    log_g = jax.nn.log_sigmoid(gk_lo @ gk_up.astype(f32) + gk_b.astype(f32)) / GLA_GATE_NORM
    hk = lambda t: t.reshape(Bn, T, G_HEADS, G_KEY_DIM)
    hv = lambda t: t.reshape(Bn, T, G_HEADS, G_VAL_DIM)
    o, S = _chunked_gla(hk(q) * (G_KEY_DIM ** -0.5), hk(k), hv(v), hk(log_g), S0)
    o = _head_rmsnorm(o, norm_g) * hv(jax.nn.silu(gate))
    return o.reshape(Bn, T, G_WIDTH).astype(p.dtype), S.astype(S0.dtype)


def _conv_ffn(x, buf, w_up, conv_w, conv_b, w_down):
    T = x.shape[1]
    u, gate = jnp.split(x @ w_up, 2, axis=-1)
    u_ext = jnp.concatenate([buf.astype(u.dtype), u], axis=1)
    c = conv_b
    for j in range(CONV_W):
        c = c + conv_w[j] * u_ext[:, j:j + T]
    out = (jax.nn.gelu(c) * gate) @ w_down
    return out.astype(x.dtype), u_ext[:, T:].astype(buf.dtype)


def _layer(x, st, prm, lb):
    rw_S, rw_shift, s5_re, s5_im, hg_S, gl_S, ffn_buf = st
    h = _rmsnorm(x, prm['norm_mix'])
    p = h @ prm['w_in']
    p_r, p_s5, p_h, p_g = jnp.split(
        p, [R_COLS, R_COLS + S5_COLS, R_COLS + S5_COLS + H_COLS], axis=-1)
    y_r, rw_shift_new, rw_S_new = _rwkv7_mix(
        p_r, rw_shift, rw_S, prm['rwkv_mu'], prm['rwkv_w0'], prm['rwkv_w_up'], prm['rwkv_a0'],
        prm['rwkv_a_up'], prm['rwkv_g_up'], prm['rwkv_k_k'], prm['rwkv_k_a'], prm['rwkv_r_k'],
        prm['rwkv_ln'])
    y_s5, s5_re_new, s5_im_new = _s5_mix(
        p_s5, s5_re, s5_im, prm['s5_A_re'], prm['s5_A_im'], prm['s5_log_dt'], prm['s5_B_re'],
        prm['s5_B_im'], prm['s5_C_re'], prm['s5_C_im'], prm['s5_D'], prm['s5_w_glu'], prm['s5_b_glu'])
    y_h, hg_S_new = _hgrn2_mix(p_h, hg_S, lb, prm['hgrn_norm'])
    y_g, gl_S_new = _gla_mix(p_g, gl_S, prm['gla_gk_up'], prm['gla_gk_b'], prm['gla_norm'])
    mix = jnp.concatenate([y_r, y_s5, y_h, y_g], axis=-1) @ prm['w_out']
    x = x + mix.astype(x.dtype)
    y_f, ffn_buf_new = _conv_ffn(_rmsnorm(x, prm['norm_ffn']), ffn_buf, prm['ffn_w_up'],
                                 prm['ffn_conv_w'], prm['ffn_conv_b'], prm['ffn_w_down'])
    x = x + y_f
    return x, (rw_S_new, rw_shift_new, s5_re_new, s5_im_new, hg_S_new, gl_S_new, ffn_buf_new)


def setup_inputs(seed: int = 0) -> dict:
    key = jax.random.key(seed)
    keys = iter(jax.random.split(key, 64))
    f32 = jnp.float32
    L = DEPTH

    def nrm(shape, scale):
        return scale * jax.random.normal(next(keys), shape, f32)

    def unif(shape, lo, hi):
        return jax.random.uniform(next(keys), shape, f32, lo, hi)

    n_idx = jnp.arange(S5_STATE, dtype=f32)
    return {
        'x_prompt': nrm((BATCH, SEQ, D_MODEL), 1.0),
        'x_sample': nrm((DEC_BATCH, DEC_SEQ, D_MODEL), 1.0),
        'state_rwkv': nrm((L, DEC_BATCH, R_HEADS, R_HEAD_DIM, R_HEAD_DIM), 0.5),
        'state_rwkv_shift': nrm((L, DEC_BATCH, R_COLS), 1.0),
        'state_s5_re': nrm((L, DEC_BATCH, S5_GROUPS, S5_STATE), 0.5),
        'state_s5_im': nrm((L, DEC_BATCH, S5_GROUPS, S5_STATE), 0.5),
        'state_hgrn': nrm((L, DEC_BATCH, H_HEADS, H_HEAD_DIM, H_HEAD_DIM), 0.5),
        'state_gla': nrm((L, DEC_BATCH, G_HEADS, G_KEY_DIM, G_VAL_DIM), 0.5),
        'state_ffn_conv': nrm((L, DEC_BATCH, CONV_W - 1, D_FF), 1.0),
        'meta_tokens': nrm((N_META, D_MODEL), 1.0),
        'norm_mix': 1.0 + nrm((L, D_MODEL), 0.02),
        'w_in': nrm((L, D_MODEL, IN_COLS), D_MODEL ** -0.5),
        'w_out': nrm((L, MIX_WIDTH, D_MODEL), MIX_WIDTH ** -0.5),
        'rwkv_mu': unif((L, R_COLS), 0.0, 1.0),
        'rwkv_w0': nrm((L, R_WIDTH), 0.5),
        'rwkv_w_up': nrm((L, R_DECAY_RANK, R_WIDTH), 0.5 * R_DECAY_RANK ** -0.5),
        'rwkv_a0': nrm((L, R_WIDTH), 0.5),
        'rwkv_a_up': nrm((L, R_ICL_RANK, R_WIDTH), 0.5 * R_ICL_RANK ** -0.5),
        'rwkv_g_up': nrm((L, R_GATE_RANK, R_WIDTH), R_GATE_RANK ** -0.5),
        'rwkv_k_k': 0.85 + nrm((L, R_WIDTH), 0.05),
        'rwkv_k_a': 1.0 + nrm((L, R_WIDTH), 0.05),
        'rwkv_r_k': nrm((L, R_HEADS, R_HEAD_DIM), 0.1),
        'rwkv_ln': 1.0 + nrm((L, R_WIDTH), 0.02),
        's5_A_re': -0.5 + nrm((L, S5_GROUPS, S5_STATE), 0.01),
        's5_A_im': math.pi * n_idx + nrm((L, S5_GROUPS, S5_STATE), 0.01),
        's5_log_dt': unif((L, S5_GROUPS), math.log(0.001), math.log(0.1)),
        's5_B_re': nrm((L, S5_GROUPS, S5_STATE, S5_GROUP_CH), (2.0 * S5_GROUP_CH) ** -0.5),
        's5_B_im': nrm((L, S5_GROUPS, S5_STATE, S5_GROUP_CH), (2.0 * S5_GROUP_CH) ** -0.5),
        's5_C_re': nrm((L, S5_GROUPS, S5_GROUP_CH, S5_STATE), (2.0 * S5_STATE) ** -0.5),
        's5_C_im': nrm((L, S5_GROUPS, S5_GROUP_CH, S5_STATE), (2.0 * S5_STATE) ** -0.5),
        's5_D': nrm((L, S5_WIDTH), 1.0),
        's5_w_glu': nrm((L, S5_WIDTH, S5_WIDTH), S5_WIDTH ** -0.5),
        's5_b_glu': nrm((L, S5_WIDTH), 0.02),
        'hgrn_lower_bounds': nrm((L, H_WIDTH), 0.1),
        'hgrn_norm': 1.0 + nrm((L, H_WIDTH), 0.02),
        'gla_gk_up': nrm((L, G_GATE_RANK, G_QK), G_GATE_RANK ** -0.5),
        'gla_gk_b': nrm((L, G_QK), 0.1),
        'gla_norm': 1.0 + nrm((L, G_WIDTH), 0.02),
        'norm_ffn': 1.0 + nrm((L, D_MODEL), 0.02),
        'ffn_w_up': nrm((L, D_MODEL, 2 * D_FF), D_MODEL ** -0.5),
        'ffn_conv_w': nrm((L, CONV_W, D_FF), CONV_W ** -0.5),
        'ffn_conv_b': nrm((L, D_FF), 0.02),
        'ffn_w_down': nrm((L, D_FF, D_MODEL), D_FF ** -0.5),
        'norm_final': 1.0 + nrm((D_MODEL,), 0.02),
    }


def reference(x_prompt, x_sample, state_rwkv, state_rwkv_shift, state_s5_re, state_s5_im,
              state_hgrn, state_gla, state_ffn_conv, meta_tokens, norm_mix, w_in, w_out,
              rwkv_mu, rwkv_w0, rwkv_w_up, rwkv_a0, rwkv_a_up, rwkv_g_up, rwkv_k_k, rwkv_k_a,
              rwkv_r_k, rwkv_ln, s5_A_re, s5_A_im, s5_log_dt, s5_B_re, s5_B_im, s5_C_re, s5_C_im,
              s5_D, s5_w_glu, s5_b_glu, hgrn_lower_bounds, hgrn_norm, gla_gk_up, gla_gk_b,
              gla_norm, norm_ffn, ffn_w_up, ffn_conv_w, ffn_conv_b, ffn_w_down, norm_final):
    f32 = jnp.float32
    lb_soft = jax.nn.softmax(hgrn_lower_bounds.astype(f32), axis=0)
    lower_bound = jnp.cumsum(lb_soft, axis=0) - lb_soft[0]

    Bp = x_prompt.shape[0]
    dt = x_prompt.dtype
    meta = jnp.broadcast_to(meta_tokens.astype(dt)[None], (Bp, N_META, D_MODEL))
    xp = jnp.concatenate([meta, x_prompt], axis=1)
    xs = x_sample
    prompt_init = (
        jnp.zeros((Bp, R_HEADS, R_HEAD_DIM, R_HEAD_DIM), dt),
        jnp.zeros((Bp, R_COLS), dt),
        jnp.zeros((Bp, S5_GROUPS, S5_STATE), dt),
        jnp.zeros((Bp, S5_GROUPS, S5_STATE), dt),
        jnp.zeros((Bp, H_HEADS, H_HEAD_DIM, H_HEAD_DIM), dt),
        jnp.zeros((Bp, G_HEADS, G_KEY_DIM, G_VAL_DIM), dt),
        jnp.zeros((Bp, CONV_W - 1, D_FF), dt),
    )
    prompt_states = []
    sample_states = []
    for l in range(DEPTH):
        prm = {
            'norm_mix': norm_mix[l], 'w_in': w_in[l], 'w_out': w_out[l],
            'rwkv_mu': rwkv_mu[l], 'rwkv_w0': rwkv_w0[l], 'rwkv_w_up': rwkv_w_up[l],
            'rwkv_a0': rwkv_a0[l], 'rwkv_a_up': rwkv_a_up[l], 'rwkv_g_up': rwkv_g_up[l],
            'rwkv_k_k': rwkv_k_k[l], 'rwkv_k_a': rwkv_k_a[l], 'rwkv_r_k': rwkv_r_k[l],
            'rwkv_ln': rwkv_ln[l],
            's5_A_re': s5_A_re[l], 's5_A_im': s5_A_im[l], 's5_log_dt': s5_log_dt[l],
            's5_B_re': s5_B_re[l], 's5_B_im': s5_B_im[l], 's5_C_re': s5_C_re[l],
            's5_C_im': s5_C_im[l], 's5_D': s5_D[l], 's5_w_glu': s5_w_glu[l], 's5_b_glu': s5_b_glu[l],
            'hgrn_norm': hgrn_norm[l],
            'gla_gk_up': gla_gk_up[l], 'gla_gk_b': gla_gk_b[l], 'gla_norm': gla_norm[l],
            'norm_ffn': norm_ffn[l], 'ffn_w_up': ffn_w_up[l], 'ffn_conv_w': ffn_conv_w[l],
            'ffn_conv_b': ffn_conv_b[l], 'ffn_w_down': ffn_w_down[l],
        }
        xp, st_p = _layer(xp, prompt_init, prm, lower_bound[l])
        st_in = (state_rwkv[l], state_rwkv_shift[l], state_s5_re[l], state_s5_im[l],
                 state_hgrn[l], state_gla[l], state_ffn_conv[l])
        xs, st_s = _layer(xs, st_in, prm, lower_bound[l])
        prompt_states.append(st_p)
        sample_states.append(st_s)

    y_prompt = _rmsnorm(xp[:, N_META:], norm_final)
    y_sample = _rmsnorm(xs, norm_final)
    sp = [jnp.stack([st[i] for st in prompt_states]) for i in range(7)]
    ss = [jnp.stack([st[i] for st in sample_states]) for i in range(7)]
    return (y_prompt, y_sample, sp[0], ss[0], sp[1], ss[1], sp[2], ss[2], sp[3], ss[3],
            sp[4], ss[4], sp[5], ss[5], sp[6], ss[6])
```

```python
import math
from contextlib import ExitStack
import numpy as np
import concourse.bass as bass
import concourse.mybir as mybir
from concourse.bass_utils import run_bass_kernel_spmd

F32 = mybir.dt.float32
BF16 = mybir.dt.bfloat16
AF = mybir.ActivationFunctionType
ALU = mybir.AluOpType
AX = mybir.AxisListType

EPS = 1e-6
RWKV_DECAY_SCALE = 0.606531
RWKV_GN_EPS = 64e-5
HGRN_MAX_INPUT = 1.0 - 1e-4
N_META = 16
TS = 4


class Cfg:
    def __init__(s, D=2048, L=4, NP=2064, NS=16, TILE=256, C=16):
        s.D, s.L, s.NP, s.NS, s.TILE, s.C = D, L, NP, NS, TILE, C
        s.CL = 32
        s.CCS = (32, 16, TS)
        s.GW = D // 4
        s.NDC = D // 128
        s.RH = s.GW // 64
        s.NJ = s.GW // 128
        s.HH = s.GW // 128
        s.GH = s.GW // 128
        s.GQT = s.GW // 256
        s.S5C = (s.GW // 16) * 64 // 128
        s.DFF = ((8 * D // 3 + 127) // 128) * 128
        s.NFT = s.DFF // 128
        NJ = s.NJ
        s.ct_r, s.ct_k, s.ct_v, s.ct_wa, s.ct_gl = 0, NJ, 2 * NJ, 3 * NJ, 3 * NJ + 1
        s.NR = 3 * NJ + 2
        s.ct_s5 = s.NR
        s.ct_hq = s.ct_s5 + NJ
        s.ct_hf, s.ct_hi, s.ct_hg = s.ct_hq + NJ, s.ct_hq + 2 * NJ, s.ct_hq + 3 * NJ
        s.ct_gq = s.ct_hq + 4 * NJ
        s.ct_gk = s.ct_gq + s.GQT
        s.ct_gv = s.ct_gk + s.GQT
        s.ct_glo = s.ct_gv + NJ
        s.ct_gg = s.ct_glo + 1
        s.NCT = s.ct_gg + NJ
        s.mix_groups = [(0, s.NR), (s.ct_s5, NJ), (s.ct_hq, 4 * NJ), (s.ct_gq, s.NCT - s.ct_gq)]
        s.tiles = []
        p = 0
        while s.NP - p > C:
            n = min(TILE, s.NP - C - p)
            s.tiles.append((p, n, 0))
            p += n
        s.tiles.append((p, s.NP - p, NS))
        s.NTOK = s.NP + NS * TS
        s.TMAX = max(max(t[1] + t[2] * TS for t in s.tiles), 1)
        v = {}
        o = 0
        for name, n in [("norm_mix", s.NDC), ("norm_ffn", s.NDC), ("mu", s.NR), ("w0", NJ), ("a0", NJ),
                        ("k_k", NJ), ("k_a", NJ), ("r_k", NJ), ("s5_D", NJ), ("s5_bglu", NJ),
                        ("gk_b", s.GQT), ("conv_w0", s.NFT), ("conv_w1", s.NFT), ("conv_w2", s.NFT),
                        ("conv_b", s.NFT), ("A_re", s.S5C), ("A_im", s.S5C), ("logdt", s.S5C),
                        ("hlb", NJ)]:
            v[name] = (o, n)
            o += n
        s.vec = v
        s.NVEC = o


def build_consts(cfg):
    c = {}
    c["ident"] = np.eye(128, dtype=np.float32)
    c["ones"] = np.ones((128, 128), np.float32)
    bo = np.zeros((128, 128), np.float32)
    bo[:64, :64] = 1
    bo[64:, 64:] = 1
    c["blockones"] = bo
    f64 = np.zeros((128, 64), np.float32)
    f64[np.arange(128), np.arange(128) % 64] = 1
    c["f64"] = f64
    for C in cfg.CCS:
        r = np.arange(128) % C
        c[f"mstrT{C}"] = (r[:, None] < r[None, :]).astype(np.float32)
        c[f"minclT{C}"] = (r[:, None] <= r[None, :]).astype(np.float32)
        c[f"mstr{C}"] = (r[:, None] > r[None, :]).astype(np.float32)
    return c


def head_mask(NJ, HPT, NH):
    DK = 128 // HPT
    m = np.zeros((128, NJ, NH), np.float32)
    for j in range(NJ):
        for p in range(128):
            h = j * HPT + p // DK
            if h < NH:
                m[p, j, h] = 1
    return m


def col_mask(C, NJ, HPT, NH):
    m = np.zeros((128, NJ, HPT), np.float32)
    for row in range(min(128, NH * C)):
        h = row // C
        m[row, h // HPT, h % HPT] = 1
    return m


class Buf:
    def __init__(s, name, ap, pages, space):
        s.name, s.ap, s.pages, s.space = name, ap, pages, space

    def __getitem__(s, k):
        return V(s.ap[k], s)

    def v(s):
        return V(s.ap, s)


class V:
    def __init__(s, ap, buf):
        s.ap, s.buf = ap, buf

    def __getitem__(s, k):
        return V(s.ap[k], s.buf)

    def re(s, pat, **kw):
        return V(s.ap.rearrange(pat, **kw), s.buf)

    def bc(s, shape):
        return V(s.ap.to_broadcast(list(shape)), s.buf)

    def us(s, axis):
        return V(s.ap.unsqueeze(axis), s.buf)

    @property
    def shape(s):
        return s.ap.shape


PAGE = 256
ENG_NAMES = ["pe", "act", "dve", "pool", "sp"]
N_DMA_SEM = 20


class Op:
    __slots__ = ("eng", "fn", "deps", "is_dma", "signal", "idx", "dsem", "dval", "dprev", "is_out", "count")


class Prog:
    def __init__(s, nc, arena_words):
        s.nc = nc
        s.arena = nc.alloc_sbuf_tensor("arena", [128, arena_words], F32)
        s.A = s.arena[:, :]
        s.arena_words = arena_words
        s.off = 0
        s.ops = {e: [] for e in ENG_NAMES}
        s.last_w = {}
        s.readers = {}
        s.dma_count = {e: 0 for e in ENG_NAMES}
        s.psum = []
        for b in range(8):
            t = nc.alloc_psum_tensor(f"psb{b}", [128, 512], F32)
            s.psum.append(Buf(f"ps{b}", t[:, :], [("ps", b)], "psum"))
        s.ps_rr = 0
        s.marks = []

    def alloc(s, name, free_shape, dtype=F32, at=None):
        n = int(np.prod(free_shape))
        words = n if dtype == F32 else (n + 1) // 2
        if at is None:
            off = s.off
            s.off += words
        else:
            off = at
        assert off + words <= s.arena_words, f"arena overflow at {name}: {off + words} > {s.arena_words}"
        ap = s.A[:, off:off + words]
        if dtype != F32:
            ap = ap.bitcast(dtype)
        if len(free_shape) > 1:
            names = " ".join(f"a{i}" for i in range(len(free_shape)))
            kw = {f"a{i}": int(free_shape[i]) for i in range(1, len(free_shape))}
            ap = ap.rearrange(f"p ({names}) -> p {names}", **kw)
        pages = list(range((off * 4) // PAGE, ((off + words) * 4 - 1) // PAGE + 1))
        return Buf(name, ap, pages, "sbuf")

    def ps(s):
        b = s.psum[s.ps_rr % 8]
        s.ps_rr += 1
        return b

    def _rec(s, eng, fn, reads, writes, is_dma=False, is_out=False):
        op = Op()
        op.eng, op.fn, op.is_dma, op.signal, op.is_out = eng, fn, is_dma, False, is_out
        op.idx = len(s.ops[eng])
        deps = set()
        rp = []
        wp = []
        for b in reads:
            if b is not None:
                rp.extend(b.pages)
        for b in writes:
            if b is not None:
                wp.extend(b.pages)
        for p in rp:
            w = s.last_w.get(p)
            if w is not None:
                deps.add(w)
        for p in wp:
            w = s.last_w.get(p)
            if w is not None:
                deps.add(w)
            for r in s.readers.get(p, ()):
                deps.add(r)
        deps.discard(op)
        op.deps = deps
        for p in wp:
            s.last_w[p] = op
            s.readers[p] = []
        for p in rp:
            lst = s.readers.setdefault(p, [])
            if not is_dma:
                lst[:] = [r for r in lst if r.eng != eng or r.is_dma]
            lst.append(op)
        if is_dma:
            k = s.dma_count[eng]
            s.dma_count[eng] += 1
            op.dsem = k % N_DMA_SEM
            op.dval = 16 * (k // N_DMA_SEM + 1)
            op.dprev = 16 * (k // N_DMA_SEM)
        s.ops[eng].append(op)
        return op

    @staticmethod
    def _b(x):
        return x.buf if isinstance(x, V) else None

    @staticmethod
    def _a(x):
        return x.ap if isinstance(x, V) else x

    def mm(s, out, lhsT, rhs, start=True, stop=True):
        o, l, r = out.ap, lhsT.ap, rhs.ap
        s._rec("pe", lambda e: e.matmul(o, l, r, start=start, stop=stop), [lhsT.buf, rhs.buf], [out.buf])

    def transpose(s, out, in_, ident):
        o, i, d = out.ap, in_.ap, ident.ap
        s._rec("pe", lambda e: e.transpose(o, i, d), [in_.buf, ident.buf], [out.buf])

    def act(s, out, in_, func, bias=None, scale=None):
        kw = {}
        rd = [in_.buf]
        if bias is not None:
            kw["bias"] = s._a(bias)
            rd.append(s._b(bias))
        if scale is not None:
            kw["scale"] = s._a(scale)
            rd.append(s._b(scale))
        o, i = out.ap, in_.ap
        s._rec("act", lambda e: e.activation(out=o, in_=i, func=func, **kw), rd, [out.buf])

    def tt(s, out, in0, in1, op, eng="dve"):
        o, a, b = out.ap, in0.ap, in1.ap
        s._rec(eng, lambda e: e.tensor_tensor(out=o, in0=a, in1=b, op=op), [in0.buf, in1.buf], [out.buf])

    def ts(s, out, in0, s1, s2=None, op0=ALU.mult, op1=None, eng="dve"):
        o, a = out.ap, in0.ap
        a1, a2 = s._a(s1), s._a(s2)
        rd = [in0.buf, s._b(s1), s._b(s2)]
        if op1 is None:
            s._rec(eng, lambda e: e.tensor_scalar(out=o, in0=a, scalar1=a1, scalar2=None, op0=op0), rd, [out.buf])
        else:
            s._rec(eng, lambda e: e.tensor_scalar(out=o, in0=a, scalar1=a1, scalar2=a2, op0=op0, op1=op1), rd,
                   [out.buf])

    def stt(s, out, in0, scalar, in1, op0, op1, eng="dve"):
        o, a, b, sc = out.ap, in0.ap, in1.ap, s._a(scalar)
        s._rec(eng, lambda e: e.scalar_tensor_tensor(out=o, in0=a, scalar=sc, in1=b, op0=op0, op1=op1),
               [in0.buf, in1.buf, s._b(scalar)], [out.buf])

    def copy(s, out, in_, eng="dve"):
        o, i = out.ap, in_.ap
        if eng == "act":
            s._rec("act", lambda e: e.activation(out=o, in_=i, func=AF.Copy), [in_.buf], [out.buf])
        else:
            s._rec(eng, lambda e: e.tensor_copy(out=o, in_=i), [in_.buf], [out.buf])

    def memset(s, out, val, eng="dve"):
        o = out.ap
        s._rec(eng, lambda e: e.memset(o, val), [], [out.buf])

    def rsum(s, out, in_, eng="dve"):
        o, i = out.ap, in_.ap
        s._rec(eng, lambda e: e.reduce_sum(out=o, in_=i, axis=AX.X), [in_.buf], [out.buf])

    def recip(s, out, in_):
        o, i = out.ap, in_.ap
        s._rec("dve", lambda e: e.reciprocal(out=o, in_=i), [in_.buf], [out.buf])

    def scan(s, out, d0, d1, init, op0=ALU.mult, op1=ALU.add):
        o, a, b, ini = out.ap, d0.ap, d1.ap, s._a(init)
        s._rec("dve", lambda e: e.tensor_tensor_scan(out=o, data0=a, data1=b, initial=ini, op0=op0, op1=op1),
               [d0.buf, d1.buf, s._b(init)], [out.buf])

    def dma(s, out, in_, q="sp", is_out=False):
        o, i = s._a(out), s._a(in_)
        s._rec(q, lambda e: e.dma_start(out=o, in_=i), [s._b(in_)], [s._b(out)], is_dma=True, is_out=is_out)

    def emit(s):
        nc = s.nc
        for e in ENG_NAMES:
            for op in s.ops[e]:
                for d in op.deps:
                    if not d.is_dma:
                        if d.eng == "pe" and op.eng == "pe" and not op.is_dma:
                            continue
                        d.signal = True
        for e in ENG_NAMES:
            cnt = 0
            for op in s.ops[e]:
                if op.signal and not op.is_dma:
                    cnt += 1
                op.count = cnt
        with ExitStack() as st:
            esem = {e: st.enter_context(nc.semaphore(f"sem_{e}")) for e in ENG_NAMES}
            dsem = {e: [st.enter_context(nc.semaphore(f"dsem_{e}_{i}")) for i in range(N_DMA_SEM)]
                    for e in ("sp", "pool", "act")}
            block = st.enter_context(nc.Block())

            def run(ename, eng):
                waited = {}
                mine = s.ops[ename]

                def wait(key, sem, val):
                    if waited.get(key, 0) >= val:
                        return
                    eng.wait_ge(sem, val)
                    waited[key] = val

                for op in mine:
                    for d in op.deps:
                        if d.is_dma:
                            wait(("d", d.eng, d.dsem), dsem[d.eng][d.dsem], d.dval)
                        else:
                            if d.eng == "pe" and ename == "pe" and not op.is_dma:
                                continue
                            wait(("e", d.eng), esem[d.eng], d.count)
                    if op.is_dma:
                        if op.dprev > 0:
                            wait(("d", ename, op.dsem), dsem[ename][op.dsem], op.dprev)
                        ins = op.fn(eng)
                        ins.then_inc(dsem[ename][op.dsem], 16)
                    else:
                        ins = op.fn(eng)
                        if op.signal:
                            ins.then_inc(esem[ename], 1)
                if ename in ("sp", "pool", "act"):
                    n = s.dma_count[ename]
                    for i in range(min(n, N_DMA_SEM)):
                        last = 16 * ((n - 1 - i) // N_DMA_SEM + 1)
                        wait(("d", ename, i), dsem[ename][i], last)

            @block.tensor
            def _(e):
                run("pe", e)

            @block.scalar
            def _(e):
                run("act", e)

            @block.vector
            def _(e):
                run("dve", e)

            @block.gpsimd
            def _(e):
                run("pool", e)

            @block.sync
            def _(e):
                run("sp", e)


def fm(vec):
    return np.ascontiguousarray(vec.reshape(-1, 128).T)


def const_layout(cfg):
    items = [("ident", 128), ("ones", 128), ("blockones", 128), ("f64", 64)]
    for C in cfg.CCS:
        items += [(f"mstrT{C}", 128), (f"minclT{C}", 128), (f"mstr{C}", 128)]
    items += [("hm_r", cfg.NJ * cfg.RH), ("hm_g", cfg.GQT * cfg.GH), ("hm_h", cfg.NJ * cfg.HH)]
    for C in cfg.CCS:
        items += [(f"cm_r{C}", cfg.NJ * 2), (f"cm_g{C}", cfg.GQT * 2), (f"cm_h{C}", cfg.NJ)]
    off = {}
    o = 0
    for k, n in items:
        off[k] = (o, n)
        o += n
    return off, o


def pack_consts(cfg):
    c = build_consts(cfg)
    c["hm_r"] = head_mask(cfg.NJ, 2, cfg.RH).reshape(128, -1)
    c["hm_g"] = head_mask(cfg.GQT, 2, cfg.GH).reshape(128, -1)
    c["hm_h"] = head_mask(cfg.NJ, 1, cfg.HH).reshape(128, -1)
    for C in cfg.CCS:
        c[f"cm_r{C}"] = col_mask(C, cfg.NJ, 2, cfg.RH).reshape(128, -1)
        c[f"cm_g{C}"] = col_mask(C, cfg.GQT, 2, cfg.GH).reshape(128, -1)
        c[f"cm_h{C}"] = col_mask(C, cfg.NJ, 1, cfg.HH).reshape(128, -1)
    off, n = const_layout(cfg)
    out = np.zeros((128, n), np.float32)
    for k, (o, w) in off.items():
        out[:, o:o + w] = c[k]
    return out


def pack_shared(cfg, inp):
    L, D, GW, NJ = cfg.L, cfg.D, cfg.GW, cfg.NJ
    sh = {}
    w = inp["w_in"]
    RC = 3 * GW + 256
    wt = np.zeros((L, D, cfg.NCT * 128), np.float32)
    wt[:, :, 0:RC] = w[:, :, 0:RC]
    o = RC
    wt[:, :, cfg.ct_s5 * 128: cfg.ct_s5 * 128 + GW] = w[:, :, o:o + GW]
    o += GW
    wt[:, :, cfg.ct_hq * 128: cfg.ct_hq * 128 + 4 * GW] = w[:, :, o:o + 4 * GW]
    o += 4 * GW
    GQK = GW // 2
    wt[:, :, cfg.ct_gq * 128: cfg.ct_gq * 128 + GQK] = w[:, :, o:o + GQK]
    o += GQK
    wt[:, :, cfg.ct_gk * 128: cfg.ct_gk * 128 + GQK] = w[:, :, o:o + GQK]
    o += GQK
    wt[:, :, cfg.ct_gv * 128: cfg.ct_gv * 128 + GW] = w[:, :, o:o + GW]
    o += GW
    wt[:, :, cfg.ct_glo * 128: cfg.ct_glo * 128 + 16] = w[:, :, o:o + 16]
    o += 16
    wt[:, :, cfg.ct_gg * 128: cfg.ct_gg * 128 + GW] = w[:, :, o:o + GW]
    o += GW
    assert o == w.shape[2]
    sh["w_in"] = wt
    sh["w_out"] = np.ascontiguousarray(inp["w_out"])
    sh["w_up"] = np.ascontiguousarray(inp["ffn_w_up"])
    sh["w_down"] = np.ascontiguousarray(inp["ffn_w_down"])
    vecs = np.zeros((L, 128, cfg.NVEC), np.float32)

    def put(name, arr):
        o, n = cfg.vec[name]
        for l in range(L):
            vecs[l, :, o:o + n] = fm(arr[l])
    put("norm_mix", inp["norm_mix"])
    put("norm_ffn", inp["norm_ffn"])
    mu = np.zeros((L, cfg.NR * 128), np.float32)
    mu[:, :RC] = inp["rwkv_mu"]
    put("mu", mu)
    put("w0", inp["rwkv_w0"])
    put("a0", inp["rwkv_a0"])
    put("k_k", inp["rwkv_k_k"])
    put("k_a", inp["rwkv_k_a"])
    put("r_k", inp["rwkv_r_k"].reshape(L, -1))
    put("s5_D", inp["s5_D"])
    put("s5_bglu", inp["s5_b_glu"])
    put("gk_b", inp["gla_gk_b"])
    for j in range(3):
        put(f"conv_w{j}", inp["ffn_conv_w"][:, j])
    put("conv_b", inp["ffn_conv_b"])
    put("A_re", inp["s5_A_re"].reshape(L, -1))
    put("A_im", inp["s5_A_im"].reshape(L, -1))
    put("logdt", np.repeat(inp["s5_log_dt"], 64, axis=1))
    put("hlb", inp["hgrn_lower_bounds"])
    sh["vecs"] = vecs
    sh["normf"] = fm(inp["norm_final"])
    sh["rw_wa"] = np.ascontiguousarray(np.concatenate([inp["rwkv_w_up"], inp["rwkv_a_up"]], axis=1))
    sh["rw_gup"] = np.ascontiguousarray(inp["rwkv_g_up"])
    for C in (cfg.C, TS):
        sh[f"rw_ln{C}"] = np.ascontiguousarray(np.repeat(inp["rwkv_ln"].reshape(L, cfg.RH, 1, 64), C, axis=2)
                                               .reshape(L, cfg.RH * C, 64))
    for C in cfg.CCS:
        sh[f"hg_n{C}"] = np.ascontiguousarray(np.repeat(inp["hgrn_norm"].reshape(L, cfg.HH, 1, 128), C, axis=2)
                                              .reshape(L, cfg.HH * C, 128))
        sh[f"gl_n{C}"] = np.ascontiguousarray(np.repeat(inp["gla_norm"].reshape(L, cfg.GH, 1, 128), C, axis=2)
                                              .reshape(L, cfg.GH * C, 128))
    sh["gk_up"] = np.ascontiguousarray(inp["gla_gk_up"])
    sh["s5_Bt_re"] = np.ascontiguousarray(inp["s5_B_re"].transpose(0, 1, 3, 2))
    sh["s5_Bt_im"] = np.ascontiguousarray(inp["s5_B_im"].transpose(0, 1, 3, 2))
    sh["s5_Ct_re"] = np.ascontiguousarray(inp["s5_C_re"].transpose(0, 1, 3, 2))
    sh["s5_Ct_im"] = np.ascontiguousarray(inp["s5_C_im"].transpose(0, 1, 3, 2))
    sh["s5_wglu"] = np.ascontiguousarray(inp["s5_w_glu"])
    sh["consts"] = pack_consts(cfg)
    ntile = len(cfg.tiles)
    sm = np.ones((ntile, 128, cfg.TMAX), np.float32)
    pos = np.zeros((ntile, 128, cfg.TMAX), np.float32)
    cmk = np.ones((ntile, 128, cfg.TMAX), np.float32)
    cmk2 = np.ones((ntile, 128, cfg.TMAX), np.float32)
    for i, (p0, pn, ns) in enumerate(cfg.tiles):
        sm[i, :, 0] = 0
        pos[i, :, :pn] = np.arange(1, pn + 1)
        cmk[i, :, 0:pn:cfg.C] = 0
        cmk2[i, :, 0:pn:cfg.CL] = 0
        for q in range(ns):
            sm[i, :, pn + q * TS] = 0
            pos[i, :, pn + q * TS: pn + (q + 1) * TS] = np.arange(1, TS + 1)
            cmk[i, :, pn + q * TS] = 0
            cmk2[i, :, pn + q * TS] = 0
    sh["tmask"] = np.ascontiguousarray(np.stack([sm, pos, cmk, cmk2], axis=1))
    return sh


def pack_core(cfg, inp, b, s0):
    L, NS = cfg.L, cfg.NS
    d = {}
    xs = inp["x_sample"][s0:s0 + NS].reshape(NS * TS, cfg.D)
    xall = np.concatenate([inp["meta_tokens"], inp["x_prompt"][b], xs], axis=0)
    d["xin"] = np.ascontiguousarray(xall.T)
    sl = slice(s0, s0 + NS)
    d["st_rwkv"] = np.ascontiguousarray(inp["state_rwkv"][:, sl].transpose(0, 1, 2, 4, 3).reshape(L, NS, cfg.GW, 64))
    sft = np.zeros((L, NS, cfg.NR * 128), np.float32)
    sft[:, :, :3 * cfg.GW + 256] = inp["state_rwkv_shift"][:, sl]
    d["st_shift"] = np.ascontiguousarray(sft.reshape(L, NS, cfg.NR, 128).transpose(0, 3, 2, 1))
    for nm, key in (("st_s5re", "state_s5_re"), ("st_s5im", "state_s5_im")):
        a = inp[key][:, sl].reshape(L, NS, cfg.S5C, 128)
        d[nm] = np.ascontiguousarray(a.transpose(0, 3, 2, 1))
    d["st_hgrn"] = np.ascontiguousarray(inp["state_hgrn"][:, sl].reshape(L, NS, cfg.GW, 128))
    d["st_gla"] = np.ascontiguousarray(inp["state_gla"][:, sl].reshape(L, NS, cfg.GW // 2, 128))
    a = inp["state_ffn_conv"][:, sl].reshape(L, NS, 2, cfg.NFT, 128)
    d["st_conv"] = np.ascontiguousarray(a.transpose(0, 4, 3, 1, 2))
    return d


OUT_SPECS = lambda cfg: {
    "yT": [cfg.D, cfg.NTOK],
    "o_rwkv_p": [cfg.L, cfg.GW, 64], "o_rwkv_s": [cfg.L, cfg.NS, cfg.GW, 64],
    "o_shift_p": [cfg.L, 128, cfg.NR], "o_shift_s": [cfg.L, 128, cfg.NR, cfg.NS],
    "o_s5re_p": [cfg.L, 128, cfg.S5C], "o_s5re_s": [cfg.L, 128, cfg.S5C, cfg.NS],
    "o_s5im_p": [cfg.L, 128, cfg.S5C], "o_s5im_s": [cfg.L, 128, cfg.S5C, cfg.NS],
    "o_hgrn_p": [cfg.L, cfg.GW, 128], "o_hgrn_s": [cfg.L, cfg.NS, cfg.GW, 128],
    "o_gla_p": [cfg.L, cfg.GW // 2, 128], "o_gla_s": [cfg.L, cfg.NS, cfg.GW // 2, 128],
    "o_conv_p": [cfg.L, 128, cfg.NFT, 2], "o_conv_s": [cfg.L, 128, cfg.NFT, cfg.NS, 2],
}


def unpack_core(cfg, r):
    L, NS, GW = cfg.L, cfg.NS, cfg.GW
    o = {}
    yT = r["yT"]
    o["y_p"] = np.ascontiguousarray(yT[:, N_META:cfg.NP].T)
    o["y_s"] = np.ascontiguousarray(yT[:, cfg.NP:].T).reshape(NS, TS, cfg.D)
    o["rwkv_p"] = r["o_rwkv_p"].reshape(L, cfg.RH, 64, 64).transpose(0, 1, 3, 2)
    o["rwkv_s"] = r["o_rwkv_s"].reshape(L, NS, cfg.RH, 64, 64).transpose(0, 1, 2, 4, 3)
    RC = 3 * GW + 256
    o["shift_p"] = r["o_shift_p"].transpose(0, 2, 1).reshape(L, -1)[:, :RC]
    o["shift_s"] = r["o_shift_s"].transpose(0, 3, 2, 1).reshape(L, NS, -1)[:, :, :RC]
    for k in ("s5re", "s5im"):
        o[k + "_p"] = r[f"o_{k}_p"].transpose(0, 2, 1).reshape(L, GW // 16, 64)
        o[k + "_s"] = r[f"o_{k}_s"].transpose(0, 3, 2, 1).reshape(L, NS, GW // 16, 64)
    o["hgrn_p"] = r["o_hgrn_p"].reshape(L, cfg.HH, 128, 128)
    o["hgrn_s"] = r["o_hgrn_s"].reshape(L, NS, cfg.HH, 128, 128)
    o["gla_p"] = r["o_gla_p"].reshape(L, cfg.GH, 64, 128)
    o["gla_s"] = r["o_gla_s"].reshape(L, NS, cfg.GH, 64, 128)
    o["conv_p"] = r["o_conv_p"].transpose(0, 3, 2, 1).reshape(L, 2, cfg.DFF)
    o["conv_s"] = r["o_conv_s"].transpose(0, 3, 4, 2, 1).reshape(L, NS, 2, cfg.DFF)
    return o


def weight_blocks(cfg):
    blocks = []
    for (c0, n) in cfg.mix_groups:
        for g0 in range(0, n, 4):
            blocks.append(("win", c0 + g0, min(4, n - g0)))
    for og in range(cfg.D // 256):
        blocks.append(("wout", og))
    for j0 in range(0, cfg.NFT, 2):
        blocks.append(("wup", j0, min(2, cfg.NFT - j0)))
    for og in range(cfg.D // 512):
        for k0 in range(0, cfg.NFT, 16):
            blocks.append(("wdown", og, k0, min(16, cfg.NFT - k0)))
    return blocks


NSLOT = 3
SLOT_ELEMS = 8192


class WStream:
    def __init__(s, P, cfg, di, order):
        s.P, s.cfg, s.di = P, cfg, di
        s.slots = [P.alloc(f"wslot{i}", [SLOT_ELEMS], BF16) for i in range(NSLOT)]
        per = weight_blocks(cfg)
        s.specs = [(l, b) for (_, l) in order for b in per]
        s.issued = 0
        s.next = 0

    def views(s, slot, spec):
        cfg = s.cfg
        l, b = spec
        k = b[0]
        sv = slot.v()
        if k == "win":
            n = b[2] * 128
            return [sv[:, 0:cfg.NDC * n].re("p (c n) -> p c n", n=n)]
        if k == "wout":
            a = sv[:, 0:cfg.RH * 256].re("p (h n) -> p h n", n=256)
            bb = sv[:, cfg.RH * 256:(cfg.RH + 3 * cfg.NJ) * 256].re("p (c n) -> p c n", n=256)
            return [a, bb]
        if k == "wup":
            n = b[2] * 128
            return [sv[:, 0:cfg.NDC * 2 * n].re("p (c g n) -> p c g n", g=2, n=n)]
        if k == "wdown":
            return [sv[:, 0:b[3] * 512].re("p (c n) -> p c n", n=512)]

    def _load(s, i):
        cfg, di, P = s.cfg, s.di, s.P
        slot = s.slots[i % NSLOT]
        l, b = s.specs[i]
        vs = s.views(slot, s.specs[i])
        k = b[0]
        if k == "win":
            c0, n = b[1] * 128, b[2] * 128
            P.dma(vs[0], di["w_in"][l, :, c0:c0 + n].rearrange("(c p) n -> p c n", p=128), q="pool")
        elif k == "wout":
            c0 = b[1] * 256
            P.dma(vs[0][0:64], di["w_out"][l, 0:cfg.GW, c0:c0 + 256].rearrange("(h k) n -> k h n", k=64), q="pool")
            P.dma(vs[1], di["w_out"][l, cfg.GW:4 * cfg.GW, c0:c0 + 256].rearrange("(c p) n -> p c n", p=128),
                  q="pool")
        elif k == "wup":
            c0, n = b[1] * 128, b[2] * 128
            for g in range(2):
                P.dma(vs[0][:, :, g, :],
                      di["w_up"][l, :, g * cfg.DFF + c0:g * cfg.DFF + c0 + n].rearrange("(c p) n -> p c n", p=128),
                      q="pool")
        elif k == "wdown":
            og, k0, nk = b[1], b[2], b[3]
            P.dma(vs[0], di["w_down"][l, k0 * 128:(k0 + nk) * 128, og * 512:(og + 1) * 512]
                  .rearrange("(c p) n -> p c n", p=128), q="pool")

    def get(s, kind):
        i = s.next
        s.next += 1
        while s.issued < min(len(s.specs), i + NSLOT):
            s._load(s.issued)
            s.issued += 1
        l, b = s.specs[i]
        assert b[0] == kind, (b, kind)
        return s.views(s.slots[i % NSLOT], s.specs[i]), b


def la_chunk(P, G, S, Cc, qv, kv, vv, gamma, H, av=None, bv=None, prodv=None):
    NH, NJ, NJV, DK, DV, HPT = G["NH"], G["NJ"], G["NJV"], G["DK"], G["DV"], G["HPT"]
    R = NH * Cc
    delta = av is not None
    XX, VX, HS = S["XX"], S["VX"], S["HS"]
    ident = G["ident"]
    mstrT, minclT, mstr = G[f"mstrT{Cc}"], G[f"minclT{Cc}"], G[f"mstr{Cc}"]
    hm, hmv = G["hm"], G["hmv"]

    def expand(x, src):
        out = XX[:, :, x, 0:R].re("p j (h t) -> p j h t", t=Cc)
        P.tt(out, src.us(2).bc([128, NJ, NH, Cc]), hm.us(3).bc([128, NJ, NH, Cc]), ALU.mult)

    if delta:
        expand(0, bv)
        expand(2, av)
        if prodv is not None:
            expand(4, prodv)
    expand(1, qv)
    expand(3, kv)
    outv = VX[:, :, 0:R].re("p j (h t) -> p j h t", t=Cc)
    P.tt(outv, vv.us(2).bc([128, NJV, NH, Cc]), hmv.us(3).bc([128, NJV, NH, Cc]), ALU.mult)
    psf = P.ps()
    for j in range(NJ):
        P.mm(psf[0:R, 0:DK], XX[:, j, 3, 0:R], G["fold_k"], start=(j == 0), stop=(j == NJ - 1))
    if delta:
        for j in range(NJ):
            P.mm(psf[0:R, DK:2 * DK], XX[:, j, 2, 0:R], G["fold_k"], start=(j == 0), stop=(j == NJ - 1))
    for j in range(NJV):
        P.mm(psf[0:R, 2 * DK:2 * DK + DV], VX[:, j, 0:R], G["fold_v"], start=(j == 0), stop=(j == NJV - 1))
    if delta:
        P.copy(HS[0:R, 0:2 * DK + DV], psf[0:R, 0:2 * DK + DV], eng="act")
    else:
        P.copy(HS[0:R, 0:DK], psf[0:R, 0:DK], eng="act")
        P.copy(HS[0:R, 2 * DK:2 * DK + DV], psf[0:R, 2 * DK:2 * DK + DV], eng="act")
    K_hs, A_hs, V_hs = HS[0:R, 0:DK], HS[0:R, DK:2 * DK], HS[0:R, 2 * DK:2 * DK + DV]
    psK = P.ps()
    RKT, NT, MT, RAT, Mm = S["RKT"], S["NT"], S["MT"], S["RAT"], S["Mm"]
    if delta:
        pk = psK[0:R, 0:256].re("p (x r) -> p x r", x=2)[:, :, 0:R]
        if R == 128:
            for j in range(NJ):
                P.mm(psK[:, 0:256], XX[:, j, 3, :], XX[:, j, 0:2, :].re("p x r -> p (x r)"), start=(j == 0),
                     stop=(j == NJ - 1))
        else:
            for xi in range(2):
                for j in range(NJ):
                    P.mm(pk[:, xi, :], XX[:, j, 3, 0:R], XX[:, j, xi, 0:R], start=(j == 0), stop=(j == NJ - 1))
        P.tt(NT[0:R, 0:R], pk[:, 0, :], mstrT[0:R, 0:R], ALU.mult)
        P.tt(RKT[0:R, 0:R], pk[:, 1, :], minclT[0:R, 0:R], ALU.mult)
        psA = P.ps()
        pa = psA[0:R, 0:256].re("p (x r) -> p x r", x=2)[:, :, 0:R]
        if R == 128:
            for j in range(NJ):
                P.mm(psA[:, 0:256], XX[:, j, 2, :], XX[:, j, 0:2, :].re("p x r -> p (x r)"), start=(j == 0),
                     stop=(j == NJ - 1))
        else:
            for xi in range(2):
                for j in range(NJ):
                    P.mm(pa[:, xi, :], XX[:, j, 2, 0:R], XX[:, j, xi, 0:R], start=(j == 0), stop=(j == NJ - 1))
        P.tt(MT[0:R, 0:R], pa[:, 0, :], mstrT[0:R, 0:R], ALU.mult)
        P.tt(RAT[0:R, 0:R], pa[:, 1, :], minclT[0:R, 0:R], ALU.mult)
        psM = P.ps()
        for j in range(NJ):
            P.mm(psM[0:R, 0:R], XX[:, j, 0, 0:R], XX[:, j, 2, 0:R], start=(j == 0), stop=(j == NJ - 1))
        P.tt(Mm[0:R, 0:R], psM[0:R, 0:R], mstr[0:R, 0:R], ALU.mult)
        levels = int(round(math.log2(Cc))) - 1
        cur, curT = Mm, MT
        pows = []
        for lev in range(levels):
            p1 = P.ps()
            P.mm(p1[0:R, 0:R], curT[0:R, 0:R], cur[0:R, 0:R])
            P.copy(S["Mp"][lev][0:R, 0:R], p1[0:R, 0:R], eng="act")
            if lev < levels - 1:
                p2 = P.ps()
                P.mm(p2[0:R, 0:R], cur[0:R, 0:R], curT[0:R, 0:R])
                P.copy(S["MpT"][lev][0:R, 0:R], p2[0:R, 0:R], eng="act")
            pows.append(S["Mp"][lev])
            cur, curT = S["Mp"][lev], S["MpT"][lev]
        Z = S["Z"][0]
        P.tt(Z[0:R, 0:R], ident[0:R, 0:R], MT[0:R, 0:R], ALU.subtract)
        for lev in range(levels):
            pz = P.ps()
            P.mm(pz[0:R, 0:R], ident[0:R, 0:R], Z[0:R, 0:R], start=True, stop=False)
            P.mm(pz[0:R, 0:R], pows[lev][0:R, 0:R], Z[0:R, 0:R], start=False, stop=True)
            Zn = S["Z"][(lev + 1) % 2]
            P.copy(Zn[0:R, 0:R], pz[0:R, 0:R], eng="act")
            Z = Zn
        TinvT = Z
        pn_ = P.ps()
        P.mm(pn_[0:R, 0:DV], NT[0:R, 0:R], V_hs)
        P.copy(S["NV"][0:R, 0:DV], pn_[0:R, 0:DV], eng="act")
        pw = P.ps()
        for j in range(NJ):
            P.mm(pw[0:R, 0:DV], XX[:, j, 0, 0:R], H[:, j, :], start=(j == 0), stop=(j == NJ - 1))
        P.tt(S["W"][0:R, 0:DV], pw[0:R, 0:DV], S["NV"][0:R, 0:DV], ALU.add)
        pu = P.ps()
        P.mm(pu[0:R, 0:DV], TinvT[0:R, 0:R], S["W"][0:R, 0:DV])
        P.act(S["nU"][0:R, 0:DV], pu[0:R, 0:DV], AF.Copy, scale=-1.0)
        nU = S["nU"][0:R, 0:DV]
    else:
        for j in range(NJ):
            P.mm(psK[0:R, 0:R], XX[:, j, 3, 0:R], XX[:, j, 1, 0:R], start=(j == 0), stop=(j == NJ - 1))
        P.tt(RKT[0:R, 0:R], psK[0:R, 0:R], minclT[0:R, 0:R], ALU.mult)
    psY = P.ps()
    for j in range(NJ):
        P.mm(psY[0:R, 0:DV], XX[:, j, 1, 0:R], H[:, j, :], start=(j == 0), stop=False)
    if delta:
        P.mm(psY[0:R, 0:DV], RAT[0:R, 0:R], nU, start=False, stop=False)
    P.mm(psY[0:R, 0:DV], RKT[0:R, 0:R], V_hs, start=False, stop=True)
    psS = None
    if prodv is not None:
        psS = P.ps()
        for j in range(NJ):
            P.mm(psS[0:R, 0:1], XX[:, j, 4, 0:R], G["ones"][:, 0:1], start=(j == 0), stop=(j == NJ - 1))
    cm = G[f"cm{Cc}"]
    K2, A2 = S["K2"], S["A2"]
    P.tt(K2[0:R], K_hs.us(1).us(1).bc([R, NJ, HPT, DK]), cm[0:R].us(3).bc([R, NJ, HPT, DK]), ALU.mult)
    if delta:
        P.tt(A2[0:R], A_hs.us(1).us(1).bc([R, NJ, HPT, DK]), cm[0:R].us(3).bc([R, NJ, HPT, DK]), ALU.mult)
    psH = P.ps()
    for j in range(NJ):
        o = psH[:, j * DV:(j + 1) * DV]
        if delta:
            P.mm(o, A2[0:R, j].re("p h k -> p (h k)"), nU, start=True, stop=False)
        P.mm(o, K2[0:R, j].re("p h k -> p (h k)"), V_hs, start=(not delta), stop=True)
    P.tt(S["tH"].v(), psH[:, 0:NJ * DV].re("p (j v) -> p j v", v=DV), H.v(), ALU.add)
    P.tt(H.v(), S["tH"].v(), gamma.us(2).bc([128, NJ, DV]), ALU.mult)
    return psY[0:R, 0:DV], V_hs, psS


MAGIC = 12582912.0
TWO_PI = 2.0 * math.pi


def build_program(cfg, debug=False):
    nc = bass.Bass("TRN2", target_bir_lowering=False)
    L, D, GW, NJ, NDC, NR, RH, HH, GH, GQT, S5C, NFT, NS, C = (cfg.L, cfg.D, cfg.GW, cfg.NJ, cfg.NDC, cfg.NR,
                                                             cfg.RH, cfg.HH, cfg.GH, cfg.GQT, cfg.S5C, cfg.NFT,
                                                             cfg.NS, cfg.C)
    TMAX = cfg.TMAX
    di = {}

    def din(name, shape):
        di[name] = nc.dram_tensor(name, [int(x) for x in shape], F32, kind="ExternalInput").ap()

    coff, NCONST = const_layout(cfg)
    ntile = len(cfg.tiles)
    din("xin", [D, cfg.NTOK])
    din("w_in", [L, D, cfg.NCT * 128])
    din("w_out", [L, D, D])
    din("w_up", [L, D, 2 * cfg.DFF])
    din("w_down", [L, cfg.DFF, D])
    din("vecs", [L, 128, cfg.NVEC])
    din("normf", [128, NDC])
    din("rw_wa", [L, 128, GW])
    din("rw_gup", [L, 128, GW])
    for Cc in (C, TS):
        din(f"rw_ln{Cc}", [L, RH * Cc, 64])
    for Cc in cfg.CCS:
        din(f"hg_n{Cc}", [L, HH * Cc, 128])
        din(f"gl_n{Cc}", [L, GH * Cc, 128])
    din("gk_up", [L, 16, GW // 2])
    G5 = GW // 16
    din("s5_Bt_re", [L, G5, 16, 64])
    din("s5_Bt_im", [L, G5, 16, 64])
    din("s5_Ct_re", [L, G5, 64, 16])
    din("s5_Ct_im", [L, G5, 64, 16])
    din("s5_wglu", [L, GW, GW])
    din("consts", [128, NCONST])
    din("tmask", [ntile, 4, 128, TMAX])
    din("st_rwkv", [L, NS, GW, 64])
    din("st_shift", [L, 128, NR, NS])
    din("st_s5re", [L, 128, S5C, NS])
    din("st_s5im", [L, 128, S5C, NS])
    din("st_hgrn", [L, NS, GW, 128])
    din("st_gla", [L, NS, GW // 2, 128])
    din("st_conv", [L, 128, NFT, NS, 2])
    do = {}
    for k, shp in OUT_SPECS(cfg).items():
        do[k] = nc.dram_tensor(k, [int(x) for x in shp], F32, kind="ExternalOutput").ap()

    words = nc.sbuf_bytes_remaining // 4 - 64
    P = Prog(nc, words)
    consts = P.alloc("consts", [NCONST])
    cv = lambda k: consts[:, coff[k][0]:coff[k][0] + coff[k][1]]
    ident, ones_f, blockones, f64 = cv("ident"), cv("ones"), cv("blockones"), cv("f64")
    ones_bf = P.alloc("ones_bf", [128], BF16)
    vecs = P.alloc("vecs", [L, cfg.NVEC])
    normf = P.alloc("normf", [NDC])
    lb = P.alloc("lb", [L, NJ])
    oml = P.alloc("oml", [L, NJ])
    tmask = P.alloc("tmask", [4, TMAX])
    x = P.alloc("x", [NDC, TMAX])
    h = P.alloc("h", [NDC, TMAX], BF16)
    Hr = [P.alloc(f"Hr{l}", [NJ, 64]) for l in range(L)]
    Hh = [P.alloc(f"Hh{l}", [NJ, 128]) for l in range(L)]
    Hg = [P.alloc(f"Hg{l}", [GQT, 128]) for l in range(L)]
    s5c = [[P.alloc(f"s5c{l}{i}", [S5C]) for i in range(2)] for l in range(L)]
    convc = [P.alloc(f"convc{l}", [NFT, 2]) for l in range(L)]
    shiftc = [P.alloc(f"shiftc{l}", [NR]) for l in range(L)]
    order = [(ti, l) for ti in range(ntile) for l in range(L)]
    WS = WStream(P, cfg, di, order)
    phase_base = P.off

    def vc(l, name, j=None):
        o, n = cfg.vec[name]
        if j is None:
            return vecs[:, l, o:o + n]
        return vecs[:, l, o + j:o + j + 1]

    def geom(NH, NJk, HPT, NJV, DV, hmk, hmv, cmk):
        g = {"NH": NH, "NJ": NJk, "NJV": NJV, "DK": 128 // HPT, "DV": DV, "HPT": HPT, "ident": ident,
             "ones": ones_f,
             "hm": cv(hmk).re("p (j h) -> p j h", h=NH), "hmv": cv(hmv).re("p (j h) -> p j h", h=NH),
             "fold_k": f64 if HPT == 2 else ident, "fold_v": f64 if DV == 64 else ident}
        for Cc in cfg.CCS:
            g[f"mstrT{Cc}"], g[f"minclT{Cc}"], g[f"mstr{Cc}"] = cv(f"mstrT{Cc}"), cv(f"minclT{Cc}"), cv(f"mstr{Cc}")
            g[f"cm{Cc}"] = cv(f"{cmk}{Cc}").re("p (j h) -> p j h", h=HPT)
        return g
    G_r = geom(RH, NJ, 2, NJ, 64, "hm_r", "hm_r", "cm_r")
    G_g = geom(GH, GQT, 2, NJ, 128, "hm_g", "hm_h", "cm_g")
    G_h = geom(HH, NJ, 1, NJ, 128, "hm_h", "hm_h", "cm_h")

    def la_scratch(G, delta):
        NJk, NJV, DK, DV, HPT = G["NJ"], G["NJV"], G["DK"], G["DV"], G["HPT"]
        S = {"XX": P.alloc("XX", [NJk, 5 if delta else 4, 128]), "VX": P.alloc("VX", [NJV, 128]),
             "HS": P.alloc("HS", [2 * DK + DV]), "RKT": P.alloc("RKT", [128]),
             "K2": P.alloc("K2", [NJk, HPT, DK]), "tH": P.alloc("tH", [NJk, DV]), "NT": None, "MT": None, "RAT": None, "Mm": None, "A2": None}
        if delta:
            for k in ("NT", "MT", "RAT", "Mm"):
                S[k] = P.alloc(k, [128])
            S["A2"] = P.alloc("A2", [NJk, HPT, DK])
            S["Mp"] = [P.alloc(f"Mp{i}", [128]) for i in range(3)]
            S["MpT"] = [P.alloc(f"MpT{i}", [128]) for i in range(3)]
            S["Z"] = [P.alloc(f"Z{i}", [128]) for i in range(2)]
            S["NV"], S["W"], S["nU"] = P.alloc("NV", [DV]), P.alloc("W", [DV]), P.alloc("nU", [DV])
        return S

    P.dma(consts.v(), di["consts"][:, :])
    P.dma(vecs.v(), di["vecs"].rearrange("l p n -> p l n"))
    P.dma(normf.v(), di["normf"][:, :])
    P.memset(ones_bf.v(), 1.0)
    for l in range(L):
        for b in (Hr[l], Hh[l], Hg[l], s5c[l][0], s5c[l][1], convc[l], shiftc[l]):
            P.memset(b.v(), 0.0)
    P.off = phase_base
    ee = P.alloc("lb_e", [L, NJ])
    se = P.alloc("lb_s", [NJ])
    for l in range(L):
        P.act(ee[:, l, :], vc(l, "hlb"), AF.Exp)
    P.copy(se.v(), ee[:, 0, :])
    for l in range(1, L):
        P.tt(se.v(), se.v(), ee[:, l, :], ALU.add)
    P.recip(se.v(), se.v())
    P.memset(lb[:, 0, :], 0.0)
    for l in range(1, L):
        P.tt(ee[:, l, :], ee[:, l, :], se.v(), ALU.mult)
        P.tt(lb[:, l, :], lb[:, l - 1, :], ee[:, l, :], ALU.add)
    P.ts(oml.v(), lb.v(), -1.0, 1.0, ALU.mult, ALU.add)

    def rmsnorm(T, gname, l, out_bf, gvec=None):
        sq = [P.alloc(f"nsq{i}", [TMAX], BF16) for i in range(2)]
        rstd = P.alloc("rstd", [TMAX])
        ps = P.ps()
        for c in range(NDC):
            s_ = sq[c % 2]
            P.act(s_[:, 0:T], x[:, c, 0:T], AF.Square)
            P.mm(ps[:, 0:T], ones_bf.v(), s_[:, 0:T], start=(c == 0), stop=(c == NDC - 1))
        P.act(rstd[:, 0:T], ps[:, 0:T], AF.Sqrt, bias=EPS, scale=1.0 / D)
        P.recip(rstd[:, 0:T], rstd[:, 0:T])
        for c in range(NDC):
            g = gvec[:, c:c + 1] if gvec is not None else vc(l, gname, c)
            P.stt(out_bf[:, c, 0:T], x[:, c, 0:T], g, rstd[:, 0:T], ALU.mult, ALU.mult)

    def project(T, ct0, nct, pt):
        done = 0
        while done < nct:
            (wv,), b = WS.get("win")
            assert b[1] == ct0 + done
            for i in range(b[2]):
                ps = P.ps()
                for c in range(NDC):
                    P.mm(ps[:, 0:T], wv[:, c, i * 128:(i + 1) * 128], h[:, c, 0:T], start=(c == 0),
                         stop=(c == NDC - 1))
                P.copy(pt[:, done + i, 0:T], ps[:, 0:T], eng="act")
            done += b[2]

    def chunks_of(pn, ns, CC=None):
        CC = CC or C
        ch = [("p", c0, min(CC, pn - c0), None) for c0 in range(0, pn, CC)]
        ch += [("s", pn + q * TS, TS, q) for q in range(ns)]
        return ch

    def range_reduce(out, in_, tmp):
        P.ts(tmp, in_, 1.0 / TWO_PI, MAGIC, ALU.mult, ALU.add)
        P.ts(tmp, tmp, -MAGIC, None, ALU.add)
        P.stt(out, tmp, -TWO_PI, in_, ALU.mult, ALU.add)

    cmk = lambda T: tmask[:, 2, 0:T]
    cmk2 = lambda T: tmask[:, 3, 0:T]
    smk = lambda T: tmask[:, 0, 0:T]
    posv = lambda T: tmask[:, 1, 0:T]

    def rwkv(ti, l, T, pn, ns, yTr):
        last = ti == ntile - 1
        P.off = mix_base
        pt = P.alloc("pt_r", [NR, TMAX])
        project(T, 0, NR, pt)
        wa = P.alloc("rw_wa", [GW])
        gup = P.alloc("rw_gup", [GW])
        ln = {C: P.alloc("rw_ln", [64])}
        P.dma(wa.v(), di["rw_wa"][l])
        P.dma(gup.v(), di["rw_gup"][l])
        P.dma(ln[C][0:RH * C], di[f"rw_ln{C}"][l])
        if ns:
            ln[TS] = P.alloc("rw_ln4", [64])
            P.dma(ln[TS][0:RH * TS], di[f"rw_ln{TS}"][l])
            stsh = P.alloc("stsh", [NR, NS])
            osh = P.alloc("osh", [NR, NS])
            P.dma(stsh.v(), di["st_shift"][l])
        d = P.alloc("rw_d", [TMAX])
        for ct in range(NR):
            p = pt[:, ct, :]
            if pn > 1:
                P.tt(d[:, 1:pn], p[:, 0:pn - 1], p[:, 1:pn], ALU.subtract)
            P.tt(d[:, 0:1], shiftc[l][:, ct:ct + 1], p[:, 0:1], ALU.subtract)
            if ns:
                pv = p[:, pn:T].re("p (s t) -> p s t", t=TS)
                dv = d[:, pn:T].re("p (s t) -> p s t", t=TS)
                P.tt(dv[:, :, 1:TS], pv[:, :, 0:TS - 1], pv[:, :, 1:TS], ALU.subtract)
                P.tt(dv[:, :, 0:1], stsh[:, ct, :].us(2), pv[:, :, 0:1], ALU.subtract)
                P.copy(osh[:, ct, :].us(2), pv[:, :, TS - 1:TS])
            P.copy(shiftc[l][:, ct:ct + 1], p[:, pn - 1:pn])
            P.stt(p[:, 0:T], d[:, 0:T], vc(l, "mu", ct), p[:, 0:T], ALU.mult, ALU.add)
        if ns:
            P.dma(do["o_shift_s"][l], osh.v(), is_out=True)
        if last:
            P.dma(do["o_shift_p"][l], shiftc[l].v(), is_out=True)
        rT = pt[:, cfg.ct_r:cfg.ct_r + NJ, :]
        kT = pt[:, cfg.ct_k:cfg.ct_k + NJ, :]
        vT = pt[:, cfg.ct_v:cfg.ct_v + NJ, :]
        pwa = pt[:, cfg.ct_wa, :]
        B = [P.alloc(f"rwB{i}", [NJ, TMAX]) for i in range(6)]
        tw = P.alloc("rw_tw", [TMAX])
        sgl = P.alloc("rw_sgl", [TMAX])
        gT = P.alloc("rw_gT", [RH, TMAX])
        P.act(tw[0:64, 0:T], pwa[0:64, 0:T], AF.Tanh)
        P.act(sgl[:, 0:T], pt[:, cfg.ct_gl, 0:T], AF.Sigmoid)
        sg, cs, t3, a_, kk, ka = B
        for j in range(NJ):
            ps = P.ps()
            P.mm(ps[:, 0:T], wa[0:64, j * 128:(j + 1) * 128], tw[0:64, 0:T])
            P.act(sg[:, j, 0:T], ps[:, 0:T], AF.Sigmoid, bias=vc(l, "w0", j))
            ps2 = P.ps()
            P.mm(ps2[:, 0:T], wa[64:128, j * 128:(j + 1) * 128], pwa[64:128, 0:T])
            P.act(a_[:, j, 0:T], ps2[:, 0:T], AF.Sigmoid, bias=vc(l, "a0", j))
            P.scan(cs[:, j, 0:T], cmk(T), sg[:, j, 0:T], 0.0)
        for hd in range(RH):
            ps = P.ps()
            P.mm(ps[0:64, 0:T], gup[:, hd * 64:(hd + 1) * 64], sgl[:, 0:T])
            P.copy(gT[0:64, hd, 0:T], ps[0:64, 0:T], eng="act")
        P.tt(t3[:, :, 0:T], cs[:, :, 0:T], sg[:, :, 0:T], ALU.subtract)
        P.act(t3[:, :, 0:T], t3[:, :, 0:T], AF.Exp, scale=-RWKV_DECAY_SCALE)
        eb, enb = sg, P.alloc("rw_enb", [NJ, TMAX])
        P.act(eb[:, :, 0:T], cs[:, :, 0:T], AF.Exp, scale=-RWKV_DECAY_SCALE)
        P.act(enb[:, :, 0:T], cs[:, :, 0:T], AF.Exp, scale=RWKV_DECAY_SCALE)
        prod = cs
        sq = P.alloc("rw_sq", [TMAX])
        nrm = P.alloc("rw_nrm", [TMAX])
        for j in range(NJ):
            P.ts(kk[:, j, 0:T], kT[:, j, 0:T], vc(l, "k_k", j), None, ALU.mult)
            P.act(sq[:, 0:T], kk[:, j, 0:T], AF.Square)
            ps = P.ps()
            P.mm(ps[:, 0:T], blockones, sq[:, 0:T])
            P.act(nrm[:, 0:T], ps[:, 0:T], AF.Sqrt)
            P.ts(nrm[:, 0:T], nrm[:, 0:T], 1e-12, None, ALU.max)
            P.recip(nrm[:, 0:T], nrm[:, 0:T])
            P.tt(kk[:, j, 0:T], kk[:, j, 0:T], nrm[:, 0:T], ALU.mult)
            P.ts(sq[:, 0:T], a_[:, j, 0:T], -1.0, vc(l, "k_a", j), ALU.add, ALU.mult)
            P.stt(kT[:, j, 0:T], sq[:, 0:T], 1.0, kT[:, j, 0:T], ALU.add, ALU.mult)
            P.tt(ka[:, j, 0:T], kk[:, j, 0:T], a_[:, j, 0:T], ALU.mult)
            P.stt(prod[:, j, 0:T], rT[:, j, 0:T], vc(l, "r_k", j), kT[:, j, 0:T], ALU.mult, ALU.mult)
        P.tt(ka[:, :, 0:T], ka[:, :, 0:T], enb[:, :, 0:T], ALU.mult)
        P.tt(kT[:, :, 0:T], kT[:, :, 0:T], enb[:, :, 0:T], ALU.mult)
        P.tt(kk[:, :, 0:T], kk[:, :, 0:T], t3[:, :, 0:T], ALU.mult)
        P.tt(rT[:, :, 0:T], rT[:, :, 0:T], eb[:, :, 0:T], ALU.mult)
        S = la_scratch(G_r, True)
        ysb = P.alloc("rw_y", [64])
        yc = P.alloc("rw_yc", [64])
        st1 = P.alloc("rw_st", [8])
        Hs = [P.alloc(f"rw_Hs{i}", [NJ, 64]) for i in range(2)]
        for (kind, c0, Cc, q) in chunks_of(pn, ns):
            R = RH * Cc
            if kind == "p":
                Hst = Hr[l]
            else:
                Hst = Hs[q % 2]
                P.dma(Hst.v(), di["st_rwkv"][l, q].rearrange("(j p) v -> p j v", p=128))
            sl = slice(c0, c0 + Cc)
            psY, V_hs, psS = la_chunk(P, G_r, S, Cc, rT[:, :, sl], kT[:, :, sl], vT[:, :, sl],
                                      eb[:, :, c0 + Cc - 1], Hst, av=ka[:, :, sl], bv=kk[:, :, sl],
                                      prodv=prod[:, :, sl])
            if kind == "s":
                P.dma(do["o_rwkv_s"][l, q].rearrange("(j p) v -> p j v", p=128), Hst.v(), is_out=True)
            P.copy(ysb[0:R, :], psY, eng="act")
            P.rsum(st1[0:R, 0:1], ysb[0:R, :])
            P.ts(st1[0:R, 0:1], st1[0:R, 0:1], -1.0 / 64, None, ALU.mult)
            P.ts(yc[0:R, :], ysb[0:R, :], st1[0:R, 0:1], None, ALU.add)
            P.tt(ysb[0:R, :], yc[0:R, :], yc[0:R, :], ALU.mult)
            P.rsum(st1[0:R, 1:2], ysb[0:R, :])
            P.act(st1[0:R, 1:2], st1[0:R, 1:2], AF.Sqrt, bias=RWKV_GN_EPS, scale=1.0 / 64)
            P.recip(st1[0:R, 1:2], st1[0:R, 1:2])
            P.stt(yc[0:R, :], yc[0:R, :], st1[0:R, 1:2], ln[Cc][0:R, :], ALU.mult, ALU.mult)
            P.copy(st1[0:R, 2:3], psS[0:R, 0:1], eng="act")
            P.stt(yc[0:R, :], V_hs, st1[0:R, 2:3], yc[0:R, :], ALU.mult, ALU.add)
            pT = P.ps()
            P.transpose(pT[0:64, 0:R], yc[0:R, :], ident[0:R, 0:R])
            P.tt(yTr[0:64, :, sl], pT[0:64, 0:R].re("p (h t) -> p h t", t=Cc), gT[0:64, :, sl], ALU.mult)
        if last:
            P.dma(do["o_rwkv_p"][l].rearrange("(j p) v -> p j v", p=128), Hr[l].v(), is_out=True)

    def la_tail(G, S, l, T, pn, ns, qT, kT, vT, eb, Hp, st_key, out_s, out_p, ng, sgate, oT, last):
        NH = G["NH"]
        NJk = G["NJ"]
        Hs = [P.alloc(f"la_Hs{i}", [NJk, 128]) for i in range(2)]
        ysb = P.alloc("la_y", [128])
        st1 = P.alloc("la_st", [4])
        for (kind, c0, Cc, q) in chunks_of(pn, ns, cfg.CL):
            R = NH * Cc
            if kind == "p":
                Hst = Hp
            else:
                Hst = Hs[q % 2]
                P.dma(Hst.v(), di[st_key][l, q].rearrange("(j p) v -> p j v", p=128))
            sl = slice(c0, c0 + Cc)
            psY, V_hs, _ = la_chunk(P, G, S, Cc, qT[:, :, sl], kT[:, :, sl], vT[:, :, sl], eb[:, :, c0 + Cc - 1], Hst)
            if kind == "s":
                P.dma(do[out_s][l, q].rearrange("(j p) v -> p j v", p=128), Hst.v(), is_out=True)
            P.act(ysb[0:R, :], psY, AF.Square)
            P.rsum(st1[0:R, 0:1], ysb[0:R, :])
            P.act(st1[0:R, 0:1], st1[0:R, 0:1], AF.Sqrt, bias=EPS, scale=1.0 / 128)
            P.recip(st1[0:R, 0:1], st1[0:R, 0:1])
            P.stt(ysb[0:R, :], psY, st1[0:R, 0:1], ng[Cc][0:R, :], ALU.mult, ALU.mult)
            pT = P.ps()
            P.transpose(pT[:, 0:R], ysb[0:R, :], ident[0:R, 0:R])
            P.tt(oT[:, :, sl], pT[:, 0:R].re("p (h t) -> p h t", t=Cc), sgate[:, :, sl], ALU.mult)
        if last:
            P.dma(do[out_p][l].rearrange("(j p) v -> p j v", p=128), Hp.v(), is_out=True)

    def hgrn(ti, l, T, pn, ns, oT):
        last = ti == ntile - 1
        P.off = mix_base
        pt = P.alloc("pt_h", [4 * NJ, TMAX])
        project(T, cfg.ct_hq, 4 * NJ, pt)
        ng = {}
        for Cc in sorted(set(c_[2] for c_ in chunks_of(pn, ns, cfg.CL))):
            ng[Cc] = P.alloc(f"hg_n{Cc}", [128])
            P.dma(ng[Cc][0:HH * Cc], di[f"hg_n{Cc}"][l])
        qT, fT, iT, gT_ = (pt[:, k * NJ:(k + 1) * NJ, :] for k in range(4))
        lf = P.alloc("hg_lf", [NJ, TMAX])
        cs = P.alloc("hg_cs", [NJ, TMAX])
        eb = P.alloc("hg_eb", [NJ, TMAX])
        P.act(qT[:, :, 0:T], qT[:, :, 0:T], AF.Silu)
        P.act(gT_[:, :, 0:T], gT_[:, :, 0:T], AF.Silu)
        P.act(fT[:, :, 0:T], fT[:, :, 0:T], AF.Sigmoid, scale=-1.0)
        for j in range(NJ):
            P.ts(fT[:, j, 0:T], fT[:, j, 0:T], oml[:, l, j:j + 1], HGRN_MAX_INPUT, ALU.mult, ALU.min)
            P.act(lf[:, j, 0:T], fT[:, j, 0:T], AF.Ln, bias=1.0, scale=-1.0)
            P.scan(cs[:, j, 0:T], cmk2(T), lf[:, j, 0:T], 0.0)
        P.act(eb[:, :, 0:T], cs[:, :, 0:T], AF.Exp)
        P.act(lf[:, :, 0:T], cs[:, :, 0:T], AF.Exp, scale=-1.0)
        P.tt(qT[:, :, 0:T], qT[:, :, 0:T], eb[:, :, 0:T], ALU.mult)
        P.tt(fT[:, :, 0:T], fT[:, :, 0:T], lf[:, :, 0:T], ALU.mult)
        S = la_scratch(G_h, False)
        la_tail(G_h, S, l, T, pn, ns, qT, fT, iT, eb, Hh[l], "st_hgrn", "o_hgrn_s", "o_hgrn_p", ng, gT_, oT, last)

    def gla(ti, l, T, pn, ns, oT):
        last = ti == ntile - 1
        P.off = mix_base
        nct = cfg.NCT - cfg.ct_gq
        pt = P.alloc("pt_g", [nct, TMAX])
        project(T, cfg.ct_gq, nct, pt)
        ng = {}
        for Cc in sorted(set(c_[2] for c_ in chunks_of(pn, ns, cfg.CL))):
            ng[Cc] = P.alloc(f"gl_n{Cc}", [128])
            P.dma(ng[Cc][0:GH * Cc], di[f"gl_n{Cc}"][l])
        gku = P.alloc("gk_up", [GW // 2])
        P.dma(gku[0:16, :], di["gk_up"][l])
        negb = P.alloc("gl_negb", [GQT])
        P.ts(negb.v(), vc(l, "gk_b"), -1.0, None, ALU.mult)
        o = 0
        qT = pt[:, o:o + GQT, :]; o += GQT
        kT = pt[:, o:o + GQT, :]; o += GQT
        vT = pt[:, o:o + NJ, :]; o += NJ
        glo = pt[:, o, :]; o += 1
        gT_ = pt[:, o:o + NJ, :]
        ll = P.alloc("gl_l", [GQT, TMAX])
        cs = P.alloc("gl_cs", [GQT, TMAX])
        eb = P.alloc("gl_eb", [GQT, TMAX])
        for j in range(GQT):
            ps = P.ps()
            P.mm(ps[:, 0:T], gku[0:16, j * 128:(j + 1) * 128], glo[0:16, 0:T])
            P.act(ll[:, j, 0:T], ps[:, 0:T], AF.Exp, bias=negb[:, j:j + 1], scale=-1.0)
            P.act(ll[:, j, 0:T], ll[:, j, 0:T], AF.Ln, bias=1.0)
            P.scan(cs[:, j, 0:T], cmk2(T), ll[:, j, 0:T], 0.0)
        P.act(gT_[:, :, 0:T], gT_[:, :, 0:T], AF.Silu)
        P.act(eb[:, :, 0:T], cs[:, :, 0:T], AF.Exp, scale=-1.0 / 16)
        P.act(ll[:, :, 0:T], cs[:, :, 0:T], AF.Exp, scale=1.0 / 16)
        P.stt(qT[:, :, 0:T], qT[:, :, 0:T], 0.125, eb[:, :, 0:T], ALU.mult, ALU.mult)
        P.tt(kT[:, :, 0:T], kT[:, :, 0:T], ll[:, :, 0:T], ALU.mult)
        S = la_scratch(G_g, False)
        la_tail(G_g, S, l, T, pn, ns, qT, kT, vT, eb, Hg[l], "st_gla", "o_gla_s", "o_gla_p", ng, gT_, oT, last)

    def s5(ti, l, T, pn, ns, oT):
        last = ti == ntile - 1
        P.off = mix_base
        pt = P.alloc("pt_s", [NJ, TMAX])
        project(T, cfg.ct_s5, NJ, pt)
        wgl = P.alloc("s5_wglu", [NJ, GW])
        P.dma(wgl.v(), di["s5_wglu"][l].rearrange("(c p) n -> p c n", p=128))
        BXr, BXi = P.alloc("BXr", [S5C, 128]), P.alloc("BXi", [S5C, 128])
        bbr, bbi = P.alloc("bbr", [S5C, 128]), P.alloc("bbi", [S5C, 128])
        CXr, CXi = P.alloc("CXr", [S5C, 128]), P.alloc("CXi", [S5C, 128])
        for b_ in (BXr, BXi, CXr, CXi):
            P.memset(b_.v(), 0.0)
        for b8 in range(8):
            for (dst, key) in ((BXr, "s5_Bt_re"), (BXi, "s5_Bt_im")):
                P.dma(dst[b8 * 16:(b8 + 1) * 16, (b8 // 2)::4, (b8 % 2) * 64:(b8 % 2) * 64 + 64],
                      di[key][l].rearrange("(a b) c p -> b c a p", b=8)[b8])
            for (dst, key) in ((CXr, "s5_Ct_re"), (CXi, "s5_Ct_im")):
                P.dma(dst[(b8 % 2) * 64:(b8 % 2) * 64 + 64, (b8 // 2)::4, b8 * 16:(b8 + 1) * 16],
                      di[key][l].rearrange("(a b) p c -> b p a c", b=8)[b8])
        P.ts(CXi.v(), CXi.v(), -1.0, None, ALU.mult)
        sc = [P.alloc(f"s5c_{i}", [S5C]) for i in range(12)]
        dtv, ar, th, rho, sn, cs_, abre, abim, den, core, coim, tmp = sc
        P.act(dtv.v(), vc(l, "logdt"), AF.Exp)
        P.tt(ar.v(), vc(l, "A_re"), dtv.v(), ALU.mult)
        P.tt(th.v(), vc(l, "A_im"), dtv.v(), ALU.mult)
        P.act(rho.v(), ar.v(), AF.Exp)
        range_reduce(sn.v(), th.v(), tmp.v())
        P.act(sn.v(), sn.v(), AF.Sin)
        P.ts(cs_.v(), th.v(), math.pi / 2, None, ALU.add)
        range_reduce(cs_.v(), cs_.v(), tmp.v())
        P.act(cs_.v(), cs_.v(), AF.Sin)
        P.tt(abre.v(), rho.v(), cs_.v(), ALU.mult)
        P.tt(abim.v(), rho.v(), sn.v(), ALU.mult)
        P.tt(den.v(), vc(l, "A_re"), vc(l, "A_re"), ALU.mult)
        P.tt(tmp.v(), vc(l, "A_im"), vc(l, "A_im"), ALU.mult)
        P.tt(den.v(), den.v(), tmp.v(), ALU.add)
        P.recip(den.v(), den.v())
        P.ts(abre.v(), abre.v(), -1.0, None, ALU.add)
        P.tt(core.v(), abre.v(), vc(l, "A_re"), ALU.mult)
        P.tt(tmp.v(), abim.v(), vc(l, "A_im"), ALU.mult)
        P.tt(core.v(), core.v(), tmp.v(), ALU.add)
        P.tt(core.v(), core.v(), den.v(), ALU.mult)
        P.tt(coim.v(), abim.v(), vc(l, "A_re"), ALU.mult)
        P.tt(tmp.v(), abre.v(), vc(l, "A_im"), ALU.mult)
        P.tt(coim.v(), coim.v(), tmp.v(), ALU.subtract)
        P.tt(coim.v(), coim.v(), den.v(), ALU.mult)
        dg = [P.alloc(f"s5dg{i}", [128]) for i in range(2)]
        t1, t2 = P.alloc("s5t1", [128]), P.alloc("s5t2", [128])
        for oc in range(S5C):
            P.ts(dg[0].v(), ident, core[:, oc:oc + 1], None, ALU.mult)
            P.ts(dg[1].v(), ident, coim[:, oc:oc + 1], None, ALU.mult)
            pr, pi_ = P.ps(), P.ps()
            P.mm(pr[:, 0:128], ones_f, dg[0].v())
            P.mm(pi_[:, 0:128], ones_f, dg[1].v())
            P.tt(t1.v(), BXr[:, oc, :], pr[:, 0:128], ALU.mult)
            P.tt(t2.v(), BXi[:, oc, :], pi_[:, 0:128], ALU.mult)
            P.tt(bbr[:, oc, :], t1.v(), t2.v(), ALU.subtract)
            P.tt(t1.v(), BXi[:, oc, :], pr[:, 0:128], ALU.mult)
            P.tt(t2.v(), BXr[:, oc, :], pi_[:, 0:128], ALU.mult)
            P.tt(bbi[:, oc, :], t1.v(), t2.v(), ALU.add)
        if ns:
            sts = [P.alloc(f"s5st{i}", [S5C, NS]) for i in range(2)]
            osts = [P.alloc(f"s5ost{i}", [S5C, NS]) for i in range(2)]
            P.dma(sts[0].v(), di["st_s5re"][l])
            P.dma(sts[1].v(), di["st_s5im"][l])
        W = [P.alloc(f"s5w{i}", [TMAX]) for i in range(9)]
        a1, snt, cst, zre, zim, dec, u1, u2, u3 = W
        hb = P.alloc("s5hb", [4, 2, TMAX])
        ygl = P.alloc("s5ygl", [NJ, TMAX])
        for oc in range(S5C):
            u = pt[:, oc // 4, 0:T]
            pbr, pbi = P.ps(), P.ps()
            P.mm(pbr[:, 0:T], bbr[:, oc, :], u)
            P.mm(pbi[:, 0:T], bbi[:, oc, :], u)
            P.ts(a1[:, 0:T], posv(T), th[:, oc:oc + 1], None, ALU.mult)
            range_reduce(snt[:, 0:T], a1[:, 0:T], u1[:, 0:T])
            P.act(snt[:, 0:T], snt[:, 0:T], AF.Sin)
            P.ts(a1[:, 0:T], a1[:, 0:T], math.pi / 2, None, ALU.add)
            range_reduce(cst[:, 0:T], a1[:, 0:T], u1[:, 0:T])
            P.act(cst[:, 0:T], cst[:, 0:T], AF.Sin)
            P.tt(u1[:, 0:T], cst[:, 0:T], pbr[:, 0:T], ALU.mult)
            P.tt(u2[:, 0:T], snt[:, 0:T], pbi[:, 0:T], ALU.mult)
            P.tt(zre[:, 0:T], u1[:, 0:T], u2[:, 0:T], ALU.add)
            P.tt(u1[:, 0:T], cst[:, 0:T], pbi[:, 0:T], ALU.mult)
            P.tt(u2[:, 0:T], snt[:, 0:T], pbr[:, 0:T], ALU.mult)
            P.tt(zim[:, 0:T], u1[:, 0:T], u2[:, 0:T], ALU.subtract)
            P.ts(dec[:, 0:T], smk(T), rho[:, oc:oc + 1], None, ALU.mult)
            for i, z in enumerate((zre, zim)):
                P.stt(z[:, 0:1], s5c[l][i][:, oc:oc + 1], rho[:, oc:oc + 1], z[:, 0:1], ALU.mult, ALU.add)
                if ns:
                    zv = z[:, pn:T].re("p (s t) -> p s t", t=TS)[:, :, 0:1]
                    P.stt(zv, sts[i][:, oc, :].us(2), rho[:, oc:oc + 1], zv, ALU.mult, ALU.add)
            P.scan(u1[:, 0:T], dec[:, 0:T], zre[:, 0:T], 0.0)
            P.scan(u2[:, 0:T], dec[:, 0:T], zim[:, 0:T], 0.0)
            hre, him = hb[:, oc % 4, 0, :], hb[:, oc % 4, 1, :]
            P.tt(zre[:, 0:T], cst[:, 0:T], u1[:, 0:T], ALU.mult)
            P.tt(zim[:, 0:T], snt[:, 0:T], u2[:, 0:T], ALU.mult)
            P.tt(hre[:, 0:T], zre[:, 0:T], zim[:, 0:T], ALU.subtract)
            P.tt(zre[:, 0:T], cst[:, 0:T], u2[:, 0:T], ALU.mult)
            P.tt(zim[:, 0:T], snt[:, 0:T], u1[:, 0:T], ALU.mult)
            P.tt(him[:, 0:T], zre[:, 0:T], zim[:, 0:T], ALU.add)
            for i, hh_ in enumerate((hre, him)):
                P.copy(s5c[l][i][:, oc:oc + 1], hh_[:, pn - 1:pn])
                if ns:
                    P.copy(osts[i][:, oc, :].us(2), hh_[:, pn:T].re("p (s t) -> p s t", t=TS)[:, :, TS - 1:TS])
            if oc % 4 == 3:
                ot = oc // 4
                py = P.ps()
                for i in range(4):
                    P.mm(py[:, 0:T], CXr[:, 4 * ot + i, :], hb[:, i, 0, 0:T], start=(i == 0), stop=False)
                    P.mm(py[:, 0:T], CXi[:, 4 * ot + i, :], hb[:, i, 1, 0:T], start=False, stop=(i == 3))
                P.stt(ygl[:, ot, 0:T], pt[:, ot, 0:T], vc(l, "s5_D", ot), py[:, 0:T], ALU.mult, ALU.add)
                P.act(ygl[:, ot, 0:T], ygl[:, ot, 0:T], AF.Gelu_apprx_tanh)
        for ot in range(NJ):
            ps = P.ps()
            for kt in range(NJ):
                P.mm(ps[:, 0:T], wgl[:, kt, ot * 128:(ot + 1) * 128], ygl[:, kt, 0:T], start=(kt == 0),
                     stop=(kt == NJ - 1))
            P.act(u3[:, 0:T], ps[:, 0:T], AF.Sigmoid, bias=vc(l, "s5_bglu", ot))
            P.tt(oT[:, ot, 0:T], ygl[:, ot, 0:T], u3[:, 0:T], ALU.mult)
        if ns:
            P.dma(do["o_s5re_s"][l], osts[0].v(), is_out=True)
            P.dma(do["o_s5im_s"][l], osts[1].v(), is_out=True)
        if last:
            P.dma(do["o_s5re_p"][l], s5c[l][0].v(), is_out=True)
            P.dma(do["o_s5im_p"][l], s5c[l][1].v(), is_out=True)

    def layer(ti, l, T, pn, ns):
        nonlocal mix_base
        last = ti == ntile - 1
        P.off = phase_base
        yTr = P.alloc("yTr", [RH, TMAX], BF16)
        mixB = [P.alloc(f"mixB{i}", [NJ, TMAX], BF16) for i in range(3)]
        mix_base = P.off
        rmsnorm(T, "norm_mix", l, h)
        mix_base = P.off
        rwkv(ti, l, T, pn, ns, yTr)
        s5(ti, l, T, pn, ns, mixB[0])
        hgrn(ti, l, T, pn, ns, mixB[1])
        gla(ti, l, T, pn, ns, mixB[2])
        for og in range(D // 256):
            (wa_, wb_), b = WS.get("wout")
            for i in range(2):
                oc = og * 2 + i
                ps = P.ps()
                for hd in range(RH):
                    P.mm(ps[:, 0:T], wa_[0:64, hd, i * 128:(i + 1) * 128], yTr[0:64, hd, 0:T], start=(hd == 0),
                         stop=False)
                for k in range(3 * NJ):
                    P.mm(ps[:, 0:T], wb_[:, k, i * 128:(i + 1) * 128], mixB[k // NJ][:, k % NJ, 0:T], start=False,
                         stop=(k == 3 * NJ - 1))
                P.tt(x[:, oc, 0:T], x[:, oc, 0:T], ps[:, 0:T], ALU.add)
        P.off = mix_base
        rmsnorm(T, "norm_ffn", l, h)
        aT = P.alloc("aT", [NFT, TMAX], BF16)
        uxp = [P.alloc(f"uxp{i}", [TMAX + 2]) for i in range(2)]
        acc = [P.alloc(f"acc{i}", [TMAX]) for i in range(2)]
        if ns:
            stc = P.alloc("stc", [NFT, NS, 2])
            ostc = P.alloc("ostc", [NFT, NS, 2])
            uxs = [P.alloc(f"uxs{i}", [NS, TS + 2]) for i in range(2)]
            P.dma(stc.v(), di["st_conv"][l])
        j = 0
        while j < NFT:
            (wv,), b = WS.get("wup")
            for jj in range(b[2]):
                pu, pg = P.ps(), P.ps()
                for c in range(NDC):
                    P.mm(pu[:, 0:T], wv[:, c, 0, jj * 128:(jj + 1) * 128], h[:, c, 0:T], start=(c == 0),
                         stop=(c == NDC - 1))
                for c in range(NDC):
                    P.mm(pg[:, 0:T], wv[:, c, 1, jj * 128:(jj + 1) * 128], h[:, c, 0:T], start=(c == 0),
                         stop=(c == NDC - 1))
                ux, ac = uxp[j % 2], acc[j % 2]
                w0, w1, w2, cb = (vc(l, f"conv_w{i}", j) for i in range(3)), None, None, vc(l, "conv_b", j)
                w0, w1, w2 = list(w0)
                P.copy(ux[:, 0:2], convc[l][:, j, :])
                P.copy(ux[:, 2:2 + pn], pu[:, 0:pn], eng="act")
                P.copy(convc[l][:, j, :], ux[:, pn:pn + 2])
                P.ts(ac[:, 0:pn], ux[:, 0:pn], w0, cb, ALU.mult, ALU.add)
                P.stt(ac[:, 0:pn], ux[:, 1:pn + 1], w1, ac[:, 0:pn], ALU.mult, ALU.add)
                P.stt(ac[:, 0:pn], ux[:, 2:pn + 2], w2, ac[:, 0:pn], ALU.mult, ALU.add)
                if ns:
                    us_ = uxs[j % 2]
                    acs = ac[:, pn:T].re("p (s t) -> p s t", t=TS)
                    P.copy(us_[:, :, 0:2], stc[:, j, :, :])
                    P.copy(us_[:, :, 2:2 + TS], pu[:, pn:T].re("p (s t) -> p s t", t=TS), eng="act")
                    P.copy(ostc[:, j, :, :], us_[:, :, TS:TS + 2])
                    P.ts(acs, us_[:, :, 0:TS], w0, cb, ALU.mult, ALU.add)
                    P.stt(acs, us_[:, :, 1:TS + 1], w1, acs, ALU.mult, ALU.add)
                    P.stt(acs, us_[:, :, 2:TS + 2], w2, acs, ALU.mult, ALU.add)
                P.act(ac[:, 0:T], ac[:, 0:T], AF.Gelu_apprx_tanh)
                P.tt(aT[:, j, 0:T], ac[:, 0:T], pg[:, 0:T], ALU.mult)
                j += 1
        if ns:
            P.dma(do["o_conv_s"][l], ostc.v(), is_out=True)
        if last:
            P.dma(do["o_conv_p"][l], convc[l].v(), is_out=True)
        for og in range(D // 512):
            pss = [P.ps() for _ in range(4)]
            k0 = 0
            while k0 < NFT:
                (wv,), b = WS.get("wdown")
                assert b[1] == og and b[2] == k0
                for kk_ in range(b[3]):
                    kt = k0 + kk_
                    for i in range(4):
                        P.mm(pss[i][:, 0:T], wv[:, kk_, i * 128:(i + 1) * 128], aT[:, kt, 0:T], start=(kt == 0),
                             stop=(kt == NFT - 1))
                k0 += b[3]
            for i in range(4):
                oc = og * 4 + i
                P.tt(x[:, oc, 0:T], x[:, oc, 0:T], pss[i][:, 0:T], ALU.add)

    mix_base = phase_base
    for ti, (p0, pn, ns) in enumerate(cfg.tiles):
        T = pn + ns * TS
        P.dma(x[:, :, 0:pn], di["xin"][:, p0:p0 + pn].rearrange("(c p) t -> p c t", p=128))
        if ns:
            P.dma(x[:, :, pn:T], di["xin"][:, cfg.NP:cfg.NP + ns * TS].rearrange("(c p) t -> p c t", p=128))
        P.dma(tmask.v(), di["tmask"][ti].rearrange("k p t -> p k t"))
        for l in range(L):
            layer(ti, l, T, pn, ns)
        P.off = mix_base
        yo = P.alloc("yo", [NDC, TMAX])
        sq = [P.alloc(f"fsq{i}", [TMAX], BF16) for i in range(2)]
        rstd = P.alloc("frstd", [TMAX])
        ps = P.ps()
        for c in range(NDC):
            s_ = sq[c % 2]
            P.act(s_[:, 0:T], x[:, c, 0:T], AF.Square)
            P.mm(ps[:, 0:T], ones_bf.v(), s_[:, 0:T], start=(c == 0), stop=(c == NDC - 1))
        P.act(rstd[:, 0:T], ps[:, 0:T], AF.Sqrt, bias=EPS, scale=1.0 / D)
        P.recip(rstd[:, 0:T], rstd[:, 0:T])
        for c in range(NDC):
            P.stt(yo[:, c, 0:T], x[:, c, 0:T], normf[:, c:c + 1], rstd[:, 0:T], ALU.mult, ALU.mult)
        P.dma(do["yT"][:, p0:p0 + pn].rearrange("(c p) t -> p c t", p=128), yo[:, :, 0:pn], is_out=True)
        if ns:
            P.dma(do["yT"][:, cfg.NP:cfg.NP + ns * TS].rearrange("(c p) t -> p c t", p=128), yo[:, :, pn:T],
                  is_out=True)
    P.emit()
    return nc, P


REAL = dict(D=2048, L=4, NP=2064, NS=16, TILE=256, C=16)
_cache = {}


def kernel(**inputs):
    cfg = Cfg(**REAL)
    inp = {k: np.asarray(v) for k, v in inputs.items()}
    B = inp["x_prompt"].shape[0]
    ncore = 8
    sh = pack_shared(cfg, inp)
    in_maps = []
    for c in range(ncore):
        m = dict(sh)
        m.update(pack_core(cfg, inp, c % B, c * cfg.NS))
        in_maps.append(m)
    nc, _ = build_program(cfg)
    res = run_bass_kernel_spmd(nc, in_maps, core_ids=list(range(ncore)))
    outs = [unpack_core(cfg, r) for r in res.results]
    cat = lambda k: np.concatenate([o[k] for o in outs], axis=1)
    stack_p = lambda k: np.stack([outs[b][k] for b in range(B)], axis=1)
    f = lambda a: np.ascontiguousarray(a, dtype=np.float32)
    return (f(np.stack([outs[b]["y_p"] for b in range(B)], axis=0)),
            f(np.concatenate([o["y_s"] for o in outs], axis=0)),
            f(stack_p("rwkv_p")), f(cat("rwkv_s")), f(stack_p("shift_p")), f(cat("shift_s")),
            f(stack_p("s5re_p")), f(cat("s5re_s")), f(stack_p("s5im_p")), f(cat("s5im_s")),
            f(stack_p("hgrn_p")), f(cat("hgrn_s")), f(stack_p("gla_p")), f(cat("gla_s")),
            f(stack_p("conv_p")), f(cat("conv_s")))
```

```python
import math
from contextlib import ExitStack
import numpy as np
import concourse.bass as bass
import concourse.mybir as mybir
from concourse.bass_utils import run_bass_kernel_spmd

F32 = mybir.dt.float32
BF16 = mybir.dt.bfloat16
AF = mybir.ActivationFunctionType
ALU = mybir.AluOpType
AX = mybir.AxisListType

EPS = 1e-6
RWKV_DECAY_SCALE = 0.606531
RWKV_GN_EPS = 64e-5
HGRN_MAX_INPUT = 1.0 - 1e-4
N_META = 16
TS = 4


class Cfg:
    def __init__(s, D=2048, L=4, NP=2064, NS=16, TILE=256, C=16):
        s.D, s.L, s.NP, s.NS, s.TILE, s.C = D, L, NP, NS, TILE, C
        s.CL = 32
        s.CCS = (32, 16, TS)
        s.GW = D // 4
        s.NDC = D // 128
        s.RH = s.GW // 64
        s.NJ = s.GW // 128
        s.HH = s.GW // 128
        s.GH = s.GW // 128
        s.GQT = s.GW // 256
        s.S5C = (s.GW // 16) * 64 // 128
        s.DFF = ((8 * D // 3 + 127) // 128) * 128
        s.NFT = s.DFF // 128
        NJ = s.NJ
        s.ct_r, s.ct_k, s.ct_v, s.ct_wa, s.ct_gl = 0, NJ, 2 * NJ, 3 * NJ, 3 * NJ + 1
        s.NR = 3 * NJ + 2
        s.ct_s5 = s.NR
        s.ct_hq = s.ct_s5 + NJ
        s.ct_hf, s.ct_hi, s.ct_hg = s.ct_hq + NJ, s.ct_hq + 2 * NJ, s.ct_hq + 3 * NJ
        s.ct_gq = s.ct_hq + 4 * NJ
        s.ct_gk = s.ct_gq + s.GQT
        s.ct_gv = s.ct_gk + s.GQT
        s.ct_glo = s.ct_gv + NJ
        s.ct_gg = s.ct_glo + 1
        s.NCT = s.ct_gg + NJ
        s.mix_groups = [(0, s.NR), (s.ct_s5, NJ), (s.ct_hq, 4 * NJ), (s.ct_gq, s.NCT - s.ct_gq)]
        s.tiles = []
        p = 0
        while s.NP - p > C:
            n = min(TILE, s.NP - C - p)
            s.tiles.append((p, n, 0))
            p += n
        s.tiles.append((p, s.NP - p, NS))
        s.NTOK = s.NP + NS * TS
        s.TMAX = max(max(t[1] + t[2] * TS for t in s.tiles), 1)
        v = {}
        o = 0
        for name, n in [("norm_mix", s.NDC), ("norm_ffn", s.NDC), ("mu", s.NR), ("w0", NJ), ("a0", NJ),
                        ("k_k", NJ), ("k_a", NJ), ("r_k", NJ), ("s5_D", NJ), ("s5_bglu", NJ),
                        ("gk_b", s.GQT), ("conv_w0", s.NFT), ("conv_w1", s.NFT), ("conv_w2", s.NFT),
                        ("conv_b", s.NFT), ("A_re", s.S5C), ("A_im", s.S5C), ("logdt", s.S5C),
                        ("hlb", NJ)]:
            v[name] = (o, n)
            o += n
        s.vec = v
        s.NVEC = o


def build_consts(cfg):
    c = {}
    c["ident"] = np.eye(128, dtype=np.float32)
    c["ones"] = np.ones((128, 128), np.float32)
    bo = np.zeros((128, 128), np.float32)
    bo[:64, :64] = 1
    bo[64:, 64:] = 1
    c["blockones"] = bo
    f64 = np.zeros((128, 64), np.float32)
    f64[np.arange(128), np.arange(128) % 64] = 1
    c["f64"] = f64
    for C in cfg.CCS:
        r = np.arange(128) % C
        c[f"mstrT{C}"] = (r[:, None] < r[None, :]).astype(np.float32)
        c[f"minclT{C}"] = (r[:, None] <= r[None, :]).astype(np.float32)
        c[f"mstr{C}"] = (r[:, None] > r[None, :]).astype(np.float32)
    return c


def head_mask(NJ, HPT, NH):
    DK = 128 // HPT
    m = np.zeros((128, NJ, NH), np.float32)
    for j in range(NJ):
        for p in range(128):
            h = j * HPT + p // DK
            if h < NH:
                m[p, j, h] = 1
    return m


def col_mask(C, NJ, HPT, NH):
    m = np.zeros((128, NJ, HPT), np.float32)
    for row in range(min(128, NH * C)):
        h = row // C
        m[row, h // HPT, h % HPT] = 1
    return m


class Buf:
    def __init__(s, name, ap, pages, space):
        s.name, s.ap, s.pages, s.space = name, ap, pages, space

    def __getitem__(s, k):
        return V(s.ap[k], s)

    def v(s):
        return V(s.ap, s)


class V:
    def __init__(s, ap, buf):
        s.ap, s.buf = ap, buf

    def __getitem__(s, k):
        return V(s.ap[k], s.buf)

    def re(s, pat, **kw):
        return V(s.ap.rearrange(pat, **kw), s.buf)

    def bc(s, shape):
        return V(s.ap.to_broadcast(list(shape)), s.buf)

    def us(s, axis):
        return V(s.ap.unsqueeze(axis), s.buf)

    @property
    def shape(s):
        return s.ap.shape


PAGE = 256
ENG_NAMES = ["pe", "act", "dve", "pool", "sp"]
N_DMA_SEM = {"sp": 24, "pool": 64, "act": 2}


class Op:
    __slots__ = ("eng", "fn", "deps", "is_dma", "signal", "idx", "dsem", "dval", "dprev", "is_out", "count")


class Prog:
    def __init__(s, nc, arena_words):
        s.nc = nc
        s.arena = nc.alloc_sbuf_tensor("arena", [128, arena_words], F32)
        s.A = s.arena[:, :]
        s.arena_words = arena_words
        s.off = 0
        s.ops = {e: [] for e in ENG_NAMES}
        s.last_w = {}
        s.readers = {}
        s.dma_count = {e: 0 for e in ENG_NAMES}
        s.psum = []
        for b in range(8):
            t = nc.alloc_psum_tensor(f"psb{b}", [128, 512], F32)
            s.psum.append(Buf(f"ps{b}", t[:, :], [("ps", b)], "psum"))
        s.ps_rr = 0
        s.marks = []

    def alloc(s, name, free_shape, dtype=F32, at=None):
        n = int(np.prod(free_shape))
        words = n if dtype == F32 else (n + 1) // 2
        if at is None:
            off = s.off
            s.off += words
        else:
            off = at
        assert off + words <= s.arena_words, f"arena overflow at {name}: {off + words} > {s.arena_words}"
        ap = s.A[:, off:off + words]
        if dtype != F32:
            ap = ap.bitcast(dtype)
        if len(free_shape) > 1:
            names = " ".join(f"a{i}" for i in range(len(free_shape)))
            kw = {f"a{i}": int(free_shape[i]) for i in range(1, len(free_shape))}
            ap = ap.rearrange(f"p ({names}) -> p {names}", **kw)
        pages = list(range((off * 4) // PAGE, ((off + words) * 4 - 1) // PAGE + 1))
        return Buf(name, ap, pages, "sbuf")

    def ps(s):
        b = s.psum[s.ps_rr % 8]
        s.ps_rr += 1
        return b

    def _rec(s, eng, fn, reads, writes, is_dma=False, is_out=False):
        op = Op()
        op.eng, op.fn, op.is_dma, op.signal, op.is_out = eng, fn, is_dma, False, is_out
        op.idx = len(s.ops[eng])
        deps = set()
        rp = []
        wp = []
        for b in reads:
            if b is not None:
                rp.extend(b.pages)
        for b in writes:
            if b is not None:
                wp.extend(b.pages)
        for p in rp:
            w = s.last_w.get(p)
            if w is not None:
                deps.add(w)
        for p in wp:
            w = s.last_w.get(p)
            if w is not None:
                deps.add(w)
            for r in s.readers.get(p, ()):
                deps.add(r)
        deps.discard(op)
        op.deps = deps
        for p in wp:
            s.last_w[p] = op
            s.readers[p] = []
        for p in rp:
            lst = s.readers.setdefault(p, [])
            if not is_dma:
                lst[:] = [r for r in lst if r.eng != eng or r.is_dma]
            lst.append(op)
        if is_dma:
            k = s.dma_count[eng]
            s.dma_count[eng] += 1
            nds = N_DMA_SEM[eng]
            op.dsem = k % nds
            op.dval = 16 * (k // nds + 1)
            op.dprev = 16 * (k // nds)
        s.ops[eng].append(op)
        return op

    @staticmethod
    def _b(x):
        return x.buf if isinstance(x, V) else None

    @staticmethod
    def _a(x):
        return x.ap if isinstance(x, V) else x

    def mm(s, out, lhsT, rhs, start=True, stop=True):
        o, l, r = out.ap, lhsT.ap, rhs.ap
        s._rec("pe", lambda e: e.matmul(o, l, r, start=start, stop=stop), [lhsT.buf, rhs.buf], [out.buf])

    def transpose(s, out, in_, ident):
        o, i, d = out.ap, in_.ap, ident.ap
        s._rec("pe", lambda e: e.transpose(o, i, d), [in_.buf, ident.buf], [out.buf])

    def act(s, out, in_, func, bias=None, scale=None):
        kw = {}
        rd = [in_.buf]
        if bias is not None:
            kw["bias"] = s._a(bias)
            rd.append(s._b(bias))
        if scale is not None:
            kw["scale"] = s._a(scale)
            rd.append(s._b(scale))
        o, i = out.ap, in_.ap
        s._rec("act", lambda e: e.activation(out=o, in_=i, func=func, **kw), rd, [out.buf])

    def tt(s, out, in0, in1, op, eng="dve"):
        o, a, b = out.ap, in0.ap, in1.ap
        s._rec(eng, lambda e: e.tensor_tensor(out=o, in0=a, in1=b, op=op), [in0.buf, in1.buf], [out.buf])

    def ts(s, out, in0, s1, s2=None, op0=ALU.mult, op1=None, eng="dve"):
        o, a = out.ap, in0.ap
        a1, a2 = s._a(s1), s._a(s2)
        rd = [in0.buf, s._b(s1), s._b(s2)]
        if op1 is None:
            s._rec(eng, lambda e: e.tensor_scalar(out=o, in0=a, scalar1=a1, scalar2=None, op0=op0), rd, [out.buf])
        else:
            s._rec(eng, lambda e: e.tensor_scalar(out=o, in0=a, scalar1=a1, scalar2=a2, op0=op0, op1=op1), rd,
                   [out.buf])

    def stt(s, out, in0, scalar, in1, op0, op1, eng="dve"):
        o, a, b, sc = out.ap, in0.ap, in1.ap, s._a(scalar)
        s._rec(eng, lambda e: e.scalar_tensor_tensor(out=o, in0=a, scalar=sc, in1=b, op0=op0, op1=op1),
               [in0.buf, in1.buf, s._b(scalar)], [out.buf])

    def copy(s, out, in_, eng="dve"):
        o, i = out.ap, in_.ap
        if eng == "act":
            s._rec("act", lambda e: e.activation(out=o, in_=i, func=AF.Copy), [in_.buf], [out.buf])
        else:
            s._rec(eng, lambda e: e.tensor_copy(out=o, in_=i), [in_.buf], [out.buf])

    def memset(s, out, val, eng="dve"):
        o = out.ap
        s._rec(eng, lambda e: e.memset(o, val), [], [out.buf])

    def rsum(s, out, in_, eng="dve"):
        o, i = out.ap, in_.ap
        s._rec(eng, lambda e: e.reduce_sum(out=o, in_=i, axis=AX.X), [in_.buf], [out.buf])

    def recip(s, out, in_):
        o, i = out.ap, in_.ap
        s._rec("dve", lambda e: e.reciprocal(out=o, in_=i), [in_.buf], [out.buf])

    def scan(s, out, d0, d1, init, op0=ALU.mult, op1=ALU.add):
        o, a, b, ini = out.ap, d0.ap, d1.ap, s._a(init)
        s._rec("dve", lambda e: e.tensor_tensor_scan(out=o, data0=a, data1=b, initial=ini, op0=op0, op1=op1),
               [d0.buf, d1.buf, s._b(init)], [out.buf])

    def dma(s, out, in_, q="sp", is_out=False, extra_reads=()):
        o, i = s._a(out), s._a(in_)
        s._rec(q, lambda e: e.dma_start(out=o, in_=i), [s._b(in_)] + list(extra_reads), [s._b(out)], is_dma=True,
               is_out=is_out)

    def emit(s):
        nc = s.nc
        for e in ENG_NAMES:
            for op in s.ops[e]:
                for d in op.deps:
                    if not d.is_dma:
                        if d.eng == "pe" and op.eng == "pe" and not op.is_dma:
                            continue
                        d.signal = True
        for e in ENG_NAMES:
            cnt = 0
            for op in s.ops[e]:
                if op.signal and not op.is_dma:
                    cnt += 1
                op.count = cnt
        with ExitStack() as st:
            esem = {e: st.enter_context(nc.semaphore(f"sem_{e}")) for e in ENG_NAMES}
            dsem = {e: [st.enter_context(nc.semaphore(f"dsem_{e}_{i}")) for i in range(N_DMA_SEM[e])]
                    for e in ("sp", "pool", "act")}
            block = st.enter_context(nc.Block())

            def run(ename, eng):
                waited = {}
                mine = s.ops[ename]

                def wait(key, sem, val):
                    if waited.get(key, 0) >= val:
                        return
                    eng.wait_ge(sem, val)
                    waited[key] = val

                for op in mine:
                    for d in op.deps:
                        if d.is_dma:
                            wait(("d", d.eng, d.dsem), dsem[d.eng][d.dsem], d.dval)
                        else:
                            if d.eng == "pe" and ename == "pe" and not op.is_dma:
                                continue
                            wait(("e", d.eng), esem[d.eng], d.count)
                    if op.is_dma:
                        if op.dprev > 0:
                            wait(("d", ename, op.dsem), dsem[ename][op.dsem], op.dprev)
                        ins = op.fn(eng)
                        ins.then_inc(dsem[ename][op.dsem], 16)
                    else:
                        ins = op.fn(eng)
                        if op.signal:
                            ins.then_inc(esem[ename], 1)
                if ename in ("sp", "pool", "act"):
                    n = s.dma_count[ename]
                    nds = N_DMA_SEM[ename]
                    for i in range(min(n, nds)):
                        last = 16 * ((n - 1 - i) // nds + 1)
                        wait(("d", ename, i), dsem[ename][i], last)

            @block.tensor
            def _(e):
                run("pe", e)

            @block.scalar
            def _(e):
                run("act", e)

            @block.vector
            def _(e):
                run("dve", e)

            @block.gpsimd
            def _(e):
                run("pool", e)

            @block.sync
            def _(e):
                run("sp", e)


def fm(vec):
    return np.ascontiguousarray(vec.reshape(-1, 128).T)


def const_layout(cfg):
    items = [("ident", 128), ("ones", 128), ("blockones", 128), ("f64", 64)]
    for C in cfg.CCS:
        items += [(f"mstrT{C}", 128), (f"minclT{C}", 128), (f"mstr{C}", 128)]
    items += [("hm_r", cfg.NJ * cfg.RH), ("hm_g", cfg.GQT * cfg.GH), ("hm_h", cfg.NJ * cfg.HH)]
    for C in cfg.CCS:
        items += [(f"cm_r{C}", cfg.NJ * 2), (f"cm_g{C}", cfg.GQT * 2), (f"cm_h{C}", cfg.NJ)]
    off = {}
    o = 0
    for k, n in items:
        off[k] = (o, n)
        o += n
    return off, o


def pack_consts(cfg):
    c = build_consts(cfg)
    c["hm_r"] = head_mask(cfg.NJ, 2, cfg.RH).reshape(128, -1)
    c["hm_g"] = head_mask(cfg.GQT, 2, cfg.GH).reshape(128, -1)
    c["hm_h"] = head_mask(cfg.NJ, 1, cfg.HH).reshape(128, -1)
    for C in cfg.CCS:
        c[f"cm_r{C}"] = col_mask(C, cfg.NJ, 2, cfg.RH).reshape(128, -1)
        c[f"cm_g{C}"] = col_mask(C, cfg.GQT, 2, cfg.GH).reshape(128, -1)
        c[f"cm_h{C}"] = col_mask(C, cfg.NJ, 1, cfg.HH).reshape(128, -1)
    off, n = const_layout(cfg)
    out = np.zeros((128, n), np.float32)
    for k, (o, w) in off.items():
        out[:, o:o + w] = c[k]
    return out


def pack_shared(cfg, inp):
    L, D, GW, NJ = cfg.L, cfg.D, cfg.GW, cfg.NJ
    sh = {}
    w = inp["w_in"]
    RC = 3 * GW + 256
    wt = np.zeros((L, D, cfg.NCT * 128), np.float32)
    wt[:, :, 0:RC] = w[:, :, 0:RC]
    o = RC
    wt[:, :, cfg.ct_s5 * 128: cfg.ct_s5 * 128 + GW] = w[:, :, o:o + GW]
    o += GW
    wt[:, :, cfg.ct_hq * 128: cfg.ct_hq * 128 + 4 * GW] = w[:, :, o:o + 4 * GW]
    o += 4 * GW
    GQK = GW // 2
    wt[:, :, cfg.ct_gq * 128: cfg.ct_gq * 128 + GQK] = w[:, :, o:o + GQK]
    o += GQK
    wt[:, :, cfg.ct_gk * 128: cfg.ct_gk * 128 + GQK] = w[:, :, o:o + GQK]
    o += GQK
    wt[:, :, cfg.ct_gv * 128: cfg.ct_gv * 128 + GW] = w[:, :, o:o + GW]
    o += GW
    wt[:, :, cfg.ct_glo * 128: cfg.ct_glo * 128 + 16] = w[:, :, o:o + 16]
    o += 16
    wt[:, :, cfg.ct_gg * 128: cfg.ct_gg * 128 + GW] = w[:, :, o:o + GW]
    o += GW
    assert o == w.shape[2]
    sh["w_in"] = wt
    sh["w_out"] = np.ascontiguousarray(inp["w_out"])
    sh["w_up"] = np.ascontiguousarray(inp["ffn_w_up"])
    sh["w_down"] = np.ascontiguousarray(inp["ffn_w_down"])
    vecs = np.zeros((L, 128, cfg.NVEC), np.float32)

    def put(name, arr):
        o, n = cfg.vec[name]
        for l in range(L):
            vecs[l, :, o:o + n] = fm(arr[l])
    put("norm_mix", inp["norm_mix"])
    put("norm_ffn", inp["norm_ffn"])
    mu = np.zeros((L, cfg.NR * 128), np.float32)
    mu[:, :RC] = inp["rwkv_mu"]
    put("mu", mu)
    put("w0", inp["rwkv_w0"])
    put("a0", inp["rwkv_a0"])
    put("k_k", inp["rwkv_k_k"])
    put("k_a", inp["rwkv_k_a"])
    put("r_k", inp["rwkv_r_k"].reshape(L, -1))
    put("s5_D", inp["s5_D"])
    put("s5_bglu", inp["s5_b_glu"])
    put("gk_b", inp["gla_gk_b"])
    for j in range(3):
        put(f"conv_w{j}", inp["ffn_conv_w"][:, j])
    put("conv_b", inp["ffn_conv_b"])
    put("A_re", inp["s5_A_re"].reshape(L, -1))
    put("A_im", inp["s5_A_im"].reshape(L, -1))
    put("logdt", np.repeat(inp["s5_log_dt"], 64, axis=1))
    put("hlb", inp["hgrn_lower_bounds"])
    sh["vecs"] = vecs
    sh["normf"] = fm(inp["norm_final"])
    sh["rw_wa"] = np.ascontiguousarray(np.concatenate([inp["rwkv_w_up"], inp["rwkv_a_up"]], axis=1))
    sh["rw_gup"] = np.ascontiguousarray(inp["rwkv_g_up"])
    for C in (cfg.C, TS):
        sh[f"rw_ln{C}"] = np.ascontiguousarray(np.repeat(inp["rwkv_ln"].reshape(L, cfg.RH, 1, 64), C, axis=2)
                                               .reshape(L, cfg.RH * C, 64))
    for C in cfg.CCS:
        sh[f"hg_n{C}"] = np.ascontiguousarray(np.repeat(inp["hgrn_norm"].reshape(L, cfg.HH, 1, 128), C, axis=2)
                                              .reshape(L, cfg.HH * C, 128))
        sh[f"gl_n{C}"] = np.ascontiguousarray(np.repeat(inp["gla_norm"].reshape(L, cfg.GH, 1, 128), C, axis=2)
                                              .reshape(L, cfg.GH * C, 128))
    sh["gk_up"] = np.ascontiguousarray(inp["gla_gk_up"])
    sh["s5_Bt_re"] = np.ascontiguousarray(inp["s5_B_re"].transpose(0, 1, 3, 2))
    sh["s5_Bt_im"] = np.ascontiguousarray(inp["s5_B_im"].transpose(0, 1, 3, 2))
    sh["s5_Ct_re"] = np.ascontiguousarray(inp["s5_C_re"].transpose(0, 1, 3, 2))
    sh["s5_Ct_im"] = np.ascontiguousarray(inp["s5_C_im"].transpose(0, 1, 3, 2))
    sh["s5_wglu"] = np.ascontiguousarray(inp["s5_w_glu"])
    sh["consts"] = pack_consts(cfg)
    ntile = len(cfg.tiles)
    sm = np.ones((ntile, 128, cfg.TMAX), np.float32)
    pos = np.zeros((ntile, 128, cfg.TMAX), np.float32)
    cmk = np.ones((ntile, 128, cfg.TMAX), np.float32)
    cmk2 = np.ones((ntile, 128, cfg.TMAX), np.float32)
    for i, (p0, pn, ns) in enumerate(cfg.tiles):
        sm[i, :, 0] = 0
        pos[i, :, :pn] = np.arange(1, pn + 1)
        cmk[i, :, 0:pn:cfg.C] = 0
        cmk2[i, :, 0:pn:cfg.CL] = 0
        for q in range(ns):
            sm[i, :, pn + q * TS] = 0
            pos[i, :, pn + q * TS: pn + (q + 1) * TS] = np.arange(1, TS + 1)
            cmk[i, :, pn + q * TS] = 0
            cmk2[i, :, pn + q * TS] = 0
    sh["tmask"] = np.ascontiguousarray(np.stack([sm, pos, cmk, cmk2], axis=1))
    return sh


def pack_core(cfg, inp, b, s0):
    L, NS = cfg.L, cfg.NS
    d = {}
    xs = inp["x_sample"][s0:s0 + NS].reshape(NS * TS, cfg.D)
    xall = np.concatenate([inp["meta_tokens"], inp["x_prompt"][b], xs], axis=0)
    d["xin"] = np.ascontiguousarray(xall.T)
    sl = slice(s0, s0 + NS)
    d["st_rwkv"] = np.ascontiguousarray(inp["state_rwkv"][:, sl].transpose(0, 1, 2, 4, 3).reshape(L, NS, cfg.GW, 64))
    sft = np.zeros((L, NS, cfg.NR * 128), np.float32)
    sft[:, :, :3 * cfg.GW + 256] = inp["state_rwkv_shift"][:, sl]
    d["st_shift"] = np.ascontiguousarray(sft.reshape(L, NS, cfg.NR, 128).transpose(0, 3, 2, 1))
    for nm, key in (("st_s5re", "state_s5_re"), ("st_s5im", "state_s5_im")):
        a = inp[key][:, sl].reshape(L, NS, cfg.S5C, 128)
        d[nm] = np.ascontiguousarray(a.transpose(0, 3, 2, 1))
    d["st_hgrn"] = np.ascontiguousarray(inp["state_hgrn"][:, sl].reshape(L, NS, cfg.GW, 128))
    d["st_gla"] = np.ascontiguousarray(inp["state_gla"][:, sl].reshape(L, NS, cfg.GW // 2, 128))
    a = inp["state_ffn_conv"][:, sl].reshape(L, NS, 2, cfg.NFT, 128)
    d["st_conv"] = np.ascontiguousarray(a.transpose(0, 4, 3, 1, 2))
    return d


OUT_SPECS = lambda cfg: {
    "yT": [cfg.D, cfg.NTOK],
    "o_rwkv_p": [cfg.L, cfg.GW, 64], "o_rwkv_s": [cfg.L, cfg.NS, cfg.GW, 64],
    "o_shift_p": [cfg.L, 128, cfg.NR], "o_shift_s": [cfg.L, 128, cfg.NR, cfg.NS],
    "o_s5re_p": [cfg.L, 128, cfg.S5C], "o_s5re_s": [cfg.L, 128, cfg.S5C, cfg.NS],
    "o_s5im_p": [cfg.L, 128, cfg.S5C], "o_s5im_s": [cfg.L, 128, cfg.S5C, cfg.NS],
    "o_hgrn_p": [cfg.L, cfg.GW, 128], "o_hgrn_s": [cfg.L, cfg.NS, cfg.GW, 128],
    "o_gla_p": [cfg.L, cfg.GW // 2, 128], "o_gla_s": [cfg.L, cfg.NS, cfg.GW // 2, 128],
    "o_conv_p": [cfg.L, 128, cfg.NFT, 2], "o_conv_s": [cfg.L, 128, cfg.NFT, cfg.NS, 2],
}


def unpack_core(cfg, r):
    L, NS, GW = cfg.L, cfg.NS, cfg.GW
    o = {}
    yT = r["yT"]
    o["y_p"] = np.ascontiguousarray(yT[:, N_META:cfg.NP].T)
    o["y_s"] = np.ascontiguousarray(yT[:, cfg.NP:].T).reshape(NS, TS, cfg.D)
    o["rwkv_p"] = r["o_rwkv_p"].reshape(L, cfg.RH, 64, 64).transpose(0, 1, 3, 2)
    o["rwkv_s"] = r["o_rwkv_s"].reshape(L, NS, cfg.RH, 64, 64).transpose(0, 1, 2, 4, 3)
    RC = 3 * GW + 256
    o["shift_p"] = r["o_shift_p"].transpose(0, 2, 1).reshape(L, -1)[:, :RC]
    o["shift_s"] = r["o_shift_s"].transpose(0, 3, 2, 1).reshape(L, NS, -1)[:, :, :RC]
    for k in ("s5re", "s5im"):
        o[k + "_p"] = r[f"o_{k}_p"].transpose(0, 2, 1).reshape(L, GW // 16, 64)
        o[k + "_s"] = r[f"o_{k}_s"].transpose(0, 3, 2, 1).reshape(L, NS, GW // 16, 64)
    o["hgrn_p"] = r["o_hgrn_p"].reshape(L, cfg.HH, 128, 128)
    o["hgrn_s"] = r["o_hgrn_s"].reshape(L, NS, cfg.HH, 128, 128)
    o["gla_p"] = r["o_gla_p"].reshape(L, cfg.GH, 64, 128)
    o["gla_s"] = r["o_gla_s"].reshape(L, NS, cfg.GH, 64, 128)
    o["conv_p"] = r["o_conv_p"].transpose(0, 3, 2, 1).reshape(L, 2, cfg.DFF)
    o["conv_s"] = r["o_conv_s"].transpose(0, 3, 4, 2, 1).reshape(L, NS, 2, cfg.DFF)
    return o


def weight_blocks(cfg):
    blocks = []
    for (c0, n) in cfg.mix_groups:
        for g0 in range(0, n, 4):
            blocks.append(("win", c0 + g0, min(4, n - g0)))
    for og in range(cfg.D // 256):
        blocks.append(("wout", og))
    for j0 in range(0, cfg.NFT, 2):
        blocks.append(("wup", j0, min(2, cfg.NFT - j0)))
    for og in range(cfg.D // 512):
        for k0 in range(0, cfg.NFT, 16):
            blocks.append(("wdown", og, k0, min(16, cfg.NFT - k0)))
    return blocks


NSLOT = 3
SLOT_ELEMS = 8192


class WStream:
    def __init__(s, P, cfg, di, order, deps):
        s.P, s.cfg, s.di, s.deps = P, cfg, di, deps
        s.slots = [P.alloc(f"wslot{i}", [SLOT_ELEMS], BF16) for i in range(NSLOT)]
        per = weight_blocks(cfg)
        s.specs = [(l, b) for (_, l) in order for b in per]
        s.issued = 0
        s.next = 0

    def views(s, slot, spec):
        cfg = s.cfg
        l, b = spec
        k = b[0]
        sv = slot.v()
        if k == "win":
            n = b[2] * 128
            return [sv[:, 0:cfg.NDC * n].re("p (c n) -> p c n", n=n)]
        if k == "wout":
            a = sv[:, 0:cfg.RH * 256].re("p (h n) -> p h n", n=256)
            bb = sv[:, cfg.RH * 256:(cfg.RH + 3 * cfg.NJ) * 256].re("p (c n) -> p c n", n=256)
            return [a, bb]
        if k == "wup":
            n = b[2] * 128
            return [sv[:, 0:cfg.NDC * 2 * n].re("p (c g n) -> p c g n", g=2, n=n)]
        if k == "wdown":
            return [sv[:, 0:b[3] * 512].re("p (c n) -> p c n", n=512)]

    def _load(s, i):
        cfg, di, P = s.cfg, s.di, s.P
        slot = s.slots[i % NSLOT]
        l, b = s.specs[i]
        vs = s.views(slot, s.specs[i])
        k = b[0]
        if k == "win":
            c0, n = b[1] * 128, b[2] * 128
            P.dma(vs[0], di["w_in"][l, :, c0:c0 + n].rearrange("(c p) n -> p c n", p=128), q="sp",
                  extra_reads=s.deps[("w_in", l)])
        elif k == "wout":
            c0 = b[1] * 256
            P.dma(vs[0][0:64], di["w_out"][l, 0:cfg.GW, c0:c0 + 256].rearrange("(h k) n -> k h n", k=64), q="sp",
                  extra_reads=s.deps[("w_out", l)])
            P.dma(vs[1], di["w_out"][l, cfg.GW:4 * cfg.GW, c0:c0 + 256].rearrange("(c p) n -> p c n", p=128),
                  q="sp", extra_reads=s.deps[("w_out", l)])
        elif k == "wup":
            c0, n = b[1] * 128, b[2] * 128
            for g in range(2):
                P.dma(vs[0][:, :, g, :],
                      di["w_up"][l, :, g * cfg.DFF + c0:g * cfg.DFF + c0 + n].rearrange("(c p) n -> p c n", p=128),
                      q="sp", extra_reads=s.deps[("w_up", l)])
        elif k == "wdown":
            og, k0, nk = b[1], b[2], b[3]
            P.dma(vs[0], di["w_down"][l, k0 * 128:(k0 + nk) * 128, og * 512:(og + 1) * 512]
                  .rearrange("(c p) n -> p c n", p=128), q="sp", extra_reads=s.deps[("w_down", l)])

    def get(s, kind):
        i = s.next
        s.next += 1
        while s.issued < min(len(s.specs), i + NSLOT):
            s._load(s.issued)
            s.issued += 1
        l, b = s.specs[i]
        assert b[0] == kind, (b, kind)
        return s.views(s.slots[i % NSLOT], s.specs[i]), b


def la_chunk(P, G, S, Cc, qv, kv, vv, gamma, H, av=None, bv=None, prodv=None):
    NH, NJ, NJV, DK, DV, HPT = G["NH"], G["NJ"], G["NJV"], G["DK"], G["DV"], G["HPT"]
    R = NH * Cc
    delta = av is not None
    XX, VX, HS = S["XX"], S["VX"], S["HS"]
    ident = G["ident"]
    mstrT, minclT, mstr = G[f"mstrT{Cc}"], G[f"minclT{Cc}"], G[f"mstr{Cc}"]
    hm, hmv = G["hm"], G["hmv"]

    def expand(x, src):
        out = XX[:, :, x, 0:R].re("p j (h t) -> p j h t", t=Cc)
        P.tt(out, src.us(2).bc([128, NJ, NH, Cc]), hm.us(3).bc([128, NJ, NH, Cc]), ALU.mult)

    if delta:
        expand(0, bv)
        expand(2, av)
        if prodv is not None:
            expand(4, prodv)
    expand(1, qv)
    expand(3, kv)
    outv = VX[:, :, 0:R].re("p j (h t) -> p j h t", t=Cc)
    P.tt(outv, vv.us(2).bc([128, NJV, NH, Cc]), hmv.us(3).bc([128, NJV, NH, Cc]), ALU.mult)
    psf = P.ps()
    for j in range(NJ):
        P.mm(psf[0:R, 0:DK], XX[:, j, 3, 0:R], G["fold_k"], start=(j == 0), stop=(j == NJ - 1))
    if delta:
        for j in range(NJ):
            P.mm(psf[0:R, DK:2 * DK], XX[:, j, 2, 0:R], G["fold_k"], start=(j == 0), stop=(j == NJ - 1))
    for j in range(NJV):
        P.mm(psf[0:R, 2 * DK:2 * DK + DV], VX[:, j, 0:R], G["fold_v"], start=(j == 0), stop=(j == NJV - 1))
    if delta:
        P.copy(HS[0:R, 0:2 * DK + DV], psf[0:R, 0:2 * DK + DV], eng="act")
    else:
        P.copy(HS[0:R, 0:DK], psf[0:R, 0:DK], eng="act")
        P.copy(HS[0:R, 2 * DK:2 * DK + DV], psf[0:R, 2 * DK:2 * DK + DV], eng="act")
    K_hs, A_hs, V_hs = HS[0:R, 0:DK], HS[0:R, DK:2 * DK], HS[0:R, 2 * DK:2 * DK + DV]
    psK = P.ps()
    RKT, NT, MT, RAT, Mm = S["RKT"], S["NT"], S["MT"], S["RAT"], S["Mm"]
    if delta:
        pk = psK[0:R, 0:256].re("p (x r) -> p x r", x=2)[:, :, 0:R]
        if R == 128:
            for j in range(NJ):
                P.mm(psK[:, 0:256], XX[:, j, 3, :], XX[:, j, 0:2, :].re("p x r -> p (x r)"), start=(j == 0),
                     stop=(j == NJ - 1))
        else:
            for xi in range(2):
                for j in range(NJ):
                    P.mm(pk[:, xi, :], XX[:, j, 3, 0:R], XX[:, j, xi, 0:R], start=(j == 0), stop=(j == NJ - 1))
        P.tt(NT[0:R, 0:R], pk[:, 0, :], mstrT[0:R, 0:R], ALU.mult)
        P.tt(RKT[0:R, 0:R], pk[:, 1, :], minclT[0:R, 0:R], ALU.mult)
        psA = P.ps()
        pa = psA[0:R, 0:256].re("p (x r) -> p x r", x=2)[:, :, 0:R]
        if R == 128:
            for j in range(NJ):
                P.mm(psA[:, 0:256], XX[:, j, 2, :], XX[:, j, 0:2, :].re("p x r -> p (x r)"), start=(j == 0),
                     stop=(j == NJ - 1))
        else:
            for xi in range(2):
                for j in range(NJ):
                    P.mm(pa[:, xi, :], XX[:, j, 2, 0:R], XX[:, j, xi, 0:R], start=(j == 0), stop=(j == NJ - 1))
        P.tt(MT[0:R, 0:R], pa[:, 0, :], mstrT[0:R, 0:R], ALU.mult)
        P.tt(RAT[0:R, 0:R], pa[:, 1, :], minclT[0:R, 0:R], ALU.mult)
        psM = P.ps()
        for j in range(NJ):
            P.mm(psM[0:R, 0:R], XX[:, j, 0, 0:R], XX[:, j, 2, 0:R], start=(j == 0), stop=(j == NJ - 1))
        P.tt(Mm[0:R, 0:R], psM[0:R, 0:R], mstr[0:R, 0:R], ALU.mult)
        levels = int(round(math.log2(Cc))) - 1
        cur, curT = Mm, MT
        pows = []
        for lev in range(levels):
            p1 = P.ps()
            P.mm(p1[0:R, 0:R], curT[0:R, 0:R], cur[0:R, 0:R])
            P.copy(S["Mp"][lev][0:R, 0:R], p1[0:R, 0:R], eng="act")
            if lev < levels - 1:
                p2 = P.ps()
                P.mm(p2[0:R, 0:R], cur[0:R, 0:R], curT[0:R, 0:R])
                P.copy(S["MpT"][lev][0:R, 0:R], p2[0:R, 0:R], eng="act")
            pows.append(S["Mp"][lev])
            cur, curT = S["Mp"][lev], S["MpT"][lev]
        Z = S["Z"][0]
        P.tt(Z[0:R, 0:R], ident[0:R, 0:R], MT[0:R, 0:R], ALU.subtract)
        for lev in range(levels):
            pz = P.ps()
            P.mm(pz[0:R, 0:R], ident[0:R, 0:R], Z[0:R, 0:R], start=True, stop=False)
            P.mm(pz[0:R, 0:R], pows[lev][0:R, 0:R], Z[0:R, 0:R], start=False, stop=True)
            Zn = S["Z"][(lev + 1) % 2]
            P.copy(Zn[0:R, 0:R], pz[0:R, 0:R], eng="act")
            Z = Zn
        TinvT = Z
        pn_ = P.ps()
        P.mm(pn_[0:R, 0:DV], NT[0:R, 0:R], V_hs)
        P.copy(S["NV"][0:R, 0:DV], pn_[0:R, 0:DV], eng="act")
        pw = P.ps()
        for j in range(NJ):
            P.mm(pw[0:R, 0:DV], XX[:, j, 0, 0:R], H[:, j, :], start=(j == 0), stop=(j == NJ - 1))
        P.tt(S["W"][0:R, 0:DV], pw[0:R, 0:DV], S["NV"][0:R, 0:DV], ALU.add)
        pu = P.ps()
        P.mm(pu[0:R, 0:DV], TinvT[0:R, 0:R], S["W"][0:R, 0:DV])
        P.act(S["nU"][0:R, 0:DV], pu[0:R, 0:DV], AF.Copy, scale=-1.0)
        nU = S["nU"][0:R, 0:DV]
    else:
        for j in range(NJ):
            P.mm(psK[0:R, 0:R], XX[:, j, 3, 0:R], XX[:, j, 1, 0:R], start=(j == 0), stop=(j == NJ - 1))
        P.tt(RKT[0:R, 0:R], psK[0:R, 0:R], minclT[0:R, 0:R], ALU.mult)
    psY = P.ps()
    for j in range(NJ):
        P.mm(psY[0:R, 0:DV], XX[:, j, 1, 0:R], H[:, j, :], start=(j == 0), stop=False)
    if delta:
        P.mm(psY[0:R, 0:DV], RAT[0:R, 0:R], nU, start=False, stop=False)
    P.mm(psY[0:R, 0:DV], RKT[0:R, 0:R], V_hs, start=False, stop=True)
    psS = None
    if prodv is not None:
        psS = P.ps()
        for j in range(NJ):
            P.mm(psS[0:R, 0:1], XX[:, j, 4, 0:R], G["ones"][:, 0:1], start=(j == 0), stop=(j == NJ - 1))
    cm = G[f"cm{Cc}"]
    K2, A2 = S["K2"], S["A2"]
    P.tt(K2[0:R], K_hs.us(1).us(1).bc([R, NJ, HPT, DK]), cm[0:R].us(3).bc([R, NJ, HPT, DK]), ALU.mult)
    if delta:
        P.tt(A2[0:R], A_hs.us(1).us(1).bc([R, NJ, HPT, DK]), cm[0:R].us(3).bc([R, NJ, HPT, DK]), ALU.mult)
    psH = P.ps()
    for j in range(NJ):
        o = psH[:, j * DV:(j + 1) * DV]
        if delta:
            P.mm(o, A2[0:R, j].re("p h k -> p (h k)"), nU, start=True, stop=False)
        P.mm(o, K2[0:R, j].re("p h k -> p (h k)"), V_hs, start=(not delta), stop=True)
    P.tt(S["tH"].v(), psH[:, 0:NJ * DV].re("p (j v) -> p j v", v=DV), H.v(), ALU.add)
    P.tt(H.v(), S["tH"].v(), gamma.us(2).bc([128, NJ, DV]), ALU.mult)
    return psY[0:R, 0:DV], V_hs, psS


MAGIC = 12582912.0
TWO_PI = 2.0 * math.pi


def build_program(cfg, debug=False):
    nc = bass.Bass("TRN2", target_bir_lowering=False)
    L, D, GW, NJ, NDC, NR, RH, HH, GH, GQT, S5C, NFT, NS, C = (cfg.L, cfg.D, cfg.GW, cfg.NJ, cfg.NDC, cfg.NR,
                                                             cfg.RH, cfg.HH, cfg.GH, cfg.GQT, cfg.S5C, cfg.NFT,
                                                             cfg.NS, cfg.C)
    TMAX = cfg.TMAX
    di = {}

    def din(name, shape):
        di[name] = nc.dram_tensor(name, [int(x) for x in shape], F32, kind="ExternalInput").ap()

    coff, NCONST = const_layout(cfg)
    ntile = len(cfg.tiles)
    din("xin", [D, cfg.NTOK])
    din("w_in", [L, D, cfg.NCT * 128])
    din("w_out", [L, D, D])
    din("w_up", [L, D, 2 * cfg.DFF])
    din("w_down", [L, cfg.DFF, D])
    din("vecs", [L, 128, cfg.NVEC])
    din("normf", [128, NDC])
    din("rw_wa", [L, 128, GW])
    din("rw_gup", [L, 128, GW])
    for Cc in (C, TS):
        din(f"rw_ln{Cc}", [L, RH * Cc, 64])
    for Cc in cfg.CCS:
        din(f"hg_n{Cc}", [L, HH * Cc, 128])
        din(f"gl_n{Cc}", [L, GH * Cc, 128])
    din("gk_up", [L, 16, GW // 2])
    G5 = GW // 16
    din("s5_Bt_re", [L, G5, 16, 64])
    din("s5_Bt_im", [L, G5, 16, 64])
    din("s5_Ct_re", [L, G5, 64, 16])
    din("s5_Ct_im", [L, G5, 64, 16])
    din("s5_wglu", [L, GW, GW])
    din("consts", [128, NCONST])
    din("tmask", [ntile, 4, 128, TMAX])
    din("st_rwkv", [L, NS, GW, 64])
    din("st_shift", [L, 128, NR, NS])
    din("st_s5re", [L, 128, S5C, NS])
    din("st_s5im", [L, 128, S5C, NS])
    din("st_hgrn", [L, NS, GW, 128])
    din("st_gla", [L, NS, GW // 2, 128])
    din("st_conv", [L, 128, NFT, NS, 2])
    do = {}
    for k, shp in OUT_SPECS(cfg).items():
        do[k] = nc.dram_tensor(k, [int(x) for x in shp], F32, kind="ExternalOutput").ap()

    words = nc.sbuf_bytes_remaining // 4 - 64
    P = Prog(nc, words)
    consts = P.alloc("consts", [NCONST])
    cv = lambda k: consts[:, coff[k][0]:coff[k][0] + coff[k][1]]
    ident, ones_f, blockones, f64 = cv("ident"), cv("ones"), cv("blockones"), cv("f64")
    ones_bf = P.alloc("ones_bf", [128], BF16)
    vecs = P.alloc("vecs", [L, cfg.NVEC])
    normf = P.alloc("normf", [NDC])
    lb = P.alloc("lb", [L, NJ])
    oml = P.alloc("oml", [L, NJ])
    tmask = P.alloc("tmask", [4, TMAX])
    x = P.alloc("x", [NDC, TMAX])
    h = P.alloc("h", [NDC, TMAX], BF16)
    Hr = [P.alloc(f"Hr{l}", [NJ, 64]) for l in range(L)]
    Hh = [P.alloc(f"Hh{l}", [NJ, 128]) for l in range(L)]
    Hg = [P.alloc(f"Hg{l}", [GQT, 128]) for l in range(L)]
    s5c = [[P.alloc(f"s5c{l}{i}", [S5C]) for i in range(2)] for l in range(L)]
    convc = [P.alloc(f"convc{l}", [NFT, 2]) for l in range(L)]
    shiftc = [P.alloc(f"shiftc{l}", [NR]) for l in range(L)]
    order = [(ti, l) for ti in range(ntile) for l in range(L)]
    wsc = {}
    wdeps = {}
    conv_list = []
    for name, rows, cols, npiece in (("w_in", D, cfg.NCT * 128, 4), ("w_out", D, D, 2), ("w_up", D, 2 * cfg.DFF, 4),
                                     ("w_down", cfg.DFF, D, 2)):
        wsc[name] = nc.dram_tensor(name + "_bf", [L, rows, cols], BF16, kind="Internal").ap()
        for l in range(L):
            wdeps[(name, l)] = []
            step = (rows + npiece - 1) // npiece
            for pi, r0 in enumerate(range(0, rows, step)):
                r1 = min(rows, r0 + step)
                db = Buf(f"{name}{l}_{pi}", None, [("dram", name, l, pi)], "dram")
                wdeps[(name, l)].append(db)
                conv_list.append((l, V(wsc[name][l, r0:r1, :], db), di[name][l, r0:r1, :]))
    WS = WStream(P, cfg, wsc, order, wdeps)
    phase_base = P.off

    def vc(l, name, j=None):
        o, n = cfg.vec[name]
        if j is None:
            return vecs[:, l, o:o + n]
        return vecs[:, l, o + j:o + j + 1]

    def geom(NH, NJk, HPT, NJV, DV, hmk, hmv, cmk):
        g = {"NH": NH, "NJ": NJk, "NJV": NJV, "DK": 128 // HPT, "DV": DV, "HPT": HPT, "ident": ident,
             "ones": ones_f,
             "hm": cv(hmk).re("p (j h) -> p j h", h=NH), "hmv": cv(hmv).re("p (j h) -> p j h", h=NH),
             "fold_k": f64 if HPT == 2 else ident, "fold_v": f64 if DV == 64 else ident}
        for Cc in cfg.CCS:
            g[f"mstrT{Cc}"], g[f"minclT{Cc}"], g[f"mstr{Cc}"] = cv(f"mstrT{Cc}"), cv(f"minclT{Cc}"), cv(f"mstr{Cc}")
            g[f"cm{Cc}"] = cv(f"{cmk}{Cc}").re("p (j h) -> p j h", h=HPT)
        return g
    G_r = geom(RH, NJ, 2, NJ, 64, "hm_r", "hm_r", "cm_r")
    G_g = geom(GH, GQT, 2, NJ, 128, "hm_g", "hm_h", "cm_g")
    G_h = geom(HH, NJ, 1, NJ, 128, "hm_h", "hm_h", "cm_h")

    def la_scratch(G, delta):
        NJk, NJV, DK, DV, HPT = G["NJ"], G["NJV"], G["DK"], G["DV"], G["HPT"]
        S = {"XX": P.alloc("XX", [NJk, 5 if delta else 4, 128]), "VX": P.alloc("VX", [NJV, 128]),
             "HS": P.alloc("HS", [2 * DK + DV]), "RKT": P.alloc("RKT", [128]),
             "K2": P.alloc("K2", [NJk, HPT, DK]), "tH": P.alloc("tH", [NJk, DV]), "NT": None, "MT": None, "RAT": None, "Mm": None, "A2": None}
        if delta:
            for k in ("NT", "MT", "RAT", "Mm"):
                S[k] = P.alloc(k, [128])
            S["A2"] = P.alloc("A2", [NJk, HPT, DK])
            S["Mp"] = [P.alloc(f"Mp{i}", [128]) for i in range(3)]
            S["MpT"] = [P.alloc(f"MpT{i}", [128]) for i in range(3)]
            S["Z"] = [P.alloc(f"Z{i}", [128]) for i in range(2)]
            S["NV"], S["W"], S["nU"] = P.alloc("NV", [DV]), P.alloc("W", [DV]), P.alloc("nU", [DV])
        return S

    P.dma(consts.v(), di["consts"][:, :])
    P.dma(vecs.v(), di["vecs"].rearrange("l p n -> p l n"))
    P.dma(normf.v(), di["normf"][:, :])
    P.memset(ones_bf.v(), 1.0)
    for (l_, o_, i_) in sorted(conv_list, key=lambda t: t[0]):
        P.dma(o_, i_, q="pool")
    for l in range(L):
        for b in (Hr[l], Hh[l], Hg[l], s5c[l][0], s5c[l][1], convc[l], shiftc[l]):
            P.memset(b.v(), 0.0)
    P.off = phase_base
    ee = P.alloc("lb_e", [L, NJ])
    se = P.alloc("lb_s", [NJ])
    for l in range(L):
        P.act(ee[:, l, :], vc(l, "hlb"), AF.Exp)
    P.copy(se.v(), ee[:, 0, :])
    for l in range(1, L):
        P.tt(se.v(), se.v(), ee[:, l, :], ALU.add)
    P.recip(se.v(), se.v())
    P.memset(lb[:, 0, :], 0.0)
    for l in range(1, L):
        P.tt(ee[:, l, :], ee[:, l, :], se.v(), ALU.mult)
        P.tt(lb[:, l, :], lb[:, l - 1, :], ee[:, l, :], ALU.add)
    P.ts(oml.v(), lb.v(), -1.0, 1.0, ALU.mult, ALU.add)

    def rmsnorm(T, gname, l, out_bf, gvec=None):
        sq = [P.alloc(f"nsq{i}", [TMAX], BF16) for i in range(2)]
        rstd = P.alloc("rstd", [TMAX])
        ps = P.ps()
        for c in range(NDC):
            s_ = sq[c % 2]
            P.act(s_[:, 0:T], x[:, c, 0:T], AF.Square)
            P.mm(ps[:, 0:T], ones_bf.v(), s_[:, 0:T], start=(c == 0), stop=(c == NDC - 1))
        P.act(rstd[:, 0:T], ps[:, 0:T], AF.Sqrt, bias=EPS, scale=1.0 / D)
        P.recip(rstd[:, 0:T], rstd[:, 0:T])
        for c in range(NDC):
            g = gvec[:, c:c + 1] if gvec is not None else vc(l, gname, c)
            P.stt(out_bf[:, c, 0:T], x[:, c, 0:T], g, rstd[:, 0:T], ALU.mult, ALU.mult)

    def project(T, ct0, nct, pt):
        done = 0
        while done < nct:
            (wv,), b = WS.get("win")
            assert b[1] == ct0 + done
            for i in range(b[2]):
                ps = P.ps()
                for c in range(NDC):
                    P.mm(ps[:, 0:T], wv[:, c, i * 128:(i + 1) * 128], h[:, c, 0:T], start=(c == 0),
                         stop=(c == NDC - 1))
                P.copy(pt[:, done + i, 0:T], ps[:, 0:T], eng="act")
            done += b[2]

    def chunks_of(pn, ns, CC=None):
        CC = CC or C
        ch = [("p", c0, min(CC, pn - c0), None) for c0 in range(0, pn, CC)]
        ch += [("s", pn + q * TS, TS, q) for q in range(ns)]
        return ch

    def range_reduce(out, in_, tmp):
        P.ts(tmp, in_, 1.0 / TWO_PI, MAGIC, ALU.mult, ALU.add)
        P.ts(tmp, tmp, -MAGIC, None, ALU.add)
        P.stt(out, tmp, -TWO_PI, in_, ALU.mult, ALU.add)

    cmk = lambda T: tmask[:, 2, 0:T]
    cmk2 = lambda T: tmask[:, 3, 0:T]
    smk = lambda T: tmask[:, 0, 0:T]
    posv = lambda T: tmask[:, 1, 0:T]

    def rwkv(ti, l, T, pn, ns, yTr):
        last = ti == ntile - 1
        P.off = mix_base
        pt = P.alloc("pt_r", [NR, TMAX])
        project(T, 0, NR, pt)
        wa = P.alloc("rw_wa", [GW])
        gup = P.alloc("rw_gup", [GW])
        ln = {C: P.alloc("rw_ln", [64])}
        P.dma(wa.v(), di["rw_wa"][l])
        P.dma(gup.v(), di["rw_gup"][l])
        P.dma(ln[C][0:RH * C], di[f"rw_ln{C}"][l])
        if ns:
            ln[TS] = P.alloc("rw_ln4", [64])
            P.dma(ln[TS][0:RH * TS], di[f"rw_ln{TS}"][l])
            stsh = P.alloc("stsh", [NR, NS])
            osh = P.alloc("osh", [NR, NS])
            P.dma(stsh.v(), di["st_shift"][l])
        d = P.alloc("rw_d", [TMAX])
        for ct in range(NR):
            p = pt[:, ct, :]
            if pn > 1:
                P.tt(d[:, 1:pn], p[:, 0:pn - 1], p[:, 1:pn], ALU.subtract)
            P.tt(d[:, 0:1], shiftc[l][:, ct:ct + 1], p[:, 0:1], ALU.subtract)
            if ns:
                pv = p[:, pn:T].re("p (s t) -> p s t", t=TS)
                dv = d[:, pn:T].re("p (s t) -> p s t", t=TS)
                P.tt(dv[:, :, 1:TS], pv[:, :, 0:TS - 1], pv[:, :, 1:TS], ALU.subtract)
                P.tt(dv[:, :, 0:1], stsh[:, ct, :].us(2), pv[:, :, 0:1], ALU.subtract)
                P.copy(osh[:, ct, :].us(2), pv[:, :, TS - 1:TS])
            P.copy(shiftc[l][:, ct:ct + 1], p[:, pn - 1:pn])
            P.stt(p[:, 0:T], d[:, 0:T], vc(l, "mu", ct), p[:, 0:T], ALU.mult, ALU.add)
        if ns:
            P.dma(do["o_shift_s"][l], osh.v(), is_out=True)
        if last:
            P.dma(do["o_shift_p"][l], shiftc[l].v(), is_out=True)
        rT = pt[:, cfg.ct_r:cfg.ct_r + NJ, :]
        kT = pt[:, cfg.ct_k:cfg.ct_k + NJ, :]
        vT = pt[:, cfg.ct_v:cfg.ct_v + NJ, :]
        pwa = pt[:, cfg.ct_wa, :]
        B = [P.alloc(f"rwB{i}", [NJ, TMAX]) for i in range(6)]
        tw = P.alloc("rw_tw", [TMAX])
        sgl = P.alloc("rw_sgl", [TMAX])
        gT = P.alloc("rw_gT", [RH, TMAX])
        P.act(tw[0:64, 0:T], pwa[0:64, 0:T], AF.Tanh)
        P.act(sgl[:, 0:T], pt[:, cfg.ct_gl, 0:T], AF.Sigmoid)
        sg, cs, t3, a_, kk, ka = B
        for j in range(NJ):
            ps = P.ps()
            P.mm(ps[:, 0:T], wa[0:64, j * 128:(j + 1) * 128], tw[0:64, 0:T])
            P.act(sg[:, j, 0:T], ps[:, 0:T], AF.Sigmoid, bias=vc(l, "w0", j))
            ps2 = P.ps()
            P.mm(ps2[:, 0:T], wa[64:128, j * 128:(j + 1) * 128], pwa[64:128, 0:T])
            P.act(a_[:, j, 0:T], ps2[:, 0:T], AF.Sigmoid, bias=vc(l, "a0", j))
            P.scan(cs[:, j, 0:T], cmk(T), sg[:, j, 0:T], 0.0)
        for hd in range(RH):
            ps = P.ps()
            P.mm(ps[0:64, 0:T], gup[:, hd * 64:(hd + 1) * 64], sgl[:, 0:T])
            P.copy(gT[0:64, hd, 0:T], ps[0:64, 0:T], eng="act")
        P.tt(t3[:, :, 0:T], cs[:, :, 0:T], sg[:, :, 0:T], ALU.subtract)
        P.act(t3[:, :, 0:T], t3[:, :, 0:T], AF.Exp, scale=-RWKV_DECAY_SCALE)
        eb, enb = sg, P.alloc("rw_enb", [NJ, TMAX])
        P.act(eb[:, :, 0:T], cs[:, :, 0:T], AF.Exp, scale=-RWKV_DECAY_SCALE)
        P.act(enb[:, :, 0:T], cs[:, :, 0:T], AF.Exp, scale=RWKV_DECAY_SCALE)
        prod = cs
        sq = P.alloc("rw_sq", [TMAX])
        nrm = P.alloc("rw_nrm", [TMAX])
        for j in range(NJ):
            P.ts(kk[:, j, 0:T], kT[:, j, 0:T], vc(l, "k_k", j), None, ALU.mult)
            P.act(sq[:, 0:T], kk[:, j, 0:T], AF.Square)
            ps = P.ps()
            P.mm(ps[:, 0:T], blockones, sq[:, 0:T])
            P.act(nrm[:, 0:T], ps[:, 0:T], AF.Sqrt)
            P.ts(nrm[:, 0:T], nrm[:, 0:T], 1e-12, None, ALU.max)
            P.recip(nrm[:, 0:T], nrm[:, 0:T])
            P.tt(kk[:, j, 0:T], kk[:, j, 0:T], nrm[:, 0:T], ALU.mult)
            P.ts(sq[:, 0:T], a_[:, j, 0:T], -1.0, vc(l, "k_a", j), ALU.add, ALU.mult)
            P.stt(kT[:, j, 0:T], sq[:, 0:T], 1.0, kT[:, j, 0:T], ALU.add, ALU.mult)
            P.tt(ka[:, j, 0:T], kk[:, j, 0:T], a_[:, j, 0:T], ALU.mult)
            P.stt(prod[:, j, 0:T], rT[:, j, 0:T], vc(l, "r_k", j), kT[:, j, 0:T], ALU.mult, ALU.mult)
        P.tt(ka[:, :, 0:T], ka[:, :, 0:T], enb[:, :, 0:T], ALU.mult)
        P.tt(kT[:, :, 0:T], kT[:, :, 0:T], enb[:, :, 0:T], ALU.mult)
        P.tt(kk[:, :, 0:T], kk[:, :, 0:T], t3[:, :, 0:T], ALU.mult)
        P.tt(rT[:, :, 0:T], rT[:, :, 0:T], eb[:, :, 0:T], ALU.mult)
        S = la_scratch(G_r, True)
        ysb = P.alloc("rw_y", [64])
        yc = P.alloc("rw_yc", [64])
        st1 = P.alloc("rw_st", [8])
        Hs = [P.alloc(f"rw_Hs{i}", [NJ, 64]) for i in range(2)]
        for (kind, c0, Cc, q) in chunks_of(pn, ns):
            R = RH * Cc
            if kind == "p":
                Hst = Hr[l]
            else:
                Hst = Hs[q % 2]
                P.dma(Hst.v(), di["st_rwkv"][l, q].rearrange("(j p) v -> p j v", p=128))
            sl = slice(c0, c0 + Cc)
            psY, V_hs, psS = la_chunk(P, G_r, S, Cc, rT[:, :, sl], kT[:, :, sl], vT[:, :, sl],
                                      eb[:, :, c0 + Cc - 1], Hst, av=ka[:, :, sl], bv=kk[:, :, sl],
                                      prodv=prod[:, :, sl])
            if kind == "s":
                P.dma(do["o_rwkv_s"][l, q].rearrange("(j p) v -> p j v", p=128), Hst.v(), is_out=True)
            P.copy(ysb[0:R, :], psY, eng="act")
            P.rsum(st1[0:R, 0:1], ysb[0:R, :])
            P.ts(st1[0:R, 0:1], st1[0:R, 0:1], -1.0 / 64, None, ALU.mult)
            P.ts(yc[0:R, :], ysb[0:R, :], st1[0:R, 0:1], None, ALU.add)
            P.tt(ysb[0:R, :], yc[0:R, :], yc[0:R, :], ALU.mult)
            P.rsum(st1[0:R, 1:2], ysb[0:R, :])
            P.act(st1[0:R, 1:2], st1[0:R, 1:2], AF.Sqrt, bias=RWKV_GN_EPS, scale=1.0 / 64)
            P.recip(st1[0:R, 1:2], st1[0:R, 1:2])
            P.stt(yc[0:R, :], yc[0:R, :], st1[0:R, 1:2], ln[Cc][0:R, :], ALU.mult, ALU.mult)
            P.copy(st1[0:R, 2:3], psS[0:R, 0:1], eng="act")
            P.stt(yc[0:R, :], V_hs, st1[0:R, 2:3], yc[0:R, :], ALU.mult, ALU.add)
            pT = P.ps()
            P.transpose(pT[0:64, 0:R], yc[0:R, :], ident[0:R, 0:R])
            P.tt(yTr[0:64, :, sl], pT[0:64, 0:R].re("p (h t) -> p h t", t=Cc), gT[0:64, :, sl], ALU.mult)
        if last:
            P.dma(do["o_rwkv_p"][l].rearrange("(j p) v -> p j v", p=128), Hr[l].v(), is_out=True)

    def la_tail(G, S, l, T, pn, ns, qT, kT, vT, eb, Hp, st_key, out_s, out_p, ng, sgate, oT, last):
        NH = G["NH"]
        NJk = G["NJ"]
        Hs = [P.alloc(f"la_Hs{i}", [NJk, 128]) for i in range(2)]
        ysb = P.alloc("la_y", [128])
        st1 = P.alloc("la_st", [4])
        for (kind, c0, Cc, q) in chunks_of(pn, ns, cfg.CL):
            R = NH * Cc
            if kind == "p":
                Hst = Hp
            else:
                Hst = Hs[q % 2]
                P.dma(Hst.v(), di[st_key][l, q].rearrange("(j p) v -> p j v", p=128))
            sl = slice(c0, c0 + Cc)
            psY, V_hs, _ = la_chunk(P, G, S, Cc, qT[:, :, sl], kT[:, :, sl], vT[:, :, sl], eb[:, :, c0 + Cc - 1], Hst)
            if kind == "s":
                P.dma(do[out_s][l, q].rearrange("(j p) v -> p j v", p=128), Hst.v(), is_out=True)
            P.act(ysb[0:R, :], psY, AF.Square)
            P.rsum(st1[0:R, 0:1], ysb[0:R, :])
            P.act(st1[0:R, 0:1], st1[0:R, 0:1], AF.Sqrt, bias=EPS, scale=1.0 / 128)
            P.recip(st1[0:R, 0:1], st1[0:R, 0:1])
            P.stt(ysb[0:R, :], psY, st1[0:R, 0:1], ng[Cc][0:R, :], ALU.mult, ALU.mult)
            pT = P.ps()
            P.transpose(pT[:, 0:R], ysb[0:R, :], ident[0:R, 0:R])
            P.tt(oT[:, :, sl], pT[:, 0:R].re("p (h t) -> p h t", t=Cc), sgate[:, :, sl], ALU.mult)
        if last:
            P.dma(do[out_p][l].rearrange("(j p) v -> p j v", p=128), Hp.v(), is_out=True)

    def hgrn(ti, l, T, pn, ns, oT):
        last = ti == ntile - 1
        P.off = mix_base
        pt = P.alloc("pt_h", [4 * NJ, TMAX])
        project(T, cfg.ct_hq, 4 * NJ, pt)
        ng = {}
        for Cc in sorted(set(c_[2] for c_ in chunks_of(pn, ns, cfg.CL))):
            ng[Cc] = P.alloc(f"hg_n{Cc}", [128])
            P.dma(ng[Cc][0:HH * Cc], di[f"hg_n{Cc}"][l])
        qT, fT, iT, gT_ = (pt[:, k * NJ:(k + 1) * NJ, :] for k in range(4))
        lf = P.alloc("hg_lf", [NJ, TMAX])
        cs = P.alloc("hg_cs", [NJ, TMAX])
        eb = P.alloc("hg_eb", [NJ, TMAX])
        P.act(qT[:, :, 0:T], qT[:, :, 0:T], AF.Silu)
        P.act(gT_[:, :, 0:T], gT_[:, :, 0:T], AF.Silu)
        P.act(fT[:, :, 0:T], fT[:, :, 0:T], AF.Sigmoid, scale=-1.0)
        for j in range(NJ):
            P.ts(fT[:, j, 0:T], fT[:, j, 0:T], oml[:, l, j:j + 1], HGRN_MAX_INPUT, ALU.mult, ALU.min)
            P.act(lf[:, j, 0:T], fT[:, j, 0:T], AF.Ln, bias=1.0, scale=-1.0)
            P.scan(cs[:, j, 0:T], cmk2(T), lf[:, j, 0:T], 0.0)
        P.act(eb[:, :, 0:T], cs[:, :, 0:T], AF.Exp)
        P.act(lf[:, :, 0:T], cs[:, :, 0:T], AF.Exp, scale=-1.0)
        P.tt(qT[:, :, 0:T], qT[:, :, 0:T], eb[:, :, 0:T], ALU.mult)
        P.tt(fT[:, :, 0:T], fT[:, :, 0:T], lf[:, :, 0:T], ALU.mult)
        S = la_scratch(G_h, False)
        la_tail(G_h, S, l, T, pn, ns, qT, fT, iT, eb, Hh[l], "st_hgrn", "o_hgrn_s", "o_hgrn_p", ng, gT_, oT, last)

    def gla(ti, l, T, pn, ns, oT):
        last = ti == ntile - 1
        P.off = mix_base
        nct = cfg.NCT - cfg.ct_gq
        pt = P.alloc("pt_g", [nct, TMAX])
        project(T, cfg.ct_gq, nct, pt)
        ng = {}
        for Cc in sorted(set(c_[2] for c_ in chunks_of(pn, ns, cfg.CL))):
            ng[Cc] = P.alloc(f"gl_n{Cc}", [128])
            P.dma(ng[Cc][0:GH * Cc], di[f"gl_n{Cc}"][l])
        gku = P.alloc("gk_up", [GW // 2])
        P.dma(gku[0:16, :], di["gk_up"][l])
        negb = P.alloc("gl_negb", [GQT])
        P.ts(negb.v(), vc(l, "gk_b"), -1.0, None, ALU.mult)
        o = 0
        qT = pt[:, o:o + GQT, :]; o += GQT
        kT = pt[:, o:o + GQT, :]; o += GQT
        vT = pt[:, o:o + NJ, :]; o += NJ
        glo = pt[:, o, :]; o += 1
        gT_ = pt[:, o:o + NJ, :]
        ll = P.alloc("gl_l", [GQT, TMAX])
        cs = P.alloc("gl_cs", [GQT, TMAX])
        eb = P.alloc("gl_eb", [GQT, TMAX])
        for j in range(GQT):
            ps = P.ps()
            P.mm(ps[:, 0:T], gku[0:16, j * 128:(j + 1) * 128], glo[0:16, 0:T])
            P.act(ll[:, j, 0:T], ps[:, 0:T], AF.Exp, bias=negb[:, j:j + 1], scale=-1.0)
            P.act(ll[:, j, 0:T], ll[:, j, 0:T], AF.Ln, bias=1.0)
            P.scan(cs[:, j, 0:T], cmk2(T), ll[:, j, 0:T], 0.0)
        P.act(gT_[:, :, 0:T], gT_[:, :, 0:T], AF.Silu)
        P.act(eb[:, :, 0:T], cs[:, :, 0:T], AF.Exp, scale=-1.0 / 16)
        P.act(ll[:, :, 0:T], cs[:, :, 0:T], AF.Exp, scale=1.0 / 16)
        P.stt(qT[:, :, 0:T], qT[:, :, 0:T], 0.125, eb[:, :, 0:T], ALU.mult, ALU.mult)
        P.tt(kT[:, :, 0:T], kT[:, :, 0:T], ll[:, :, 0:T], ALU.mult)
        S = la_scratch(G_g, False)
        la_tail(G_g, S, l, T, pn, ns, qT, kT, vT, eb, Hg[l], "st_gla", "o_gla_s", "o_gla_p", ng, gT_, oT, last)

    def s5(ti, l, T, pn, ns, oT):
        last = ti == ntile - 1
        P.off = mix_base
        pt = P.alloc("pt_s", [NJ, TMAX])
        project(T, cfg.ct_s5, NJ, pt)
        wgl = P.alloc("s5_wglu", [NJ, GW])
        P.dma(wgl.v(), di["s5_wglu"][l].rearrange("(c p) n -> p c n", p=128))
        BXr, BXi = P.alloc("BXr", [S5C, 128]), P.alloc("BXi", [S5C, 128])
        bbr, bbi = P.alloc("bbr", [S5C, 128]), P.alloc("bbi", [S5C, 128])
        CXr, CXi = P.alloc("CXr", [S5C, 128]), P.alloc("CXi", [S5C, 128])
        for b_ in (BXr, BXi, CXr, CXi):
            P.memset(b_.v(), 0.0)
        for b8 in range(8):
            for (dst, key) in ((BXr, "s5_Bt_re"), (BXi, "s5_Bt_im")):
                P.dma(dst[b8 * 16:(b8 + 1) * 16, (b8 // 2)::4, (b8 % 2) * 64:(b8 % 2) * 64 + 64],
                      di[key][l].rearrange("(a b) c p -> b c a p", b=8)[b8])
            for (dst, key) in ((CXr, "s5_Ct_re"), (CXi, "s5_Ct_im")):
                P.dma(dst[(b8 % 2) * 64:(b8 % 2) * 64 + 64, (b8 // 2)::4, b8 * 16:(b8 + 1) * 16],
                      di[key][l].rearrange("(a b) p c -> b p a c", b=8)[b8])
        P.ts(CXi.v(), CXi.v(), -1.0, None, ALU.mult)
        sc = [P.alloc(f"s5c_{i}", [S5C]) for i in range(12)]
        dtv, ar, th, rho, sn, cs_, abre, abim, den, core, coim, tmp = sc
        P.act(dtv.v(), vc(l, "logdt"), AF.Exp)
        P.tt(ar.v(), vc(l, "A_re"), dtv.v(), ALU.mult)
        P.tt(th.v(), vc(l, "A_im"), dtv.v(), ALU.mult)
        P.act(rho.v(), ar.v(), AF.Exp)
        range_reduce(sn.v(), th.v(), tmp.v())
        P.act(sn.v(), sn.v(), AF.Sin)
        P.ts(cs_.v(), th.v(), math.pi / 2, None, ALU.add)
        range_reduce(cs_.v(), cs_.v(), tmp.v())
        P.act(cs_.v(), cs_.v(), AF.Sin)
        P.tt(abre.v(), rho.v(), cs_.v(), ALU.mult)
        P.tt(abim.v(), rho.v(), sn.v(), ALU.mult)
        P.tt(den.v(), vc(l, "A_re"), vc(l, "A_re"), ALU.mult)
        P.tt(tmp.v(), vc(l, "A_im"), vc(l, "A_im"), ALU.mult)
        P.tt(den.v(), den.v(), tmp.v(), ALU.add)
        P.recip(den.v(), den.v())
        P.ts(abre.v(), abre.v(), -1.0, None, ALU.add)
        P.tt(core.v(), abre.v(), vc(l, "A_re"), ALU.mult)
        P.tt(tmp.v(), abim.v(), vc(l, "A_im"), ALU.mult)
        P.tt(core.v(), core.v(), tmp.v(), ALU.add)
        P.tt(core.v(), core.v(), den.v(), ALU.mult)
        P.tt(coim.v(), abim.v(), vc(l, "A_re"), ALU.mult)
        P.tt(tmp.v(), abre.v(), vc(l, "A_im"), ALU.mult)
        P.tt(coim.v(), coim.v(), tmp.v(), ALU.subtract)
        P.tt(coim.v(), coim.v(), den.v(), ALU.mult)
        dg = [P.alloc(f"s5dg{i}", [128]) for i in range(2)]
        t1, t2 = P.alloc("s5t1", [128]), P.alloc("s5t2", [128])
        for oc in range(S5C):
            P.ts(dg[0].v(), ident, core[:, oc:oc + 1], None, ALU.mult)
            P.ts(dg[1].v(), ident, coim[:, oc:oc + 1], None, ALU.mult)
            pr, pi_ = P.ps(), P.ps()
            P.mm(pr[:, 0:128], ones_f, dg[0].v())
            P.mm(pi_[:, 0:128], ones_f, dg[1].v())
            P.tt(t1.v(), BXr[:, oc, :], pr[:, 0:128], ALU.mult)
            P.tt(t2.v(), BXi[:, oc, :], pi_[:, 0:128], ALU.mult)
            P.tt(bbr[:, oc, :], t1.v(), t2.v(), ALU.subtract)
            P.tt(t1.v(), BXi[:, oc, :], pr[:, 0:128], ALU.mult)
            P.tt(t2.v(), BXr[:, oc, :], pi_[:, 0:128], ALU.mult)
            P.tt(bbi[:, oc, :], t1.v(), t2.v(), ALU.add)
        if ns:
            sts = [P.alloc(f"s5st{i}", [S5C, NS]) for i in range(2)]
            osts = [P.alloc(f"s5ost{i}", [S5C, NS]) for i in range(2)]
            P.dma(sts[0].v(), di["st_s5re"][l])
            P.dma(sts[1].v(), di["st_s5im"][l])
        W = [P.alloc(f"s5w{i}", [TMAX]) for i in range(9)]
        a1, snt, cst, zre, zim, dec, u1, u2, u3 = W
        hb = P.alloc("s5hb", [4, 2, TMAX])
        ygl = P.alloc("s5ygl", [NJ, TMAX])
        for oc in range(S5C):
            u = pt[:, oc // 4, 0:T]
            pbr, pbi = P.ps(), P.ps()
            P.mm(pbr[:, 0:T], bbr[:, oc, :], u)
            P.mm(pbi[:, 0:T], bbi[:, oc, :], u)
            P.ts(a1[:, 0:T], posv(T), th[:, oc:oc + 1], None, ALU.mult)
            range_reduce(snt[:, 0:T], a1[:, 0:T], u1[:, 0:T])
            P.act(snt[:, 0:T], snt[:, 0:T], AF.Sin)
            P.ts(a1[:, 0:T], a1[:, 0:T], math.pi / 2, None, ALU.add)
            range_reduce(cst[:, 0:T], a1[:, 0:T], u1[:, 0:T])
            P.act(cst[:, 0:T], cst[:, 0:T], AF.Sin)
            P.tt(u1[:, 0:T], cst[:, 0:T], pbr[:, 0:T], ALU.mult)
            P.tt(u2[:, 0:T], snt[:, 0:T], pbi[:, 0:T], ALU.mult)
            P.tt(zre[:, 0:T], u1[:, 0:T], u2[:, 0:T], ALU.add)
            P.tt(u1[:, 0:T], cst[:, 0:T], pbi[:, 0:T], ALU.mult)
            P.tt(u2[:, 0:T], snt[:, 0:T], pbr[:, 0:T], ALU.mult)
            P.tt(zim[:, 0:T], u1[:, 0:T], u2[:, 0:T], ALU.subtract)
            P.ts(dec[:, 0:T], smk(T), rho[:, oc:oc + 1], None, ALU.mult)
            for i, z in enumerate((zre, zim)):
                P.stt(z[:, 0:1], s5c[l][i][:, oc:oc + 1], rho[:, oc:oc + 1], z[:, 0:1], ALU.mult, ALU.add)
                if ns:
                    zv = z[:, pn:T].re("p (s t) -> p s t", t=TS)[:, :, 0:1]
                    P.stt(zv, sts[i][:, oc, :].us(2), rho[:, oc:oc + 1], zv, ALU.mult, ALU.add)
            P.scan(u1[:, 0:T], dec[:, 0:T], zre[:, 0:T], 0.0)
            P.scan(u2[:, 0:T], dec[:, 0:T], zim[:, 0:T], 0.0)
            hre, him = hb[:, oc % 4, 0, :], hb[:, oc % 4, 1, :]
            P.tt(zre[:, 0:T], cst[:, 0:T], u1[:, 0:T], ALU.mult)
            P.tt(zim[:, 0:T], snt[:, 0:T], u2[:, 0:T], ALU.mult)
            P.tt(hre[:, 0:T], zre[:, 0:T], zim[:, 0:T], ALU.subtract)
            P.tt(zre[:, 0:T], cst[:, 0:T], u2[:, 0:T], ALU.mult)
            P.tt(zim[:, 0:T], snt[:, 0:T], u1[:, 0:T], ALU.mult)
            P.tt(him[:, 0:T], zre[:, 0:T], zim[:, 0:T], ALU.add)
            for i, hh_ in enumerate((hre, him)):
                P.copy(s5c[l][i][:, oc:oc + 1], hh_[:, pn - 1:pn])
                if ns:
                    P.copy(osts[i][:, oc, :].us(2), hh_[:, pn:T].re("p (s t) -> p s t", t=TS)[:, :, TS - 1:TS])
            if oc % 4 == 3:
                ot = oc // 4
                py = P.ps()
                for i in range(4):
                    P.mm(py[:, 0:T], CXr[:, 4 * ot + i, :], hb[:, i, 0, 0:T], start=(i == 0), stop=False)
                    P.mm(py[:, 0:T], CXi[:, 4 * ot + i, :], hb[:, i, 1, 0:T], start=False, stop=(i == 3))
                P.stt(ygl[:, ot, 0:T], pt[:, ot, 0:T], vc(l, "s5_D", ot), py[:, 0:T], ALU.mult, ALU.add)
                P.act(ygl[:, ot, 0:T], ygl[:, ot, 0:T], AF.Gelu_apprx_tanh)
        for ot in range(NJ):
            ps = P.ps()
            for kt in range(NJ):
                P.mm(ps[:, 0:T], wgl[:, kt, ot * 128:(ot + 1) * 128], ygl[:, kt, 0:T], start=(kt == 0),
                     stop=(kt == NJ - 1))
            P.act(u3[:, 0:T], ps[:, 0:T], AF.Sigmoid, bias=vc(l, "s5_bglu", ot))
            P.tt(oT[:, ot, 0:T], ygl[:, ot, 0:T], u3[:, 0:T], ALU.mult)
        if ns:
            P.dma(do["o_s5re_s"][l], osts[0].v(), is_out=True)
            P.dma(do["o_s5im_s"][l], osts[1].v(), is_out=True)
        if last:
            P.dma(do["o_s5re_p"][l], s5c[l][0].v(), is_out=True)
            P.dma(do["o_s5im_p"][l], s5c[l][1].v(), is_out=True)

    def layer(ti, l, T, pn, ns):
        nonlocal mix_base
        last = ti == ntile - 1
        P.off = phase_base
        yTr = P.alloc("yTr", [RH, TMAX], BF16)
        mixB = [P.alloc(f"mixB{i}", [NJ, TMAX], BF16) for i in range(3)]
        mix_base = P.off
        rmsnorm(T, "norm_mix", l, h)
        mix_base = P.off
        rwkv(ti, l, T, pn, ns, yTr)
        s5(ti, l, T, pn, ns, mixB[0])
        hgrn(ti, l, T, pn, ns, mixB[1])
        gla(ti, l, T, pn, ns, mixB[2])
        for og in range(D // 256):
            (wa_, wb_), b = WS.get("wout")
            for i in range(2):
                oc = og * 2 + i
                ps = P.ps()
                for hd in range(RH):
                    P.mm(ps[:, 0:T], wa_[0:64, hd, i * 128:(i + 1) * 128], yTr[0:64, hd, 0:T], start=(hd == 0),
                         stop=False)
                for k in range(3 * NJ):
                    P.mm(ps[:, 0:T], wb_[:, k, i * 128:(i + 1) * 128], mixB[k // NJ][:, k % NJ, 0:T], start=False,
                         stop=(k == 3 * NJ - 1))
                P.tt(x[:, oc, 0:T], x[:, oc, 0:T], ps[:, 0:T], ALU.add)
        P.off = mix_base
        rmsnorm(T, "norm_ffn", l, h)
        aT = P.alloc("aT", [NFT, TMAX], BF16)
        uxp = [P.alloc(f"uxp{i}", [TMAX + 2]) for i in range(2)]
        acc = [P.alloc(f"acc{i}", [TMAX]) for i in range(2)]
        if ns:
            stc = P.alloc("stc", [NFT, NS, 2])
            ostc = P.alloc("ostc", [NFT, NS, 2])
            uxs = [P.alloc(f"uxs{i}", [NS, TS + 2]) for i in range(2)]
            P.dma(stc.v(), di["st_conv"][l])
        j = 0
        while j < NFT:
            (wv,), b = WS.get("wup")
            for jj in range(b[2]):
                pu, pg = P.ps(), P.ps()
                for c in range(NDC):
                    P.mm(pu[:, 0:T], wv[:, c, 0, jj * 128:(jj + 1) * 128], h[:, c, 0:T], start=(c == 0),
                         stop=(c == NDC - 1))
                for c in range(NDC):
                    P.mm(pg[:, 0:T], wv[:, c, 1, jj * 128:(jj + 1) * 128], h[:, c, 0:T], start=(c == 0),
                         stop=(c == NDC - 1))
                ux, ac = uxp[j % 2], acc[j % 2]
                w0, w1, w2, cb = (vc(l, f"conv_w{i}", j) for i in range(3)), None, None, vc(l, "conv_b", j)
                w0, w1, w2 = list(w0)
                P.copy(ux[:, 0:2], convc[l][:, j, :])
                P.copy(ux[:, 2:2 + pn], pu[:, 0:pn], eng="act")
                P.copy(convc[l][:, j, :], ux[:, pn:pn + 2])
                P.ts(ac[:, 0:pn], ux[:, 0:pn], w0, cb, ALU.mult, ALU.add)
                P.stt(ac[:, 0:pn], ux[:, 1:pn + 1], w1, ac[:, 0:pn], ALU.mult, ALU.add)
                P.stt(ac[:, 0:pn], ux[:, 2:pn + 2], w2, ac[:, 0:pn], ALU.mult, ALU.add)
                if ns:
                    us_ = uxs[j % 2]
                    acs = ac[:, pn:T].re("p (s t) -> p s t", t=TS)
                    P.copy(us_[:, :, 0:2], stc[:, j, :, :])
                    P.copy(us_[:, :, 2:2 + TS], pu[:, pn:T].re("p (s t) -> p s t", t=TS), eng="act")
                    P.copy(ostc[:, j, :, :], us_[:, :, TS:TS + 2])
                    P.ts(acs, us_[:, :, 0:TS], w0, cb, ALU.mult, ALU.add)
                    P.stt(acs, us_[:, :, 1:TS + 1], w1, acs, ALU.mult, ALU.add)
                    P.stt(acs, us_[:, :, 2:TS + 2], w2, acs, ALU.mult, ALU.add)
                P.act(ac[:, 0:T], ac[:, 0:T], AF.Gelu_apprx_tanh)
                P.tt(aT[:, j, 0:T], ac[:, 0:T], pg[:, 0:T], ALU.mult)
                j += 1
        if ns:
            P.dma(do["o_conv_s"][l], ostc.v(), is_out=True)
        if last:
            P.dma(do["o_conv_p"][l], convc[l].v(), is_out=True)
        for og in range(D // 512):
            pss = [P.ps() for _ in range(4)]
            k0 = 0
            while k0 < NFT:
                (wv,), b = WS.get("wdown")
                assert b[1] == og and b[2] == k0
                for kk_ in range(b[3]):
                    kt = k0 + kk_
                    for i in range(4):
                        P.mm(pss[i][:, 0:T], wv[:, kk_, i * 128:(i + 1) * 128], aT[:, kt, 0:T], start=(kt == 0),
                             stop=(kt == NFT - 1))
                k0 += b[3]
            for i in range(4):
                oc = og * 4 + i
                P.tt(x[:, oc, 0:T], x[:, oc, 0:T], pss[i][:, 0:T], ALU.add)

    mix_base = phase_base
    for ti, (p0, pn, ns) in enumerate(cfg.tiles):
        T = pn + ns * TS
        P.dma(x[:, :, 0:pn], di["xin"][:, p0:p0 + pn].rearrange("(c p) t -> p c t", p=128))
        if ns:
            P.dma(x[:, :, pn:T], di["xin"][:, cfg.NP:cfg.NP + ns * TS].rearrange("(c p) t -> p c t", p=128))
        P.dma(tmask.v(), di["tmask"][ti].rearrange("k p t -> p k t"))
        for l in range(L):
            layer(ti, l, T, pn, ns)
        P.off = mix_base
        yo = P.alloc("yo", [NDC, TMAX])
        sq = [P.alloc(f"fsq{i}", [TMAX], BF16) for i in range(2)]
        rstd = P.alloc("frstd", [TMAX])
        ps = P.ps()
        for c in range(NDC):
            s_ = sq[c % 2]
            P.act(s_[:, 0:T], x[:, c, 0:T], AF.Square)
            P.mm(ps[:, 0:T], ones_bf.v(), s_[:, 0:T], start=(c == 0), stop=(c == NDC - 1))
        P.act(rstd[:, 0:T], ps[:, 0:T], AF.Sqrt, bias=EPS, scale=1.0 / D)
        P.recip(rstd[:, 0:T], rstd[:, 0:T])
        for c in range(NDC):
            P.stt(yo[:, c, 0:T], x[:, c, 0:T], normf[:, c:c + 1], rstd[:, 0:T], ALU.mult, ALU.mult)
        P.dma(do["yT"][:, p0:p0 + pn].rearrange("(c p) t -> p c t", p=128), yo[:, :, 0:pn], is_out=True)
        if ns:
            P.dma(do["yT"][:, cfg.NP:cfg.NP + ns * TS].rearrange("(c p) t -> p c t", p=128), yo[:, :, pn:T],
                  is_out=True)
    P.emit()
    return nc, P


REAL = dict(D=2048, L=4, NP=2064, NS=16, TILE=256, C=16)
_cache = {}


def kernel(**inputs):
    cfg = Cfg(**REAL)
    inp = {k: np.asarray(v) for k, v in inputs.items()}
    B = inp["x_prompt"].shape[0]
    ncore = 8
    sh = pack_shared(cfg, inp)
    in_maps = []
    for c in range(ncore):
        m = dict(sh)
        m.update(pack_core(cfg, inp, c % B, c * cfg.NS))
        in_maps.append(m)
    nc, _ = build_program(cfg)
    res = run_bass_kernel_spmd(nc, in_maps, core_ids=list(range(ncore)))
    outs = [unpack_core(cfg, r) for r in res.results]
    cat = lambda k: np.concatenate([o[k] for o in outs], axis=1)
    stack_p = lambda k: np.stack([outs[b][k] for b in range(B)], axis=1)
    f = lambda a: np.ascontiguousarray(a, dtype=np.float32)
    return (f(np.stack([outs[b]["y_p"] for b in range(B)], axis=0)),
            f(np.concatenate([o["y_s"] for o in outs], axis=0)),
            f(stack_p("rwkv_p")), f(cat("rwkv_s")), f(stack_p("shift_p")), f(cat("shift_s")),
            f(stack_p("s5re_p")), f(cat("s5re_s")), f(stack_p("s5im_p")), f(cat("s5im_s")),
            f(stack_p("hgrn_p")), f(cat("hgrn_s")), f(stack_p("gla_p")), f(cat("gla_s")),
            f(stack_p("conv_p")), f(cat("conv_s")))
```

```python
import math
from contextlib import ExitStack
import numpy as np
import concourse.bass as bass
import concourse.mybir as mybir
from concourse.bass_utils import run_bass_kernel_spmd

F32 = mybir.dt.float32
BF16 = mybir.dt.bfloat16
AF = mybir.ActivationFunctionType
ALU = mybir.AluOpType
AX = mybir.AxisListType

EPS = 1e-6
RWKV_DECAY_SCALE = 0.606531
RWKV_GN_EPS = 64e-5
HGRN_MAX_INPUT = 1.0 - 1e-4
N_META = 16
TS = 4


class Cfg:
    def __init__(s, D=2048, L=4, NP=2064, NS=16, TILE=256, C=16):
        s.D, s.L, s.NP, s.NS, s.TILE, s.C = D, L, NP, NS, TILE, C
        s.CL = 32
        s.CCS = (32, 16, TS)
        s.GW = D // 4
        s.NDC = D // 128
        s.RH = s.GW // 64
        s.NJ = s.GW // 128
        s.HH = s.GW // 128
        s.GH = s.GW // 128
        s.GQT = s.GW // 256
        s.S5C = (s.GW // 16) * 64 // 128
        s.DFF = ((8 * D // 3 + 127) // 128) * 128
        s.NFT = s.DFF // 128
        NJ = s.NJ
        s.ct_r, s.ct_k, s.ct_v, s.ct_wa, s.ct_gl = 0, NJ, 2 * NJ, 3 * NJ, 3 * NJ + 1
        s.NR = 3 * NJ + 2
        s.ct_s5 = s.NR
        s.ct_hq = s.ct_s5 + NJ
        s.ct_hf, s.ct_hi, s.ct_hg = s.ct_hq + NJ, s.ct_hq + 2 * NJ, s.ct_hq + 3 * NJ
        s.ct_gq = s.ct_hq + 4 * NJ
        s.ct_gk = s.ct_gq + s.GQT
        s.ct_gv = s.ct_gk + s.GQT
        s.ct_glo = s.ct_gv + NJ
        s.ct_gg = s.ct_glo + 1
        s.NCT = s.ct_gg + NJ
        s.mix_groups = [(0, s.NR), (s.ct_s5, NJ), (s.ct_hq, 4 * NJ), (s.ct_gq, s.NCT - s.ct_gq)]
        s.tiles = []
        p = 0
        while s.NP - p > C:
            n = min(TILE, s.NP - C - p)
            s.tiles.append((p, n, 0))
            p += n
        s.tiles.append((p, s.NP - p, NS))
        s.NTOK = s.NP + NS * TS
        s.TMAX = max(max(t[1] + t[2] * TS for t in s.tiles), 1)
        v = {}
        o = 0
        for name, n in [("norm_mix", s.NDC), ("norm_ffn", s.NDC), ("mu", s.NR), ("w0", NJ), ("a0", NJ),
                        ("k_k", NJ), ("k_a", NJ), ("r_k", NJ), ("s5_D", NJ), ("s5_bglu", NJ),
                        ("gk_b", s.GQT), ("conv_w0", s.NFT), ("conv_w1", s.NFT), ("conv_w2", s.NFT),
                        ("conv_b", s.NFT), ("A_re", s.S5C), ("A_im", s.S5C), ("logdt", s.S5C),
                        ("hlb", NJ)]:
            v[name] = (o, n)
            o += n
        s.vec = v
        s.NVEC = o


def build_consts(cfg):
    c = {}
    c["ident"] = np.eye(128, dtype=np.float32)
    c["ones"] = np.ones((128, 128), np.float32)
    bo = np.zeros((128, 128), np.float32)
    bo[:64, :64] = 1
    bo[64:, 64:] = 1
    c["blockones"] = bo
    f64 = np.zeros((128, 64), np.float32)
    f64[np.arange(128), np.arange(128) % 64] = 1
    c["f64"] = f64
    for C in cfg.CCS:
        r = np.arange(128) % C
        c[f"mstrT{C}"] = (r[:, None] < r[None, :]).astype(np.float32)
        c[f"minclT{C}"] = (r[:, None] <= r[None, :]).astype(np.float32)
        c[f"mstr{C}"] = (r[:, None] > r[None, :]).astype(np.float32)
    return c


def head_mask(NJ, HPT, NH):
    DK = 128 // HPT
    m = np.zeros((128, NJ, NH), np.float32)
    for j in range(NJ):
        for p in range(128):
            h = j * HPT + p // DK
            if h < NH:
                m[p, j, h] = 1
    return m


def col_mask(C, NJ, HPT, NH):
    m = np.zeros((128, NJ, HPT), np.float32)
    for row in range(min(128, NH * C)):
        h = row // C
        m[row, h // HPT, h % HPT] = 1
    return m


class Buf:
    def __init__(s, name, ap, pages, space):
        s.name, s.ap, s.pages, s.space = name, ap, pages, space

    def __getitem__(s, k):
        return V(s.ap[k], s)

    def v(s):
        return V(s.ap, s)


class V:
    def __init__(s, ap, buf):
        s.ap, s.buf = ap, buf

    def __getitem__(s, k):
        return V(s.ap[k], s.buf)

    def re(s, pat, **kw):
        return V(s.ap.rearrange(pat, **kw), s.buf)

    def bc(s, shape):
        return V(s.ap.to_broadcast(list(shape)), s.buf)

    def us(s, axis):
        return V(s.ap.unsqueeze(axis), s.buf)

    @property
    def shape(s):
        return s.ap.shape


PAGE = 256
ENG_NAMES = ["pe", "act", "dve", "pool", "sp"]
N_DMA_SEM = {"sp": 24, "pool": 64, "act": 2}


class Op:
    __slots__ = ("eng", "fn", "deps", "is_dma", "signal", "idx", "dsem", "dval", "dprev", "is_out", "count")


class Prog:
    def __init__(s, nc, arena_words):
        s.nc = nc
        s.arena = nc.alloc_sbuf_tensor("arena", [128, arena_words], F32)
        s.A = s.arena[:, :]
        s.arena_words = arena_words
        s.off = 0
        s.ops = {e: [] for e in ENG_NAMES}
        s.last_w = {}
        s.readers = {}
        s.dma_count = {e: 0 for e in ENG_NAMES}
        s.psum = []
        for b in range(8):
            t = nc.alloc_psum_tensor(f"psb{b}", [128, 512], F32)
            s.psum.append(Buf(f"ps{b}", t[:, :], [("ps", b)], "psum"))
        s.ps_rr = 0
        s.marks = []

    def alloc(s, name, free_shape, dtype=F32, at=None):
        n = int(np.prod(free_shape))
        words = n if dtype == F32 else (n + 1) // 2
        if at is None:
            off = s.off
            s.off += words
        else:
            off = at
        assert off + words <= s.arena_words, f"arena overflow at {name}: {off + words} > {s.arena_words}"
        ap = s.A[:, off:off + words]
        if dtype != F32:
            ap = ap.bitcast(dtype)
        if len(free_shape) > 1:
            names = " ".join(f"a{i}" for i in range(len(free_shape)))
            kw = {f"a{i}": int(free_shape[i]) for i in range(1, len(free_shape))}
            ap = ap.rearrange(f"p ({names}) -> p {names}", **kw)
        pages = list(range((off * 4) // PAGE, ((off + words) * 4 - 1) // PAGE + 1))
        return Buf(name, ap, pages, "sbuf")

    def ps(s):
        b = s.psum[s.ps_rr % 8]
        s.ps_rr += 1
        return b

    def _rec(s, eng, fn, reads, writes, is_dma=False, is_out=False):
        op = Op()
        op.eng, op.fn, op.is_dma, op.signal, op.is_out = eng, fn, is_dma, False, is_out
        op.idx = len(s.ops[eng])
        deps = set()
        rp = []
        wp = []
        for b in reads:
            if b is not None:
                rp.extend(b.pages)
        for b in writes:
            if b is not None:
                wp.extend(b.pages)
        for p in rp:
            w = s.last_w.get(p)
            if w is not None:
                deps.add(w)
        for p in wp:
            w = s.last_w.get(p)
            if w is not None:
                deps.add(w)
            for r in s.readers.get(p, ()):
                deps.add(r)
        deps.discard(op)
        op.deps = deps
        for p in wp:
            s.last_w[p] = op
            s.readers[p] = []
        for p in rp:
            lst = s.readers.setdefault(p, [])
            if not is_dma:
                lst[:] = [r for r in lst if r.eng != eng or r.is_dma]
            lst.append(op)
        if is_dma:
            k = s.dma_count[eng]
            s.dma_count[eng] += 1
            nds = N_DMA_SEM[eng]
            op.dsem = k % nds
            op.dval = 16 * (k // nds + 1)
            op.dprev = 16 * (k // nds)
        s.ops[eng].append(op)
        return op

    @staticmethod
    def _b(x):
        return x.buf if isinstance(x, V) else None

    @staticmethod
    def _a(x):
        return x.ap if isinstance(x, V) else x

    def mm(s, out, lhsT, rhs, start=True, stop=True):
        o, l, r = out.ap, lhsT.ap, rhs.ap
        s._rec("pe", lambda e: e.matmul(o, l, r, start=start, stop=stop), [lhsT.buf, rhs.buf], [out.buf])

    def transpose(s, out, in_, ident):
        o, i, d = out.ap, in_.ap, ident.ap
        s._rec("pe", lambda e: e.transpose(o, i, d), [in_.buf, ident.buf], [out.buf])

    def act(s, out, in_, func, bias=None, scale=None):
        kw = {}
        rd = [in_.buf]
        if bias is not None:
            kw["bias"] = s._a(bias)
            rd.append(s._b(bias))
        if scale is not None:
            kw["scale"] = s._a(scale)
            rd.append(s._b(scale))
        o, i = out.ap, in_.ap
        s._rec("act", lambda e: e.activation(out=o, in_=i, func=func, **kw), rd, [out.buf])

    def tt(s, out, in0, in1, op, eng="dve"):
        o, a, b = out.ap, in0.ap, in1.ap
        s._rec(eng, lambda e: e.tensor_tensor(out=o, in0=a, in1=b, op=op), [in0.buf, in1.buf], [out.buf])

    def ts(s, out, in0, s1, s2=None, op0=ALU.mult, op1=None, eng="dve"):
        o, a = out.ap, in0.ap
        a1, a2 = s._a(s1), s._a(s2)
        rd = [in0.buf, s._b(s1), s._b(s2)]
        if op1 is None:
            s._rec(eng, lambda e: e.tensor_scalar(out=o, in0=a, scalar1=a1, scalar2=None, op0=op0), rd, [out.buf])
        else:
            s._rec(eng, lambda e: e.tensor_scalar(out=o, in0=a, scalar1=a1, scalar2=a2, op0=op0, op1=op1), rd,
                   [out.buf])

    def stt(s, out, in0, scalar, in1, op0, op1, eng="dve"):
        o, a, b, sc = out.ap, in0.ap, in1.ap, s._a(scalar)
        s._rec(eng, lambda e: e.scalar_tensor_tensor(out=o, in0=a, scalar=sc, in1=b, op0=op0, op1=op1),
               [in0.buf, in1.buf, s._b(scalar)], [out.buf])

    def copy(s, out, in_, eng="dve"):
        o, i = out.ap, in_.ap
        if eng == "act":
            s._rec("act", lambda e: e.activation(out=o, in_=i, func=AF.Copy), [in_.buf], [out.buf])
        else:
            s._rec(eng, lambda e: e.tensor_copy(out=o, in_=i), [in_.buf], [out.buf])

    def memset(s, out, val, eng="dve"):
        o = out.ap
        s._rec(eng, lambda e: e.memset(o, val), [], [out.buf])

    def rsum(s, out, in_, eng="dve"):
        o, i = out.ap, in_.ap
        s._rec(eng, lambda e: e.reduce_sum(out=o, in_=i, axis=AX.X), [in_.buf], [out.buf])

    def recip(s, out, in_):
        o, i = out.ap, in_.ap
        s._rec("dve", lambda e: e.reciprocal(out=o, in_=i), [in_.buf], [out.buf])

    def scan(s, out, d0, d1, init, op0=ALU.mult, op1=ALU.add):
        o, a, b, ini = out.ap, d0.ap, d1.ap, s._a(init)
        s._rec("dve", lambda e: e.tensor_tensor_scan(out=o, data0=a, data1=b, initial=ini, op0=op0, op1=op1),
               [d0.buf, d1.buf, s._b(init)], [out.buf])

    def dma(s, out, in_, q="sp", is_out=False, extra_reads=()):
        o, i = s._a(out), s._a(in_)
        s._rec(q, lambda e: e.dma_start(out=o, in_=i), [s._b(in_)] + list(extra_reads), [s._b(out)], is_dma=True,
               is_out=is_out)

    def emit(s):
        nc = s.nc
        for e in ENG_NAMES:
            for op in s.ops[e]:
                for d in op.deps:
                    if not d.is_dma:
                        if d.eng == "pe" and op.eng == "pe" and not op.is_dma:
                            continue
                        d.signal = True
        for e in ENG_NAMES:
            cnt = 0
            for op in s.ops[e]:
                if op.signal and not op.is_dma:
                    cnt += 1
                op.count = cnt
        with ExitStack() as st:
            esem = {e: st.enter_context(nc.semaphore(f"sem_{e}")) for e in ENG_NAMES}
            dsem = {e: [st.enter_context(nc.semaphore(f"dsem_{e}_{i}")) for i in range(N_DMA_SEM[e])]
                    for e in ("sp", "pool", "act")}
            block = st.enter_context(nc.Block())

            def run(ename, eng):
                waited = {}
                mine = s.ops[ename]

                def wait(key, sem, val):
                    if waited.get(key, 0) >= val:
                        return
                    eng.wait_ge(sem, val)
                    waited[key] = val

                for op in mine:
                    for d in op.deps:
                        if d.is_dma:
                            wait(("d", d.eng, d.dsem), dsem[d.eng][d.dsem], d.dval)
                        else:
                            if d.eng == "pe" and ename == "pe" and not op.is_dma:
                                continue
                            wait(("e", d.eng), esem[d.eng], d.count)
                    if op.is_dma:
                        if op.dprev > 0:
                            wait(("d", ename, op.dsem), dsem[ename][op.dsem], op.dprev)
                        ins = op.fn(eng)
                        ins.then_inc(dsem[ename][op.dsem], 16)
                    else:
                        ins = op.fn(eng)
                        if op.signal:
                            ins.then_inc(esem[ename], 1)
                if ename in ("sp", "pool", "act"):
                    n = s.dma_count[ename]
                    nds = N_DMA_SEM[ename]
                    for i in range(min(n, nds)):
                        last = 16 * ((n - 1 - i) // nds + 1)
                        wait(("d", ename, i), dsem[ename][i], last)

            @block.tensor
            def _(e):
                run("pe", e)

            @block.scalar
            def _(e):
                run("act", e)

            @block.vector
            def _(e):
                run("dve", e)

            @block.gpsimd
            def _(e):
                run("pool", e)

            @block.sync
            def _(e):
                run("sp", e)


def fm(vec):
    return np.ascontiguousarray(vec.reshape(-1, 128).T)


def const_layout(cfg):
    items = [("ident", 128), ("ones", 128), ("blockones", 128), ("f64", 64)]
    for C in cfg.CCS:
        items += [(f"mstrT{C}", 128), (f"minclT{C}", 128), (f"mstr{C}", 128)]
    items += [("hm_r", cfg.NJ * cfg.RH), ("hm_g", cfg.GQT * cfg.GH), ("hm_h", cfg.NJ * cfg.HH)]
    for C in cfg.CCS:
        items += [(f"cm_r{C}", cfg.NJ * 2), (f"cm_g{C}", cfg.GQT * 2), (f"cm_h{C}", cfg.NJ)]
    off = {}
    o = 0
    for k, n in items:
        off[k] = (o, n)
        o += n
    return off, o


def pack_consts(cfg):
    c = build_consts(cfg)
    c["hm_r"] = head_mask(cfg.NJ, 2, cfg.RH).reshape(128, -1)
    c["hm_g"] = head_mask(cfg.GQT, 2, cfg.GH).reshape(128, -1)
    c["hm_h"] = head_mask(cfg.NJ, 1, cfg.HH).reshape(128, -1)
    for C in cfg.CCS:
        c[f"cm_r{C}"] = col_mask(C, cfg.NJ, 2, cfg.RH).reshape(128, -1)
        c[f"cm_g{C}"] = col_mask(C, cfg.GQT, 2, cfg.GH).reshape(128, -1)
        c[f"cm_h{C}"] = col_mask(C, cfg.NJ, 1, cfg.HH).reshape(128, -1)
    off, n = const_layout(cfg)
    out = np.zeros((128, n), np.float32)
    for k, (o, w) in off.items():
        out[:, o:o + w] = c[k]
    return out


def pack_shared(cfg, inp):
    L, D, GW, NJ = cfg.L, cfg.D, cfg.GW, cfg.NJ
    sh = {}
    w = inp["w_in"]
    RC = 3 * GW + 256
    wt = np.zeros((L, D, cfg.NCT * 128), np.float32)
    wt[:, :, 0:RC] = w[:, :, 0:RC]
    o = RC
    wt[:, :, cfg.ct_s5 * 128: cfg.ct_s5 * 128 + GW] = w[:, :, o:o + GW]
    o += GW
    wt[:, :, cfg.ct_hq * 128: cfg.ct_hq * 128 + 4 * GW] = w[:, :, o:o + 4 * GW]
    o += 4 * GW
    GQK = GW // 2
    wt[:, :, cfg.ct_gq * 128: cfg.ct_gq * 128 + GQK] = w[:, :, o:o + GQK]
    o += GQK
    wt[:, :, cfg.ct_gk * 128: cfg.ct_gk * 128 + GQK] = w[:, :, o:o + GQK]
    o += GQK
    wt[:, :, cfg.ct_gv * 128: cfg.ct_gv * 128 + GW] = w[:, :, o:o + GW]
    o += GW
    wt[:, :, cfg.ct_glo * 128: cfg.ct_glo * 128 + 16] = w[:, :, o:o + 16]
    o += 16
    wt[:, :, cfg.ct_gg * 128: cfg.ct_gg * 128 + GW] = w[:, :, o:o + GW]
    o += GW
    assert o == w.shape[2]
    sh["w_in"] = wt
    sh["w_out"] = np.ascontiguousarray(inp["w_out"])
    sh["w_up"] = np.ascontiguousarray(inp["ffn_w_up"])
    sh["w_down"] = np.ascontiguousarray(inp["ffn_w_down"])
    vecs = np.zeros((L, 128, cfg.NVEC), np.float32)

    def put(name, arr):
        o, n = cfg.vec[name]
        for l in range(L):
            vecs[l, :, o:o + n] = fm(arr[l])
    put("norm_mix", inp["norm_mix"])
    put("norm_ffn", inp["norm_ffn"])
    mu = np.zeros((L, cfg.NR * 128), np.float32)
    mu[:, :RC] = inp["rwkv_mu"]
    put("mu", mu)
    put("w0", inp["rwkv_w0"])
    put("a0", inp["rwkv_a0"])
    put("k_k", inp["rwkv_k_k"])
    put("k_a", inp["rwkv_k_a"])
    put("r_k", inp["rwkv_r_k"].reshape(L, -1))
    put("s5_D", inp["s5_D"])
    put("s5_bglu", inp["s5_b_glu"])
    put("gk_b", inp["gla_gk_b"])
    for j in range(3):
        put(f"conv_w{j}", inp["ffn_conv_w"][:, j])
    put("conv_b", inp["ffn_conv_b"])
    put("A_re", inp["s5_A_re"].reshape(L, -1))
    put("A_im", inp["s5_A_im"].reshape(L, -1))
    put("logdt", np.repeat(inp["s5_log_dt"], 64, axis=1))
    put("hlb", inp["hgrn_lower_bounds"])
    sh["vecs"] = vecs
    sh["normf"] = fm(inp["norm_final"])
    sh["rw_wa"] = np.ascontiguousarray(np.concatenate([inp["rwkv_w_up"], inp["rwkv_a_up"]], axis=1))
    sh["rw_gup"] = np.ascontiguousarray(inp["rwkv_g_up"])
    for C in (cfg.C, TS):
        sh[f"rw_ln{C}"] = np.ascontiguousarray(np.repeat(inp["rwkv_ln"].reshape(L, cfg.RH, 1, 64), C, axis=2)
                                               .reshape(L, cfg.RH * C, 64))
    for C in cfg.CCS:
        sh[f"hg_n{C}"] = np.ascontiguousarray(np.repeat(inp["hgrn_norm"].reshape(L, cfg.HH, 1, 128), C, axis=2)
                                              .reshape(L, cfg.HH * C, 128))
        sh[f"gl_n{C}"] = np.ascontiguousarray(np.repeat(inp["gla_norm"].reshape(L, cfg.GH, 1, 128), C, axis=2)
                                              .reshape(L, cfg.GH * C, 128))
    sh["gk_up"] = np.ascontiguousarray(inp["gla_gk_up"])
    sh["s5_Bt_re"] = np.ascontiguousarray(inp["s5_B_re"].transpose(0, 1, 3, 2))
    sh["s5_Bt_im"] = np.ascontiguousarray(inp["s5_B_im"].transpose(0, 1, 3, 2))
    sh["s5_Ct_re"] = np.ascontiguousarray(inp["s5_C_re"].transpose(0, 1, 3, 2))
    sh["s5_Ct_im"] = np.ascontiguousarray(inp["s5_C_im"].transpose(0, 1, 3, 2))
    sh["s5_wglu"] = np.ascontiguousarray(inp["s5_w_glu"])
    sh["consts"] = pack_consts(cfg)
    ntile = len(cfg.tiles)
    sm = np.ones((ntile, 128, cfg.TMAX), np.float32)
    pos = np.zeros((ntile, 128, cfg.TMAX), np.float32)
    cmk = np.ones((ntile, 128, cfg.TMAX), np.float32)
    cmk2 = np.ones((ntile, 128, cfg.TMAX), np.float32)
    for i, (p0, pn, ns) in enumerate(cfg.tiles):
        sm[i, :, 0] = 0
        pos[i, :, :pn] = np.arange(1, pn + 1)
        cmk[i, :, 0:pn:cfg.C] = 0
        cmk2[i, :, 0:pn:cfg.CL] = 0
        for q in range(ns):
            sm[i, :, pn + q * TS] = 0
            pos[i, :, pn + q * TS: pn + (q + 1) * TS] = np.arange(1, TS + 1)
            cmk[i, :, pn + q * TS] = 0
            cmk2[i, :, pn + q * TS] = 0
    sh["tmask"] = np.ascontiguousarray(np.stack([sm, pos, cmk, cmk2], axis=1))
    return sh


def pack_core(cfg, inp, b, s0):
    L, NS = cfg.L, cfg.NS
    d = {}
    xs = inp["x_sample"][s0:s0 + NS].reshape(NS * TS, cfg.D)
    xall = np.concatenate([inp["meta_tokens"], inp["x_prompt"][b], xs], axis=0)
    d["xin"] = np.ascontiguousarray(xall.T)
    sl = slice(s0, s0 + NS)
    d["st_rwkv"] = np.ascontiguousarray(inp["state_rwkv"][:, sl].transpose(0, 1, 2, 4, 3).reshape(L, NS, cfg.GW, 64))
    sft = np.zeros((L, NS, cfg.NR * 128), np.float32)
    sft[:, :, :3 * cfg.GW + 256] = inp["state_rwkv_shift"][:, sl]
    d["st_shift"] = np.ascontiguousarray(sft.reshape(L, NS, cfg.NR, 128).transpose(0, 3, 2, 1))
    for nm, key in (("st_s5re", "state_s5_re"), ("st_s5im", "state_s5_im")):
        a = inp[key][:, sl].reshape(L, NS, cfg.S5C, 128)
        d[nm] = np.ascontiguousarray(a.transpose(0, 3, 2, 1))
    d["st_hgrn"] = np.ascontiguousarray(inp["state_hgrn"][:, sl].reshape(L, NS, cfg.GW, 128))
    d["st_gla"] = np.ascontiguousarray(inp["state_gla"][:, sl].reshape(L, NS, cfg.GW // 2, 128))
    a = inp["state_ffn_conv"][:, sl].reshape(L, NS, 2, cfg.NFT, 128)
    d["st_conv"] = np.ascontiguousarray(a.transpose(0, 4, 3, 1, 2))
    return d


OUT_SPECS = lambda cfg: {
    "yT": [cfg.D, cfg.NTOK],
    "o_rwkv_p": [cfg.L, cfg.GW, 64], "o_rwkv_s": [cfg.L, cfg.NS, cfg.GW, 64],
    "o_shift_p": [cfg.L, 128, cfg.NR], "o_shift_s": [cfg.L, 128, cfg.NR, cfg.NS],
    "o_s5re_p": [cfg.L, 128, cfg.S5C], "o_s5re_s": [cfg.L, 128, cfg.S5C, cfg.NS],
    "o_s5im_p": [cfg.L, 128, cfg.S5C], "o_s5im_s": [cfg.L, 128, cfg.S5C, cfg.NS],
    "o_hgrn_p": [cfg.L, cfg.GW, 128], "o_hgrn_s": [cfg.L, cfg.NS, cfg.GW, 128],
    "o_gla_p": [cfg.L, cfg.GW // 2, 128], "o_gla_s": [cfg.L, cfg.NS, cfg.GW // 2, 128],
    "o_conv_p": [cfg.L, 128, cfg.NFT, 2], "o_conv_s": [cfg.L, 128, cfg.NFT, cfg.NS, 2],
}


def unpack_core(cfg, r):
    L, NS, GW = cfg.L, cfg.NS, cfg.GW
    o = {}
    yT = r["yT"]
    o["y_p"] = np.ascontiguousarray(yT[:, N_META:cfg.NP].T)
    o["y_s"] = np.ascontiguousarray(yT[:, cfg.NP:].T).reshape(NS, TS, cfg.D)
    o["rwkv_p"] = r["o_rwkv_p"].reshape(L, cfg.RH, 64, 64).transpose(0, 1, 3, 2)
    o["rwkv_s"] = r["o_rwkv_s"].reshape(L, NS, cfg.RH, 64, 64).transpose(0, 1, 2, 4, 3)
    RC = 3 * GW + 256
    o["shift_p"] = r["o_shift_p"].transpose(0, 2, 1).reshape(L, -1)[:, :RC]
    o["shift_s"] = r["o_shift_s"].transpose(0, 3, 2, 1).reshape(L, NS, -1)[:, :, :RC]
    for k in ("s5re", "s5im"):
        o[k + "_p"] = r[f"o_{k}_p"].transpose(0, 2, 1).reshape(L, GW // 16, 64)
        o[k + "_s"] = r[f"o_{k}_s"].transpose(0, 3, 2, 1).reshape(L, NS, GW // 16, 64)
    o["hgrn_p"] = r["o_hgrn_p"].reshape(L, cfg.HH, 128, 128)
    o["hgrn_s"] = r["o_hgrn_s"].reshape(L, NS, cfg.HH, 128, 128)
    o["gla_p"] = r["o_gla_p"].reshape(L, cfg.GH, 64, 128)
    o["gla_s"] = r["o_gla_s"].reshape(L, NS, cfg.GH, 64, 128)
    o["conv_p"] = r["o_conv_p"].transpose(0, 3, 2, 1).reshape(L, 2, cfg.DFF)
    o["conv_s"] = r["o_conv_s"].transpose(0, 3, 4, 2, 1).reshape(L, NS, 2, cfg.DFF)
    return o


def weight_blocks(cfg):
    blocks = []
    for (c0, n) in cfg.mix_groups:
        for g0 in range(0, n, 4):
            blocks.append(("win", c0 + g0, min(4, n - g0)))
    for og in range(cfg.D // 256):
        blocks.append(("wout", og))
    for j0 in range(0, cfg.NFT, 2):
        blocks.append(("wup", j0, min(2, cfg.NFT - j0)))
    for og in range(cfg.D // 512):
        for k0 in range(0, cfg.NFT, 16):
            blocks.append(("wdown", og, k0, min(16, cfg.NFT - k0)))
    return blocks


NSLOT = 3
SLOT_ELEMS = 8192


class WStream:
    def __init__(s, P, cfg, di, order, nc):
        s.P, s.cfg, s.di = P, cfg, di
        s.slots = [P.alloc(f"wslot{i}", [SLOT_ELEMS], BF16) for i in range(NSLOT)]
        s.per = weight_blocks(cfg)
        s.specs = [(l, b) for (_, l) in order for b in s.per]
        s.issued = 0
        s.next = 0
        s.regs = []
        off = 0
        for b in s.per:
            k = b[0]
            if k == "win":
                rr = [(128, cfg.NDC * b[2] * 128)]
            elif k == "wout":
                rr = [(64, cfg.RH * 256), (128, 3 * cfg.NJ * 256)]
            elif k == "wup":
                rr = [(128, cfg.NDC * 2 * b[2] * 128)]
            else:
                rr = [(128, b[3] * 512)]
            lst = []
            for (np_, m) in rr:
                lst.append((off, np_, m))
                off += np_ * m
            s.regs.append(lst)
        s.total = off
        s.scr = [nc.dram_tensor(f"wscr_bf{l}", [off], BF16, kind="Internal").ap() for l in range(cfg.L)]
        s.dbuf = {(l, bi): Buf(f"wscr{l}_{bi}", None, [("dram", l, bi)], "dram")
                  for l in range(cfg.L) for bi in range(len(s.per))}

    def region(s, l, bi, ri):
        off, np_, m = s.regs[bi][ri]
        return V(s.scr[l][off:off + np_ * m].rearrange("(p m) -> p m", p=np_), s.dbuf[(l, bi)])

    def convert_layer(s, l):
        cfg, di, P = s.cfg, s.di, s.P
        for bi, b in enumerate(s.per):
            k = b[0]
            if k == "win":
                c0, n = b[1] * 128, b[2] * 128
                P.dma(s.region(l, bi, 0).re("p (c n) -> p c n", n=n),
                      di["w_in"][l, :, c0:c0 + n].rearrange("(c p) n -> p c n", p=128), q="pool")
            elif k == "wout":
                c0 = b[1] * 256
                P.dma(s.region(l, bi, 0).re("p (h n) -> p h n", n=256),
                      di["w_out"][l, 0:cfg.GW, c0:c0 + 256].rearrange("(h k) n -> k h n", k=64), q="pool")
                P.dma(s.region(l, bi, 1).re("p (c n) -> p c n", n=256),
                      di["w_out"][l, cfg.GW:4 * cfg.GW, c0:c0 + 256].rearrange("(c p) n -> p c n", p=128), q="pool")
            elif k == "wup":
                c0, n = b[1] * 128, b[2] * 128
                dst = s.region(l, bi, 0).re("p (c g n) -> p c g n", g=2, n=n)
                for g in range(2):
                    P.dma(dst[:, :, g, :],
                          di["w_up"][l, :, g * cfg.DFF + c0:g * cfg.DFF + c0 + n].rearrange("(c p) n -> p c n", p=128),
                          q="pool")
            else:
                og, k0, nk = b[1], b[2], b[3]
                P.dma(s.region(l, bi, 0).re("p (c n) -> p c n", n=512),
                      di["w_down"][l, k0 * 128:(k0 + nk) * 128, og * 512:(og + 1) * 512]
                      .rearrange("(c p) n -> p c n", p=128), q="pool")

    def views(s, slot, spec):
        cfg = s.cfg
        l, b = spec
        k = b[0]
        sv = slot.v()
        if k == "win":
            n = b[2] * 128
            return [sv[:, 0:cfg.NDC * n].re("p (c n) -> p c n", n=n)]
        if k == "wout":
            a = sv[:, 0:cfg.RH * 256].re("p (h n) -> p h n", n=256)
            bb = sv[:, cfg.RH * 256:(cfg.RH + 3 * cfg.NJ) * 256].re("p (c n) -> p c n", n=256)
            return [a, bb]
        if k == "wup":
            n = b[2] * 128
            return [sv[:, 0:cfg.NDC * 2 * n].re("p (c g n) -> p c g n", g=2, n=n)]
        if k == "wdown":
            return [sv[:, 0:b[3] * 512].re("p (c n) -> p c n", n=512)]

    def _load(s, i):
        cfg, P = s.cfg, s.P
        slot = s.slots[i % NSLOT]
        l, b = s.specs[i]
        bi = i % len(s.per)
        sv = slot.v()
        if b[0] == "wout":
            ma, mb = cfg.RH * 256, 3 * cfg.NJ * 256
            P.dma(sv[0:64, 0:ma], s.region(l, bi, 0), q="sp")
            P.dma(sv[:, ma:ma + mb], s.region(l, bi, 1), q="sp")
        else:
            m = s.regs[bi][0][2]
            P.dma(sv[:, 0:m], s.region(l, bi, 0), q="sp")

    def get(s, kind):
        i = s.next
        s.next += 1
        while s.issued < min(len(s.specs), i + NSLOT):
            s._load(s.issued)
            s.issued += 1
        l, b = s.specs[i]
        assert b[0] == kind, (b, kind)
        return s.views(s.slots[i % NSLOT], s.specs[i]), b


def la_chunk(P, G, S, Cc, qv, kv, vv, gamma, H, av=None, bv=None, prodv=None):
    NH, NJ, NJV, DK, DV, HPT = G["NH"], G["NJ"], G["NJV"], G["DK"], G["DV"], G["HPT"]
    R = NH * Cc
    delta = av is not None
    XX, VX, HS = S["XX"], S["VX"], S["HS"]
    ident = G["ident"]
    mstrT, minclT, mstr = G[f"mstrT{Cc}"], G[f"minclT{Cc}"], G[f"mstr{Cc}"]
    hm, hmv = G["hm"], G["hmv"]

    def expand(x, src):
        out = XX[:, :, x, 0:R].re("p j (h t) -> p j h t", t=Cc)
        P.tt(out, src.us(2).bc([128, NJ, NH, Cc]), hm.us(3).bc([128, NJ, NH, Cc]), ALU.mult)

    if delta:
        expand(0, bv)
        expand(2, av)
        if prodv is not None:
            expand(4, prodv)
    expand(1, qv)
    expand(3, kv)
    outv = VX[:, :, 0:R].re("p j (h t) -> p j h t", t=Cc)
    P.tt(outv, vv.us(2).bc([128, NJV, NH, Cc]), hmv.us(3).bc([128, NJV, NH, Cc]), ALU.mult)
    psf = P.ps()
    for j in range(NJ):
        P.mm(psf[0:R, 0:DK], XX[:, j, 3, 0:R], G["fold_k"], start=(j == 0), stop=(j == NJ - 1))
    if delta:
        for j in range(NJ):
            P.mm(psf[0:R, DK:2 * DK], XX[:, j, 2, 0:R], G["fold_k"], start=(j == 0), stop=(j == NJ - 1))
    for j in range(NJV):
        P.mm(psf[0:R, 2 * DK:2 * DK + DV], VX[:, j, 0:R], G["fold_v"], start=(j == 0), stop=(j == NJV - 1))
    if delta:
        P.copy(HS[0:R, 0:2 * DK + DV], psf[0:R, 0:2 * DK + DV], eng="act")
    else:
        P.copy(HS[0:R, 0:DK], psf[0:R, 0:DK], eng="act")
        P.copy(HS[0:R, 2 * DK:2 * DK + DV], psf[0:R, 2 * DK:2 * DK + DV], eng="act")
    K_hs, A_hs, V_hs = HS[0:R, 0:DK], HS[0:R, DK:2 * DK], HS[0:R, 2 * DK:2 * DK + DV]
    psK = P.ps()
    RKT, NT, MT, RAT, Mm = S["RKT"], S["NT"], S["MT"], S["RAT"], S["Mm"]
    if delta:
        pk = psK[0:R, 0:256].re("p (x r) -> p x r", x=2)[:, :, 0:R]
        if R == 128:
            for j in range(NJ):
                P.mm(psK[:, 0:256], XX[:, j, 3, :], XX[:, j, 0:2, :].re("p x r -> p (x r)"), start=(j == 0),
                     stop=(j == NJ - 1))
        else:
            for xi in range(2):
                for j in range(NJ):
                    P.mm(pk[:, xi, :], XX[:, j, 3, 0:R], XX[:, j, xi, 0:R], start=(j == 0), stop=(j == NJ - 1))
        P.tt(NT[0:R, 0:R], pk[:, 0, :], mstrT[0:R, 0:R], ALU.mult)
        P.tt(RKT[0:R, 0:R], pk[:, 1, :], minclT[0:R, 0:R], ALU.mult)
        psA = P.ps()
        pa = psA[0:R, 0:256].re("p (x r) -> p x r", x=2)[:, :, 0:R]
        if R == 128:
            for j in range(NJ):
                P.mm(psA[:, 0:256], XX[:, j, 2, :], XX[:, j, 0:2, :].re("p x r -> p (x r)"), start=(j == 0),
                     stop=(j == NJ - 1))
        else:
            for xi in range(2):
                for j in range(NJ):
                    P.mm(pa[:, xi, :], XX[:, j, 2, 0:R], XX[:, j, xi, 0:R], start=(j == 0), stop=(j == NJ - 1))
        P.tt(MT[0:R, 0:R], pa[:, 0, :], mstrT[0:R, 0:R], ALU.mult)
        P.tt(RAT[0:R, 0:R], pa[:, 1, :], minclT[0:R, 0:R], ALU.mult)
        psM = P.ps()
        for j in range(NJ):
            P.mm(psM[0:R, 0:R], XX[:, j, 0, 0:R], XX[:, j, 2, 0:R], start=(j == 0), stop=(j == NJ - 1))
        P.tt(Mm[0:R, 0:R], psM[0:R, 0:R], mstr[0:R, 0:R], ALU.mult)
        levels = int(round(math.log2(Cc))) - 1
        cur, curT = Mm, MT
        pows = []
        for lev in range(levels):
            p1 = P.ps()
            P.mm(p1[0:R, 0:R], curT[0:R, 0:R], cur[0:R, 0:R])
            P.copy(S["Mp"][lev][0:R, 0:R], p1[0:R, 0:R], eng="act")
            if lev < levels - 1:
                p2 = P.ps()
                P.mm(p2[0:R, 0:R], cur[0:R, 0:R], curT[0:R, 0:R])
                P.copy(S["MpT"][lev][0:R, 0:R], p2[0:R, 0:R], eng="act")
            pows.append(S["Mp"][lev])
            cur, curT = S["Mp"][lev], S["MpT"][lev]
        Z = S["Z"][0]
        P.tt(Z[0:R, 0:R], ident[0:R, 0:R], MT[0:R, 0:R], ALU.subtract)
        for lev in range(levels):
            pz = P.ps()
            P.mm(pz[0:R, 0:R], ident[0:R, 0:R], Z[0:R, 0:R], start=True, stop=False)
            P.mm(pz[0:R, 0:R], pows[lev][0:R, 0:R], Z[0:R, 0:R], start=False, stop=True)
            Zn = S["Z"][(lev + 1) % 2]
            P.copy(Zn[0:R, 0:R], pz[0:R, 0:R], eng="act")
            Z = Zn
        TinvT = Z
        pn_ = P.ps()
        P.mm(pn_[0:R, 0:DV], NT[0:R, 0:R], V_hs)
        P.copy(S["NV"][0:R, 0:DV], pn_[0:R, 0:DV], eng="act")
        pw = P.ps()
        for j in range(NJ):
            P.mm(pw[0:R, 0:DV], XX[:, j, 0, 0:R], H[:, j, :], start=(j == 0), stop=(j == NJ - 1))
        P.tt(S["W"][0:R, 0:DV], pw[0:R, 0:DV], S["NV"][0:R, 0:DV], ALU.add)
        pu = P.ps()
        P.mm(pu[0:R, 0:DV], TinvT[0:R, 0:R], S["W"][0:R, 0:DV])
        P.act(S["nU"][0:R, 0:DV], pu[0:R, 0:DV], AF.Copy, scale=-1.0)
        nU = S["nU"][0:R, 0:DV]
    else:
        for j in range(NJ):
            P.mm(psK[0:R, 0:R], XX[:, j, 3, 0:R], XX[:, j, 1, 0:R], start=(j == 0), stop=(j == NJ - 1))
        P.tt(RKT[0:R, 0:R], psK[0:R, 0:R], minclT[0:R, 0:R], ALU.mult)
    psY = P.ps()
    for j in range(NJ):
        P.mm(psY[0:R, 0:DV], XX[:, j, 1, 0:R], H[:, j, :], start=(j == 0), stop=False)
    if delta:
        P.mm(psY[0:R, 0:DV], RAT[0:R, 0:R], nU, start=False, stop=False)
    P.mm(psY[0:R, 0:DV], RKT[0:R, 0:R], V_hs, start=False, stop=True)
    psS = None
    if prodv is not None:
        psS = P.ps()
        for j in range(NJ):
            P.mm(psS[0:R, 0:1], XX[:, j, 4, 0:R], G["ones"][:, 0:1], start=(j == 0), stop=(j == NJ - 1))
    cm = G[f"cm{Cc}"]
    K2, A2 = S["K2"], S["A2"]
    P.tt(K2[0:R], K_hs.us(1).us(1).bc([R, NJ, HPT, DK]), cm[0:R].us(3).bc([R, NJ, HPT, DK]), ALU.mult)
    if delta:
        P.tt(A2[0:R], A_hs.us(1).us(1).bc([R, NJ, HPT, DK]), cm[0:R].us(3).bc([R, NJ, HPT, DK]), ALU.mult)
    psH = P.ps()
    for j in range(NJ):
        o = psH[:, j * DV:(j + 1) * DV]
        if delta:
            P.mm(o, A2[0:R, j].re("p h k -> p (h k)"), nU, start=True, stop=False)
        P.mm(o, K2[0:R, j].re("p h k -> p (h k)"), V_hs, start=(not delta), stop=True)
    P.tt(S["tH"].v(), psH[:, 0:NJ * DV].re("p (j v) -> p j v", v=DV), H.v(), ALU.add)
    P.tt(H.v(), S["tH"].v(), gamma.us(2).bc([128, NJ, DV]), ALU.mult)
    return psY[0:R, 0:DV], V_hs, psS


MAGIC = 12582912.0
TWO_PI = 2.0 * math.pi


def build_program(cfg, debug=False):
    nc = bass.Bass("TRN2", target_bir_lowering=False)
    L, D, GW, NJ, NDC, NR, RH, HH, GH, GQT, S5C, NFT, NS, C = (cfg.L, cfg.D, cfg.GW, cfg.NJ, cfg.NDC, cfg.NR,
                                                             cfg.RH, cfg.HH, cfg.GH, cfg.GQT, cfg.S5C, cfg.NFT,
                                                             cfg.NS, cfg.C)
    TMAX = cfg.TMAX
    di = {}

    def din(name, shape):
        di[name] = nc.dram_tensor(name, [int(x) for x in shape], F32, kind="ExternalInput").ap()

    coff, NCONST = const_layout(cfg)
    ntile = len(cfg.tiles)
    din("xin", [D, cfg.NTOK])
    din("w_in", [L, D, cfg.NCT * 128])
    din("w_out", [L, D, D])
    din("w_up", [L, D, 2 * cfg.DFF])
    din("w_down", [L, cfg.DFF, D])
    din("vecs", [L, 128, cfg.NVEC])
    din("normf", [128, NDC])
    din("rw_wa", [L, 128, GW])
    din("rw_gup", [L, 128, GW])
    for Cc in (C, TS):
        din(f"rw_ln{Cc}", [L, RH * Cc, 64])
    for Cc in cfg.CCS:
        din(f"hg_n{Cc}", [L, HH * Cc, 128])
        din(f"gl_n{Cc}", [L, GH * Cc, 128])
    din("gk_up", [L, 16, GW // 2])
    G5 = GW // 16
    din("s5_Bt_re", [L, G5, 16, 64])
    din("s5_Bt_im", [L, G5, 16, 64])
    din("s5_Ct_re", [L, G5, 64, 16])
    din("s5_Ct_im", [L, G5, 64, 16])
    din("s5_wglu", [L, GW, GW])
    din("consts", [128, NCONST])
    din("tmask", [ntile, 4, 128, TMAX])
    din("st_rwkv", [L, NS, GW, 64])
    din("st_shift", [L, 128, NR, NS])
    din("st_s5re", [L, 128, S5C, NS])
    din("st_s5im", [L, 128, S5C, NS])
    din("st_hgrn", [L, NS, GW, 128])
    din("st_gla", [L, NS, GW // 2, 128])
    din("st_conv", [L, 128, NFT, NS, 2])
    do = {}
    for k, shp in OUT_SPECS(cfg).items():
        do[k] = nc.dram_tensor(k, [int(x) for x in shp], F32, kind="ExternalOutput").ap()

    words = nc.sbuf_bytes_remaining // 4 - 64
    P = Prog(nc, words)
    consts = P.alloc("consts", [NCONST])
    cv = lambda k: consts[:, coff[k][0]:coff[k][0] + coff[k][1]]
    ident, ones_f, blockones, f64 = cv("ident"), cv("ones"), cv("blockones"), cv("f64")
    ones_bf = P.alloc("ones_bf", [128], BF16)
    vecs = P.alloc("vecs", [L, cfg.NVEC])
    normf = P.alloc("normf", [NDC])
    lb = P.alloc("lb", [L, NJ])
    oml = P.alloc("oml", [L, NJ])
    tmask = P.alloc("tmask", [4, TMAX])
    x = P.alloc("x", [NDC, TMAX])
    h = P.alloc("h", [NDC, TMAX], BF16)
    Hr = [P.alloc(f"Hr{l}", [NJ, 64]) for l in range(L)]
    Hh = [P.alloc(f"Hh{l}", [NJ, 128]) for l in range(L)]
    Hg = [P.alloc(f"Hg{l}", [GQT, 128]) for l in range(L)]
    s5c = [[P.alloc(f"s5c{l}{i}", [S5C]) for i in range(2)] for l in range(L)]
    convc = [P.alloc(f"convc{l}", [NFT, 2]) for l in range(L)]
    shiftc = [P.alloc(f"shiftc{l}", [NR]) for l in range(L)]
    order = [(ti, l) for ti in range(ntile) for l in range(L)]
    WS = WStream(P, cfg, di, order, nc)
    phase_base = P.off

    def vc(l, name, j=None):
        o, n = cfg.vec[name]
        if j is None:
            return vecs[:, l, o:o + n]
        return vecs[:, l, o + j:o + j + 1]

    def geom(NH, NJk, HPT, NJV, DV, hmk, hmv, cmk):
        g = {"NH": NH, "NJ": NJk, "NJV": NJV, "DK": 128 // HPT, "DV": DV, "HPT": HPT, "ident": ident,
             "ones": ones_f,
             "hm": cv(hmk).re("p (j h) -> p j h", h=NH), "hmv": cv(hmv).re("p (j h) -> p j h", h=NH),
             "fold_k": f64 if HPT == 2 else ident, "fold_v": f64 if DV == 64 else ident}
        for Cc in cfg.CCS:
            g[f"mstrT{Cc}"], g[f"minclT{Cc}"], g[f"mstr{Cc}"] = cv(f"mstrT{Cc}"), cv(f"minclT{Cc}"), cv(f"mstr{Cc}")
            g[f"cm{Cc}"] = cv(f"{cmk}{Cc}").re("p (j h) -> p j h", h=HPT)
        return g
    G_r = geom(RH, NJ, 2, NJ, 64, "hm_r", "hm_r", "cm_r")
    G_g = geom(GH, GQT, 2, NJ, 128, "hm_g", "hm_h", "cm_g")
    G_h = geom(HH, NJ, 1, NJ, 128, "hm_h", "hm_h", "cm_h")

    def la_scratch(G, delta):
        NJk, NJV, DK, DV, HPT = G["NJ"], G["NJV"], G["DK"], G["DV"], G["HPT"]
        S = {"XX": P.alloc("XX", [NJk, 5 if delta else 4, 128]), "VX": P.alloc("VX", [NJV, 128]),
             "HS": P.alloc("HS", [2 * DK + DV]), "RKT": P.alloc("RKT", [128]),
             "K2": P.alloc("K2", [NJk, HPT, DK]), "tH": P.alloc("tH", [NJk, DV]), "NT": None, "MT": None, "RAT": None, "Mm": None, "A2": None}
        if delta:
            for k in ("NT", "MT", "RAT", "Mm"):
                S[k] = P.alloc(k, [128])
            S["A2"] = P.alloc("A2", [NJk, HPT, DK])
            S["Mp"] = [P.alloc(f"Mp{i}", [128]) for i in range(3)]
            S["MpT"] = [P.alloc(f"MpT{i}", [128]) for i in range(3)]
            S["Z"] = [P.alloc(f"Z{i}", [128]) for i in range(2)]
            S["NV"], S["W"], S["nU"] = P.alloc("NV", [DV]), P.alloc("W", [DV]), P.alloc("nU", [DV])
        return S

    P.dma(consts.v(), di["consts"][:, :])
    P.dma(vecs.v(), di["vecs"].rearrange("l p n -> p l n"))
    P.dma(normf.v(), di["normf"][:, :])
    P.memset(ones_bf.v(), 1.0)
    for l_ in range(L):
        WS.convert_layer(l_)
    for l in range(L):
        for b in (Hr[l], Hh[l], Hg[l], s5c[l][0], s5c[l][1], convc[l], shiftc[l]):
            P.memset(b.v(), 0.0)
    P.off = phase_base
    ee = P.alloc("lb_e", [L, NJ])
    se = P.alloc("lb_s", [NJ])
    for l in range(L):
        P.act(ee[:, l, :], vc(l, "hlb"), AF.Exp)
    P.copy(se.v(), ee[:, 0, :])
    for l in range(1, L):
        P.tt(se.v(), se.v(), ee[:, l, :], ALU.add)
    P.recip(se.v(), se.v())
    P.memset(lb[:, 0, :], 0.0)
    for l in range(1, L):
        P.tt(ee[:, l, :], ee[:, l, :], se.v(), ALU.mult)
        P.tt(lb[:, l, :], lb[:, l - 1, :], ee[:, l, :], ALU.add)
    P.ts(oml.v(), lb.v(), -1.0, 1.0, ALU.mult, ALU.add)

    def rmsnorm(T, gname, l, out_bf, gvec=None):
        sq = [P.alloc(f"nsq{i}", [TMAX], BF16) for i in range(2)]
        rstd = P.alloc("rstd", [TMAX])
        ps = P.ps()
        for c in range(NDC):
            s_ = sq[c % 2]
            P.act(s_[:, 0:T], x[:, c, 0:T], AF.Square)
            P.mm(ps[:, 0:T], ones_bf.v(), s_[:, 0:T], start=(c == 0), stop=(c == NDC - 1))
        P.act(rstd[:, 0:T], ps[:, 0:T], AF.Sqrt, bias=EPS, scale=1.0 / D)
        P.recip(rstd[:, 0:T], rstd[:, 0:T])
        for c in range(NDC):
            g = gvec[:, c:c + 1] if gvec is not None else vc(l, gname, c)
            P.stt(out_bf[:, c, 0:T], x[:, c, 0:T], g, rstd[:, 0:T], ALU.mult, ALU.mult)

    def project(T, ct0, nct, pt):
        done = 0
        while done < nct:
            (wv,), b = WS.get("win")
            assert b[1] == ct0 + done
            for i in range(b[2]):
                ps = P.ps()
                for c in range(NDC):
                    P.mm(ps[:, 0:T], wv[:, c, i * 128:(i + 1) * 128], h[:, c, 0:T], start=(c == 0),
                         stop=(c == NDC - 1))
                P.copy(pt[:, done + i, 0:T], ps[:, 0:T], eng="act")
            done += b[2]

    def chunks_of(pn, ns, CC=None):
        CC = CC or C
        ch = [("p", c0, min(CC, pn - c0), None) for c0 in range(0, pn, CC)]
        ch += [("s", pn + q * TS, TS, q) for q in range(ns)]
        return ch

    def range_reduce(out, in_, tmp):
        P.ts(tmp, in_, 1.0 / TWO_PI, MAGIC, ALU.mult, ALU.add)
        P.ts(tmp, tmp, -MAGIC, None, ALU.add)
        P.stt(out, tmp, -TWO_PI, in_, ALU.mult, ALU.add)

    cmk = lambda T: tmask[:, 2, 0:T]
    cmk2 = lambda T: tmask[:, 3, 0:T]
    smk = lambda T: tmask[:, 0, 0:T]
    posv = lambda T: tmask[:, 1, 0:T]

    def rwkv(ti, l, T, pn, ns, yTr):
        last = ti == ntile - 1
        P.off = mix_base
        pt = P.alloc("pt_r", [NR, TMAX])
        project(T, 0, NR, pt)
        wa = P.alloc("rw_wa", [GW])
        gup = P.alloc("rw_gup", [GW])
        ln = {C: P.alloc("rw_ln", [64])}
        P.dma(wa.v(), di["rw_wa"][l])
        P.dma(gup.v(), di["rw_gup"][l])
        P.dma(ln[C][0:RH * C], di[f"rw_ln{C}"][l])
        if ns:
            ln[TS] = P.alloc("rw_ln4", [64])
            P.dma(ln[TS][0:RH * TS], di[f"rw_ln{TS}"][l])
            stsh = P.alloc("stsh", [NR, NS])
            osh = P.alloc("osh", [NR, NS])
            P.dma(stsh.v(), di["st_shift"][l])
        d = P.alloc("rw_d", [TMAX])
        for ct in range(NR):
            p = pt[:, ct, :]
            if pn > 1:
                P.tt(d[:, 1:pn], p[:, 0:pn - 1], p[:, 1:pn], ALU.subtract)
            P.tt(d[:, 0:1], shiftc[l][:, ct:ct + 1], p[:, 0:1], ALU.subtract)
            if ns:
                pv = p[:, pn:T].re("p (s t) -> p s t", t=TS)
                dv = d[:, pn:T].re("p (s t) -> p s t", t=TS)
                P.tt(dv[:, :, 1:TS], pv[:, :, 0:TS - 1], pv[:, :, 1:TS], ALU.subtract)
                P.tt(dv[:, :, 0:1], stsh[:, ct, :].us(2), pv[:, :, 0:1], ALU.subtract)
                P.copy(osh[:, ct, :].us(2), pv[:, :, TS - 1:TS])
            P.copy(shiftc[l][:, ct:ct + 1], p[:, pn - 1:pn])
            P.stt(p[:, 0:T], d[:, 0:T], vc(l, "mu", ct), p[:, 0:T], ALU.mult, ALU.add)
        if ns:
            P.dma(do["o_shift_s"][l], osh.v(), is_out=True)
        if last:
            P.dma(do["o_shift_p"][l], shiftc[l].v(), is_out=True)
        rT = pt[:, cfg.ct_r:cfg.ct_r + NJ, :]
        kT = pt[:, cfg.ct_k:cfg.ct_k + NJ, :]
        vT = pt[:, cfg.ct_v:cfg.ct_v + NJ, :]
        pwa = pt[:, cfg.ct_wa, :]
        B = [P.alloc(f"rwB{i}", [NJ, TMAX]) for i in range(6)]
        tw = P.alloc("rw_tw", [TMAX])
        sgl = P.alloc("rw_sgl", [TMAX])
        gT = P.alloc("rw_gT", [RH, TMAX])
        P.act(tw[0:64, 0:T], pwa[0:64, 0:T], AF.Tanh)
        P.act(sgl[:, 0:T], pt[:, cfg.ct_gl, 0:T], AF.Sigmoid)
        sg, cs, t3, a_, kk, ka = B
        for j in range(NJ):
            ps = P.ps()
            P.mm(ps[:, 0:T], wa[0:64, j * 128:(j + 1) * 128], tw[0:64, 0:T])
            P.act(sg[:, j, 0:T], ps[:, 0:T], AF.Sigmoid, bias=vc(l, "w0", j))
            ps2 = P.ps()
            P.mm(ps2[:, 0:T], wa[64:128, j * 128:(j + 1) * 128], pwa[64:128, 0:T])
            P.act(a_[:, j, 0:T], ps2[:, 0:T], AF.Sigmoid, bias=vc(l, "a0", j))
            P.scan(cs[:, j, 0:T], cmk(T), sg[:, j, 0:T], 0.0)
        for hd in range(RH):
            ps = P.ps()
            P.mm(ps[0:64, 0:T], gup[:, hd * 64:(hd + 1) * 64], sgl[:, 0:T])
            P.copy(gT[0:64, hd, 0:T], ps[0:64, 0:T], eng="act")
        P.tt(t3[:, :, 0:T], cs[:, :, 0:T], sg[:, :, 0:T], ALU.subtract)
        P.act(t3[:, :, 0:T], t3[:, :, 0:T], AF.Exp, scale=-RWKV_DECAY_SCALE)
        eb, enb = sg, P.alloc("rw_enb", [NJ, TMAX])
        P.act(eb[:, :, 0:T], cs[:, :, 0:T], AF.Exp, scale=-RWKV_DECAY_SCALE)
        P.act(enb[:, :, 0:T], cs[:, :, 0:T], AF.Exp, scale=RWKV_DECAY_SCALE)
        prod = cs
        sq = P.alloc("rw_sq", [TMAX])
        nrm = P.alloc("rw_nrm", [TMAX])
        for j in range(NJ):
            P.ts(kk[:, j, 0:T], kT[:, j, 0:T], vc(l, "k_k", j), None, ALU.mult)
            P.act(sq[:, 0:T], kk[:, j, 0:T], AF.Square)
            ps = P.ps()
            P.mm(ps[:, 0:T], blockones, sq[:, 0:T])
            P.act(nrm[:, 0:T], ps[:, 0:T], AF.Sqrt)
            P.ts(nrm[:, 0:T], nrm[:, 0:T], 1e-12, None, ALU.max)
            P.recip(nrm[:, 0:T], nrm[:, 0:T])
            P.tt(kk[:, j, 0:T], kk[:, j, 0:T], nrm[:, 0:T], ALU.mult)
            P.ts(sq[:, 0:T], a_[:, j, 0:T], -1.0, vc(l, "k_a", j), ALU.add, ALU.mult)
            P.stt(kT[:, j, 0:T], sq[:, 0:T], 1.0, kT[:, j, 0:T], ALU.add, ALU.mult)
            P.tt(ka[:, j, 0:T], kk[:, j, 0:T], a_[:, j, 0:T], ALU.mult)
            P.stt(prod[:, j, 0:T], rT[:, j, 0:T], vc(l, "r_k", j), kT[:, j, 0:T], ALU.mult, ALU.mult)
        P.tt(ka[:, :, 0:T], ka[:, :, 0:T], enb[:, :, 0:T], ALU.mult)
        P.tt(kT[:, :, 0:T], kT[:, :, 0:T], enb[:, :, 0:T], ALU.mult)
        P.tt(kk[:, :, 0:T], kk[:, :, 0:T], t3[:, :, 0:T], ALU.mult)
        P.tt(rT[:, :, 0:T], rT[:, :, 0:T], eb[:, :, 0:T], ALU.mult)
        S = la_scratch(G_r, True)
        ysb = P.alloc("rw_y", [64])
        yc = P.alloc("rw_yc", [64])
        st1 = P.alloc("rw_st", [8])
        Hs = [P.alloc(f"rw_Hs{i}", [NJ, 64]) for i in range(2)]
        for (kind, c0, Cc, q) in chunks_of(pn, ns):
            R = RH * Cc
            if kind == "p":
                Hst = Hr[l]
            else:
                Hst = Hs[q % 2]
                P.dma(Hst.v(), di["st_rwkv"][l, q].rearrange("(j p) v -> p j v", p=128))
            sl = slice(c0, c0 + Cc)
            psY, V_hs, psS = la_chunk(P, G_r, S, Cc, rT[:, :, sl], kT[:, :, sl], vT[:, :, sl],
                                      eb[:, :, c0 + Cc - 1], Hst, av=ka[:, :, sl], bv=kk[:, :, sl],
                                      prodv=prod[:, :, sl])
            if kind == "s":
                P.dma(do["o_rwkv_s"][l, q].rearrange("(j p) v -> p j v", p=128), Hst.v(), is_out=True)
            P.copy(ysb[0:R, :], psY, eng="act")
            P.rsum(st1[0:R, 0:1], ysb[0:R, :])
            P.ts(st1[0:R, 0:1], st1[0:R, 0:1], -1.0 / 64, None, ALU.mult)
            P.ts(yc[0:R, :], ysb[0:R, :], st1[0:R, 0:1], None, ALU.add)
            P.tt(ysb[0:R, :], yc[0:R, :], yc[0:R, :], ALU.mult)
            P.rsum(st1[0:R, 1:2], ysb[0:R, :])
            P.act(st1[0:R, 1:2], st1[0:R, 1:2], AF.Sqrt, bias=RWKV_GN_EPS, scale=1.0 / 64)
            P.recip(st1[0:R, 1:2], st1[0:R, 1:2])
            P.stt(yc[0:R, :], yc[0:R, :], st1[0:R, 1:2], ln[Cc][0:R, :], ALU.mult, ALU.mult)
            P.copy(st1[0:R, 2:3], psS[0:R, 0:1], eng="act")
            P.stt(yc[0:R, :], V_hs, st1[0:R, 2:3], yc[0:R, :], ALU.mult, ALU.add)
            pT = P.ps()
            P.transpose(pT[0:64, 0:R], yc[0:R, :], ident[0:R, 0:R])
            P.tt(yTr[0:64, :, sl], pT[0:64, 0:R].re("p (h t) -> p h t", t=Cc), gT[0:64, :, sl], ALU.mult)
        if last:
            P.dma(do["o_rwkv_p"][l].rearrange("(j p) v -> p j v", p=128), Hr[l].v(), is_out=True)

    def la_tail(G, S, l, T, pn, ns, qT, kT, vT, eb, Hp, st_key, out_s, out_p, ng, sgate, oT, last):
        NH = G["NH"]
        NJk = G["NJ"]
        Hs = [P.alloc(f"la_Hs{i}", [NJk, 128]) for i in range(2)]
        ysb = P.alloc("la_y", [128])
        st1 = P.alloc("la_st", [4])
        for (kind, c0, Cc, q) in chunks_of(pn, ns, cfg.CL):
            R = NH * Cc
            if kind == "p":
                Hst = Hp
            else:
                Hst = Hs[q % 2]
                P.dma(Hst.v(), di[st_key][l, q].rearrange("(j p) v -> p j v", p=128))
            sl = slice(c0, c0 + Cc)
            psY, V_hs, _ = la_chunk(P, G, S, Cc, qT[:, :, sl], kT[:, :, sl], vT[:, :, sl], eb[:, :, c0 + Cc - 1], Hst)
            if kind == "s":
                P.dma(do[out_s][l, q].rearrange("(j p) v -> p j v", p=128), Hst.v(), is_out=True)
            P.act(ysb[0:R, :], psY, AF.Square)
            P.rsum(st1[0:R, 0:1], ysb[0:R, :])
            P.act(st1[0:R, 0:1], st1[0:R, 0:1], AF.Sqrt, bias=EPS, scale=1.0 / 128)
            P.recip(st1[0:R, 0:1], st1[0:R, 0:1])
            P.stt(ysb[0:R, :], psY, st1[0:R, 0:1], ng[Cc][0:R, :], ALU.mult, ALU.mult)
            pT = P.ps()
            P.transpose(pT[:, 0:R], ysb[0:R, :], ident[0:R, 0:R])
            P.tt(oT[:, :, sl], pT[:, 0:R].re("p (h t) -> p h t", t=Cc), sgate[:, :, sl], ALU.mult)
        if last:
            P.dma(do[out_p][l].rearrange("(j p) v -> p j v", p=128), Hp.v(), is_out=True)

    def hgrn(ti, l, T, pn, ns, oT):
        last = ti == ntile - 1
        P.off = mix_base
        pt = P.alloc("pt_h", [4 * NJ, TMAX])
        project(T, cfg.ct_hq, 4 * NJ, pt)
        ng = {}
        for Cc in sorted(set(c_[2] for c_ in chunks_of(pn, ns, cfg.CL))):
            ng[Cc] = P.alloc(f"hg_n{Cc}", [128])
            P.dma(ng[Cc][0:HH * Cc], di[f"hg_n{Cc}"][l])
        qT, fT, iT, gT_ = (pt[:, k * NJ:(k + 1) * NJ, :] for k in range(4))
        lf = P.alloc("hg_lf", [NJ, TMAX])
        cs = P.alloc("hg_cs", [NJ, TMAX])
        eb = P.alloc("hg_eb", [NJ, TMAX])
        P.act(qT[:, :, 0:T], qT[:, :, 0:T], AF.Silu)
        P.act(gT_[:, :, 0:T], gT_[:, :, 0:T], AF.Silu)
        P.act(fT[:, :, 0:T], fT[:, :, 0:T], AF.Sigmoid, scale=-1.0)
        for j in range(NJ):
            P.ts(fT[:, j, 0:T], fT[:, j, 0:T], oml[:, l, j:j + 1], HGRN_MAX_INPUT, ALU.mult, ALU.min)
            P.act(lf[:, j, 0:T], fT[:, j, 0:T], AF.Ln, bias=1.0, scale=-1.0)
            P.scan(cs[:, j, 0:T], cmk2(T), lf[:, j, 0:T], 0.0)
        P.act(eb[:, :, 0:T], cs[:, :, 0:T], AF.Exp)
        P.act(lf[:, :, 0:T], cs[:, :, 0:T], AF.Exp, scale=-1.0)
        P.tt(qT[:, :, 0:T], qT[:, :, 0:T], eb[:, :, 0:T], ALU.mult)
        P.tt(fT[:, :, 0:T], fT[:, :, 0:T], lf[:, :, 0:T], ALU.mult)
        S = la_scratch(G_h, False)
        la_tail(G_h, S, l, T, pn, ns, qT, fT, iT, eb, Hh[l], "st_hgrn", "o_hgrn_s", "o_hgrn_p", ng, gT_, oT, last)

    def gla(ti, l, T, pn, ns, oT):
        last = ti == ntile - 1
        P.off = mix_base
        nct = cfg.NCT - cfg.ct_gq
        pt = P.alloc("pt_g", [nct, TMAX])
        project(T, cfg.ct_gq, nct, pt)
        ng = {}
        for Cc in sorted(set(c_[2] for c_ in chunks_of(pn, ns, cfg.CL))):
            ng[Cc] = P.alloc(f"gl_n{Cc}", [128])
            P.dma(ng[Cc][0:GH * Cc], di[f"gl_n{Cc}"][l])
        gku = P.alloc("gk_up", [GW // 2])
        P.dma(gku[0:16, :], di["gk_up"][l])
        negb = P.alloc("gl_negb", [GQT])
        P.ts(negb.v(), vc(l, "gk_b"), -1.0, None, ALU.mult)
        o = 0
        qT = pt[:, o:o + GQT, :]; o += GQT
        kT = pt[:, o:o + GQT, :]; o += GQT
        vT = pt[:, o:o + NJ, :]; o += NJ
        glo = pt[:, o, :]; o += 1
        gT_ = pt[:, o:o + NJ, :]
        ll = P.alloc("gl_l", [GQT, TMAX])
        cs = P.alloc("gl_cs", [GQT, TMAX])
        eb = P.alloc("gl_eb", [GQT, TMAX])
        for j in range(GQT):
            ps = P.ps()
            P.mm(ps[:, 0:T], gku[0:16, j * 128:(j + 1) * 128], glo[0:16, 0:T])
            P.act(ll[:, j, 0:T], ps[:, 0:T], AF.Exp, bias=negb[:, j:j + 1], scale=-1.0)
            P.act(ll[:, j, 0:T], ll[:, j, 0:T], AF.Ln, bias=1.0)
            P.scan(cs[:, j, 0:T], cmk2(T), ll[:, j, 0:T], 0.0)
        P.act(gT_[:, :, 0:T], gT_[:, :, 0:T], AF.Silu)
        P.act(eb[:, :, 0:T], cs[:, :, 0:T], AF.Exp, scale=-1.0 / 16)
        P.act(ll[:, :, 0:T], cs[:, :, 0:T], AF.Exp, scale=1.0 / 16)
        P.stt(qT[:, :, 0:T], qT[:, :, 0:T], 0.125, eb[:, :, 0:T], ALU.mult, ALU.mult)
        P.tt(kT[:, :, 0:T], kT[:, :, 0:T], ll[:, :, 0:T], ALU.mult)
        S = la_scratch(G_g, False)
        la_tail(G_g, S, l, T, pn, ns, qT, kT, vT, eb, Hg[l], "st_gla", "o_gla_s", "o_gla_p", ng, gT_, oT, last)

    def s5(ti, l, T, pn, ns, oT):
        last = ti == ntile - 1
        P.off = mix_base
        pt = P.alloc("pt_s", [NJ, TMAX])
        project(T, cfg.ct_s5, NJ, pt)
        wgl = P.alloc("s5_wglu", [NJ, GW])
        P.dma(wgl.v(), di["s5_wglu"][l].rearrange("(c p) n -> p c n", p=128))
        BXr, BXi = P.alloc("BXr", [S5C, 128]), P.alloc("BXi", [S5C, 128])
        bbr, bbi = P.alloc("bbr", [S5C, 128]), P.alloc("bbi", [S5C, 128])
        CXr, CXi = P.alloc("CXr", [S5C, 128]), P.alloc("CXi", [S5C, 128])
        for b_ in (BXr, BXi, CXr, CXi):
            P.memset(b_.v(), 0.0)
        for b8 in range(8):
            for (dst, key) in ((BXr, "s5_Bt_re"), (BXi, "s5_Bt_im")):
                P.dma(dst[b8 * 16:(b8 + 1) * 16, (b8 // 2)::4, (b8 % 2) * 64:(b8 % 2) * 64 + 64],
                      di[key][l].rearrange("(a b) c p -> b c a p", b=8)[b8])
            for (dst, key) in ((CXr, "s5_Ct_re"), (CXi, "s5_Ct_im")):
                P.dma(dst[(b8 % 2) * 64:(b8 % 2) * 64 + 64, (b8 // 2)::4, b8 * 16:(b8 + 1) * 16],
                      di[key][l].rearrange("(a b) p c -> b p a c", b=8)[b8])
        P.ts(CXi.v(), CXi.v(), -1.0, None, ALU.mult)
        sc = [P.alloc(f"s5c_{i}", [S5C]) for i in range(12)]
        dtv, ar, th, rho, sn, cs_, abre, abim, den, core, coim, tmp = sc
        P.act(dtv.v(), vc(l, "logdt"), AF.Exp)
        P.tt(ar.v(), vc(l, "A_re"), dtv.v(), ALU.mult)
        P.tt(th.v(), vc(l, "A_im"), dtv.v(), ALU.mult)
        P.act(rho.v(), ar.v(), AF.Exp)
        range_reduce(sn.v(), th.v(), tmp.v())
        P.act(sn.v(), sn.v(), AF.Sin)
        P.ts(cs_.v(), th.v(), math.pi / 2, None, ALU.add)
        range_reduce(cs_.v(), cs_.v(), tmp.v())
        P.act(cs_.v(), cs_.v(), AF.Sin)
        P.tt(abre.v(), rho.v(), cs_.v(), ALU.mult)
        P.tt(abim.v(), rho.v(), sn.v(), ALU.mult)
        P.tt(den.v(), vc(l, "A_re"), vc(l, "A_re"), ALU.mult)
        P.tt(tmp.v(), vc(l, "A_im"), vc(l, "A_im"), ALU.mult)
        P.tt(den.v(), den.v(), tmp.v(), ALU.add)
        P.recip(den.v(), den.v())
        P.ts(abre.v(), abre.v(), -1.0, None, ALU.add)
        P.tt(core.v(), abre.v(), vc(l, "A_re"), ALU.mult)
        P.tt(tmp.v(), abim.v(), vc(l, "A_im"), ALU.mult)
        P.tt(core.v(), core.v(), tmp.v(), ALU.add)
        P.tt(core.v(), core.v(), den.v(), ALU.mult)
        P.tt(coim.v(), abim.v(), vc(l, "A_re"), ALU.mult)
        P.tt(tmp.v(), abre.v(), vc(l, "A_im"), ALU.mult)
        P.tt(coim.v(), coim.v(), tmp.v(), ALU.subtract)
        P.tt(coim.v(), coim.v(), den.v(), ALU.mult)
        dg = [P.alloc(f"s5dg{i}", [128]) for i in range(2)]
        t1, t2 = P.alloc("s5t1", [128]), P.alloc("s5t2", [128])
        for oc in range(S5C):
            P.ts(dg[0].v(), ident, core[:, oc:oc + 1], None, ALU.mult)
            P.ts(dg[1].v(), ident, coim[:, oc:oc + 1], None, ALU.mult)
            pr, pi_ = P.ps(), P.ps()
            P.mm(pr[:, 0:128], ones_f, dg[0].v())
            P.mm(pi_[:, 0:128], ones_f, dg[1].v())
            P.tt(t1.v(), BXr[:, oc, :], pr[:, 0:128], ALU.mult)
            P.tt(t2.v(), BXi[:, oc, :], pi_[:, 0:128], ALU.mult)
            P.tt(bbr[:, oc, :], t1.v(), t2.v(), ALU.subtract)
            P.tt(t1.v(), BXi[:, oc, :], pr[:, 0:128], ALU.mult)
            P.tt(t2.v(), BXr[:, oc, :], pi_[:, 0:128], ALU.mult)
            P.tt(bbi[:, oc, :], t1.v(), t2.v(), ALU.add)
        if ns:
            sts = [P.alloc(f"s5st{i}", [S5C, NS]) for i in range(2)]
            osts = [P.alloc(f"s5ost{i}", [S5C, NS]) for i in range(2)]
            P.dma(sts[0].v(), di["st_s5re"][l])
            P.dma(sts[1].v(), di["st_s5im"][l])
        W = [P.alloc(f"s5w{i}", [TMAX]) for i in range(9)]
        a1, snt, cst, zre, zim, dec, u1, u2, u3 = W
        hb = P.alloc("s5hb", [4, 2, TMAX])
        ygl = P.alloc("s5ygl", [NJ, TMAX])
        for oc in range(S5C):
            u = pt[:, oc // 4, 0:T]
            pbr, pbi = P.ps(), P.ps()
            P.mm(pbr[:, 0:T], bbr[:, oc, :], u)
            P.mm(pbi[:, 0:T], bbi[:, oc, :], u)
            P.ts(a1[:, 0:T], posv(T), th[:, oc:oc + 1], None, ALU.mult)
            range_reduce(snt[:, 0:T], a1[:, 0:T], u1[:, 0:T])
            P.act(snt[:, 0:T], snt[:, 0:T], AF.Sin)
            P.ts(a1[:, 0:T], a1[:, 0:T], math.pi / 2, None, ALU.add)
            range_reduce(cst[:, 0:T], a1[:, 0:T], u1[:, 0:T])
            P.act(cst[:, 0:T], cst[:, 0:T], AF.Sin)
            P.tt(u1[:, 0:T], cst[:, 0:T], pbr[:, 0:T], ALU.mult)
            P.tt(u2[:, 0:T], snt[:, 0:T], pbi[:, 0:T], ALU.mult)
            P.tt(zre[:, 0:T], u1[:, 0:T], u2[:, 0:T], ALU.add)
            P.tt(u1[:, 0:T], cst[:, 0:T], pbi[:, 0:T], ALU.mult)
            P.tt(u2[:, 0:T], snt[:, 0:T], pbr[:, 0:T], ALU.mult)
            P.tt(zim[:, 0:T], u1[:, 0:T], u2[:, 0:T], ALU.subtract)
            P.ts(dec[:, 0:T], smk(T), rho[:, oc:oc + 1], None, ALU.mult)
            for i, z in enumerate((zre, zim)):
                P.stt(z[:, 0:1], s5c[l][i][:, oc:oc + 1], rho[:, oc:oc + 1], z[:, 0:1], ALU.mult, ALU.add)
                if ns:
                    zv = z[:, pn:T].re("p (s t) -> p s t", t=TS)[:, :, 0:1]
                    P.stt(zv, sts[i][:, oc, :].us(2), rho[:, oc:oc + 1], zv, ALU.mult, ALU.add)
            P.scan(u1[:, 0:T], dec[:, 0:T], zre[:, 0:T], 0.0)
            P.scan(u2[:, 0:T], dec[:, 0:T], zim[:, 0:T], 0.0)
            hre, him = hb[:, oc % 4, 0, :], hb[:, oc % 4, 1, :]
            P.tt(zre[:, 0:T], cst[:, 0:T], u1[:, 0:T], ALU.mult)
            P.tt(zim[:, 0:T], snt[:, 0:T], u2[:, 0:T], ALU.mult)
            P.tt(hre[:, 0:T], zre[:, 0:T], zim[:, 0:T], ALU.subtract)
            P.tt(zre[:, 0:T], cst[:, 0:T], u2[:, 0:T], ALU.mult)
            P.tt(zim[:, 0:T], snt[:, 0:T], u1[:, 0:T], ALU.mult)
            P.tt(him[:, 0:T], zre[:, 0:T], zim[:, 0:T], ALU.add)
            for i, hh_ in enumerate((hre, him)):
                P.copy(s5c[l][i][:, oc:oc + 1], hh_[:, pn - 1:pn])
                if ns:
                    P.copy(osts[i][:, oc, :].us(2), hh_[:, pn:T].re("p (s t) -> p s t", t=TS)[:, :, TS - 1:TS])
            if oc % 4 == 3:
                ot = oc // 4
                py = P.ps()
                for i in range(4):
                    P.mm(py[:, 0:T], CXr[:, 4 * ot + i, :], hb[:, i, 0, 0:T], start=(i == 0), stop=False)
                    P.mm(py[:, 0:T], CXi[:, 4 * ot + i, :], hb[:, i, 1, 0:T], start=False, stop=(i == 3))
                P.stt(ygl[:, ot, 0:T], pt[:, ot, 0:T], vc(l, "s5_D", ot), py[:, 0:T], ALU.mult, ALU.add)
                P.act(ygl[:, ot, 0:T], ygl[:, ot, 0:T], AF.Gelu_apprx_tanh)
        for ot in range(NJ):
            ps = P.ps()
            for kt in range(NJ):
                P.mm(ps[:, 0:T], wgl[:, kt, ot * 128:(ot + 1) * 128], ygl[:, kt, 0:T], start=(kt == 0),
                     stop=(kt == NJ - 1))
            P.act(u3[:, 0:T], ps[:, 0:T], AF.Sigmoid, bias=vc(l, "s5_bglu", ot))
            P.tt(oT[:, ot, 0:T], ygl[:, ot, 0:T], u3[:, 0:T], ALU.mult)
        if ns:
            P.dma(do["o_s5re_s"][l], osts[0].v(), is_out=True)
            P.dma(do["o_s5im_s"][l], osts[1].v(), is_out=True)
        if last:
            P.dma(do["o_s5re_p"][l], s5c[l][0].v(), is_out=True)
            P.dma(do["o_s5im_p"][l], s5c[l][1].v(), is_out=True)

    def layer(ti, l, T, pn, ns):
        nonlocal mix_base
        last = ti == ntile - 1
        P.off = phase_base
        yTr = P.alloc("yTr", [RH, TMAX], BF16)
        mixB = [P.alloc(f"mixB{i}", [NJ, TMAX], BF16) for i in range(3)]
        mix_base = P.off
        rmsnorm(T, "norm_mix", l, h)
        mix_base = P.off
        rwkv(ti, l, T, pn, ns, yTr)
        s5(ti, l, T, pn, ns, mixB[0])
        hgrn(ti, l, T, pn, ns, mixB[1])
        gla(ti, l, T, pn, ns, mixB[2])
        for og in range(D // 256):
            (wa_, wb_), b = WS.get("wout")
            for i in range(2):
                oc = og * 2 + i
                ps = P.ps()
                for hd in range(RH):
                    P.mm(ps[:, 0:T], wa_[0:64, hd, i * 128:(i + 1) * 128], yTr[0:64, hd, 0:T], start=(hd == 0),
                         stop=False)
                for k in range(3 * NJ):
                    P.mm(ps[:, 0:T], wb_[:, k, i * 128:(i + 1) * 128], mixB[k // NJ][:, k % NJ, 0:T], start=False,
                         stop=(k == 3 * NJ - 1))
                P.tt(x[:, oc, 0:T], x[:, oc, 0:T], ps[:, 0:T], ALU.add)
        P.off = mix_base
        rmsnorm(T, "norm_ffn", l, h)
        aT = P.alloc("aT", [NFT, TMAX], BF16)
        uxp = [P.alloc(f"uxp{i}", [TMAX + 2]) for i in range(2)]
        acc = [P.alloc(f"acc{i}", [TMAX]) for i in range(2)]
        if ns:
            stc = P.alloc("stc", [NFT, NS, 2])
            ostc = P.alloc("ostc", [NFT, NS, 2])
            uxs = [P.alloc(f"uxs{i}", [NS, TS + 2]) for i in range(2)]
            P.dma(stc.v(), di["st_conv"][l])
        j = 0
        while j < NFT:
            (wv,), b = WS.get("wup")
            for jj in range(b[2]):
                pu, pg = P.ps(), P.ps()
                for c in range(NDC):
                    P.mm(pu[:, 0:T], wv[:, c, 0, jj * 128:(jj + 1) * 128], h[:, c, 0:T], start=(c == 0),
                         stop=(c == NDC - 1))
                for c in range(NDC):
                    P.mm(pg[:, 0:T], wv[:, c, 1, jj * 128:(jj + 1) * 128], h[:, c, 0:T], start=(c == 0),
                         stop=(c == NDC - 1))
                ux, ac = uxp[j % 2], acc[j % 2]
                w0, w1, w2, cb = (vc(l, f"conv_w{i}", j) for i in range(3)), None, None, vc(l, "conv_b", j)
                w0, w1, w2 = list(w0)
                P.copy(ux[:, 0:2], convc[l][:, j, :])
                P.copy(ux[:, 2:2 + pn], pu[:, 0:pn], eng="act")
                P.copy(convc[l][:, j, :], ux[:, pn:pn + 2])
                P.ts(ac[:, 0:pn], ux[:, 0:pn], w0, cb, ALU.mult, ALU.add)
                P.stt(ac[:, 0:pn], ux[:, 1:pn + 1], w1, ac[:, 0:pn], ALU.mult, ALU.add)
                P.stt(ac[:, 0:pn], ux[:, 2:pn + 2], w2, ac[:, 0:pn], ALU.mult, ALU.add)
                if ns:
                    us_ = uxs[j % 2]
                    acs = ac[:, pn:T].re("p (s t) -> p s t", t=TS)
                    P.copy(us_[:, :, 0:2], stc[:, j, :, :])
                    P.copy(us_[:, :, 2:2 + TS], pu[:, pn:T].re("p (s t) -> p s t", t=TS), eng="act")
                    P.copy(ostc[:, j, :, :], us_[:, :, TS:TS + 2])
                    P.ts(acs, us_[:, :, 0:TS], w0, cb, ALU.mult, ALU.add)
                    P.stt(acs, us_[:, :, 1:TS + 1], w1, acs, ALU.mult, ALU.add)
                    P.stt(acs, us_[:, :, 2:TS + 2], w2, acs, ALU.mult, ALU.add)
                P.act(ac[:, 0:T], ac[:, 0:T], AF.Gelu_apprx_tanh)
                P.tt(aT[:, j, 0:T], ac[:, 0:T], pg[:, 0:T], ALU.mult)
                j += 1
        if ns:
            P.dma(do["o_conv_s"][l], ostc.v(), is_out=True)
        if last:
            P.dma(do["o_conv_p"][l], convc[l].v(), is_out=True)
        for og in range(D // 512):
            pss = [P.ps() for _ in range(4)]
            k0 = 0
            while k0 < NFT:
                (wv,), b = WS.get("wdown")
                assert b[1] == og and b[2] == k0
                for kk_ in range(b[3]):
                    kt = k0 + kk_
                    for i in range(4):
                        P.mm(pss[i][:, 0:T], wv[:, kk_, i * 128:(i + 1) * 128], aT[:, kt, 0:T], start=(kt == 0),
                             stop=(kt == NFT - 1))
                k0 += b[3]
            for i in range(4):
                oc = og * 4 + i
                P.tt(x[:, oc, 0:T], x[:, oc, 0:T], pss[i][:, 0:T], ALU.add)

    mix_base = phase_base
    for ti, (p0, pn, ns) in enumerate(cfg.tiles):
        T = pn + ns * TS
        P.dma(x[:, :, 0:pn], di["xin"][:, p0:p0 + pn].rearrange("(c p) t -> p c t", p=128))
        if ns:
            P.dma(x[:, :, pn:T], di["xin"][:, cfg.NP:cfg.NP + ns * TS].rearrange("(c p) t -> p c t", p=128))
        P.dma(tmask.v(), di["tmask"][ti].rearrange("k p t -> p k t"))
        for l in range(L):
            layer(ti, l, T, pn, ns)
        P.off = mix_base
        yo = P.alloc("yo", [NDC, TMAX])
        sq = [P.alloc(f"fsq{i}", [TMAX], BF16) for i in range(2)]
        rstd = P.alloc("frstd", [TMAX])
        ps = P.ps()
        for c in range(NDC):
            s_ = sq[c % 2]
            P.act(s_[:, 0:T], x[:, c, 0:T], AF.Square)
            P.mm(ps[:, 0:T], ones_bf.v(), s_[:, 0:T], start=(c == 0), stop=(c == NDC - 1))
        P.act(rstd[:, 0:T], ps[:, 0:T], AF.Sqrt, bias=EPS, scale=1.0 / D)
        P.recip(rstd[:, 0:T], rstd[:, 0:T])
        for c in range(NDC):
            P.stt(yo[:, c, 0:T], x[:, c, 0:T], normf[:, c:c + 1], rstd[:, 0:T], ALU.mult, ALU.mult)
        P.dma(do["yT"][:, p0:p0 + pn].rearrange("(c p) t -> p c t", p=128), yo[:, :, 0:pn], is_out=True)
        if ns:
            P.dma(do["yT"][:, cfg.NP:cfg.NP + ns * TS].rearrange("(c p) t -> p c t", p=128), yo[:, :, pn:T],
                  is_out=True)
    P.emit()
    return nc, P


REAL = dict(D=2048, L=4, NP=2064, NS=16, TILE=256, C=16)
_cache = {}


def kernel(**inputs):
    cfg = Cfg(**REAL)
    inp = {k: np.asarray(v) for k, v in inputs.items()}
    B = inp["x_prompt"].shape[0]
    ncore = 8
    sh = pack_shared(cfg, inp)
    in_maps = []
    for c in range(ncore):
        m = dict(sh)
        m.update(pack_core(cfg, inp, c % B, c * cfg.NS))
        in_maps.append(m)
    nc, _ = build_program(cfg)
    res = run_bass_kernel_spmd(nc, in_maps, core_ids=list(range(ncore)))
    outs = [unpack_core(cfg, r) for r in res.results]
    cat = lambda k: np.concatenate([o[k] for o in outs], axis=1)
    stack_p = lambda k: np.stack([outs[b][k] for b in range(B)], axis=1)
    f = lambda a: np.ascontiguousarray(a, dtype=np.float32)
    return (f(np.stack([outs[b]["y_p"] for b in range(B)], axis=0)),
            f(np.concatenate([o["y_s"] for o in outs], axis=0)),
            f(stack_p("rwkv_p")), f(cat("rwkv_s")), f(stack_p("shift_p")), f(cat("shift_s")),
            f(stack_p("s5re_p")), f(cat("s5re_s")), f(stack_p("s5im_p")), f(cat("s5im_s")),
            f(stack_p("hgrn_p")), f(cat("hgrn_s")), f(stack_p("gla_p")), f(cat("gla_s")),
            f(stack_p("conv_p")), f(cat("conv_s")))
```

```python
import math
from contextlib import ExitStack
import numpy as np
import concourse.bass as bass
import concourse.mybir as mybir
from concourse.bass_utils import run_bass_kernel_spmd

F32 = mybir.dt.float32
BF16 = mybir.dt.bfloat16
AF = mybir.ActivationFunctionType
ALU = mybir.AluOpType
AX = mybir.AxisListType

EPS = 1e-6
RWKV_DECAY_SCALE = 0.606531
RWKV_GN_EPS = 64e-5
HGRN_MAX_INPUT = 1.0 - 1e-4
N_META = 16
TS = 4


class Cfg:
    def __init__(s, D=2048, L=4, NP=2064, NS=16, TILE=256, C=16):
        s.D, s.L, s.NP, s.NS, s.TILE, s.C = D, L, NP, NS, TILE, C
        s.CL = 32
        s.CCS = (32, 16, TS)
        s.GW = D // 4
        s.NDC = D // 128
        s.RH = s.GW // 64
        s.NJ = s.GW // 128
        s.HH = s.GW // 128
        s.GH = s.GW // 128
        s.GQT = s.GW // 256
        s.S5C = (s.GW // 16) * 64 // 128
        s.DFF = ((8 * D // 3 + 127) // 128) * 128
        s.NFT = s.DFF // 128
        NJ = s.NJ
        s.ct_r, s.ct_k, s.ct_v, s.ct_wa, s.ct_gl = 0, NJ, 2 * NJ, 3 * NJ, 3 * NJ + 1
        s.NR = 3 * NJ + 2
        s.ct_s5 = s.NR
        s.ct_hq = s.ct_s5 + NJ
        s.ct_hf, s.ct_hi, s.ct_hg = s.ct_hq + NJ, s.ct_hq + 2 * NJ, s.ct_hq + 3 * NJ
        s.ct_gq = s.ct_hq + 4 * NJ
        s.ct_gk = s.ct_gq + s.GQT
        s.ct_gv = s.ct_gk + s.GQT
        s.ct_glo = s.ct_gv + NJ
        s.ct_gg = s.ct_glo + 1
        s.NCT = s.ct_gg + NJ
        s.mix_groups = [(0, s.NR), (s.ct_s5, NJ), (s.ct_hq, 4 * NJ), (s.ct_gq, s.NCT - s.ct_gq)]
        s.tiles = []
        p = 0
        while s.NP - p > C:
            n = min(TILE, s.NP - C - p)
            s.tiles.append((p, n, 0))
            p += n
        s.tiles.append((p, s.NP - p, NS))
        s.NTOK = s.NP + NS * TS
        s.TMAX = max(max(t[1] + t[2] * TS for t in s.tiles), 1)
        v = {}
        o = 0
        for name, n in [("norm_mix", s.NDC), ("norm_ffn", s.NDC), ("mu", s.NR), ("w0", NJ), ("a0", NJ),
                        ("k_k", NJ), ("k_a", NJ), ("r_k", NJ), ("s5_D", NJ), ("s5_bglu", NJ),
                        ("gk_b", s.GQT), ("conv_w0", s.NFT), ("conv_w1", s.NFT), ("conv_w2", s.NFT),
                        ("conv_b", s.NFT), ("A_re", s.S5C), ("A_im", s.S5C), ("logdt", s.S5C),
                        ("hlb", NJ)]:
            v[name] = (o, n)
            o += n
        s.vec = v
        s.NVEC = o


def build_consts(cfg):
    c = {}
    c["ident"] = np.eye(128, dtype=np.float32)
    c["ones"] = np.ones((128, 128), np.float32)
    bo = np.zeros((128, 128), np.float32)
    bo[:64, :64] = 1
    bo[64:, 64:] = 1
    c["blockones"] = bo
    f64 = np.zeros((128, 64), np.float32)
    f64[np.arange(128), np.arange(128) % 64] = 1
    c["f64"] = f64
    for C in cfg.CCS:
        r = np.arange(128) % C
        c[f"mstrT{C}"] = (r[:, None] < r[None, :]).astype(np.float32)
        c[f"minclT{C}"] = (r[:, None] <= r[None, :]).astype(np.float32)
        c[f"mstr{C}"] = (r[:, None] > r[None, :]).astype(np.float32)
    return c


def head_mask(NJ, HPT, NH):
    DK = 128 // HPT
    m = np.zeros((128, NJ, NH), np.float32)
    for j in range(NJ):
        for p in range(128):
            h = j * HPT + p // DK
            if h < NH:
                m[p, j, h] = 1
    return m


def col_mask(C, NJ, HPT, NH):
    m = np.zeros((128, NJ, HPT), np.float32)
    for row in range(min(128, NH * C)):
        h = row // C
        m[row, h // HPT, h % HPT] = 1
    return m


class Buf:
    def __init__(s, name, ap, pages, space):
        s.name, s.ap, s.pages, s.space = name, ap, pages, space

    def __getitem__(s, k):
        return V(s.ap[k], s)

    def v(s):
        return V(s.ap, s)


class V:
    def __init__(s, ap, buf):
        s.ap, s.buf = ap, buf

    def __getitem__(s, k):
        return V(s.ap[k], s.buf)

    def re(s, pat, **kw):
        return V(s.ap.rearrange(pat, **kw), s.buf)

    def bc(s, shape):
        return V(s.ap.to_broadcast(list(shape)), s.buf)

    def us(s, axis):
        return V(s.ap.unsqueeze(axis), s.buf)

    @property
    def shape(s):
        return s.ap.shape


PAGE = 256
ENG_NAMES = ["pe", "act", "dve", "pool", "sp"]
N_DMA_SEM = {"sp": 24, "pool": 64, "act": 2}


class Op:
    __slots__ = ("eng", "fn", "deps", "is_dma", "signal", "idx", "dsem", "dval", "dprev", "is_out", "count")


class Prog:
    def __init__(s, nc, arena_words):
        s.nc = nc
        s.arena = nc.alloc_sbuf_tensor("arena", [128, arena_words], F32)
        s.A = s.arena[:, :]
        s.arena_words = arena_words
        s.off = 0
        s.ops = {e: [] for e in ENG_NAMES}
        s.last_w = {}
        s.readers = {}
        s.dma_count = {e: 0 for e in ENG_NAMES}
        s.psum = []
        for b in range(8):
            t = nc.alloc_psum_tensor(f"psb{b}", [128, 512], F32)
            s.psum.append(Buf(f"ps{b}", t[:, :], [("ps", b)], "psum"))
        s.ps_rr = 0
        s.marks = []

    def alloc(s, name, free_shape, dtype=F32, at=None):
        n = int(np.prod(free_shape))
        words = n if dtype == F32 else (n + 1) // 2
        if at is None:
            off = s.off
            s.off += words
        else:
            off = at
        assert off + words <= s.arena_words, f"arena overflow at {name}: {off + words} > {s.arena_words}"
        ap = s.A[:, off:off + words]
        if dtype != F32:
            ap = ap.bitcast(dtype)
        if len(free_shape) > 1:
            names = " ".join(f"a{i}" for i in range(len(free_shape)))
            kw = {f"a{i}": int(free_shape[i]) for i in range(1, len(free_shape))}
            ap = ap.rearrange(f"p ({names}) -> p {names}", **kw)
        pages = list(range((off * 4) // PAGE, ((off + words) * 4 - 1) // PAGE + 1))
        return Buf(name, ap, pages, "sbuf")

    def ps(s):
        b = s.psum[s.ps_rr % 8]
        s.ps_rr += 1
        return b

    def _rec(s, eng, fn, reads, writes, is_dma=False, is_out=False):
        op = Op()
        op.eng, op.fn, op.is_dma, op.signal, op.is_out = eng, fn, is_dma, False, is_out
        op.idx = len(s.ops[eng])
        deps = set()
        rp = []
        wp = []
        for b in reads:
            if b is not None:
                rp.extend(b.pages)
        for b in writes:
            if b is not None:
                wp.extend(b.pages)
        for p in rp:
            w = s.last_w.get(p)
            if w is not None:
                deps.add(w)
        for p in wp:
            w = s.last_w.get(p)
            if w is not None:
                deps.add(w)
            for r in s.readers.get(p, ()):
                deps.add(r)
        deps.discard(op)
        op.deps = deps
        for p in wp:
            s.last_w[p] = op
            s.readers[p] = []
        for p in rp:
            lst = s.readers.setdefault(p, [])
            if not is_dma:
                lst[:] = [r for r in lst if r.eng != eng or r.is_dma]
            lst.append(op)
        if is_dma:
            k = s.dma_count[eng]
            s.dma_count[eng] += 1
            nds = N_DMA_SEM[eng]
            op.dsem = k % nds
            op.dval = 16 * (k // nds + 1)
            op.dprev = 16 * (k // nds)
        s.ops[eng].append(op)
        return op

    @staticmethod
    def _b(x):
        return x.buf if isinstance(x, V) else None

    @staticmethod
    def _a(x):
        return x.ap if isinstance(x, V) else x

    def mm(s, out, lhsT, rhs, start=True, stop=True):
        o, l, r = out.ap, lhsT.ap, rhs.ap
        s._rec("pe", lambda e: e.matmul(o, l, r, start=start, stop=stop), [lhsT.buf, rhs.buf], [out.buf])

    def transpose(s, out, in_, ident):
        o, i, d = out.ap, in_.ap, ident.ap
        s._rec("pe", lambda e: e.transpose(o, i, d), [in_.buf, ident.buf], [out.buf])

    def act(s, out, in_, func, bias=None, scale=None):
        kw = {}
        rd = [in_.buf]
        if bias is not None:
            kw["bias"] = s._a(bias)
            rd.append(s._b(bias))
        if scale is not None:
            kw["scale"] = s._a(scale)
            rd.append(s._b(scale))
        o, i = out.ap, in_.ap
        s._rec("act", lambda e: e.activation(out=o, in_=i, func=func, **kw), rd, [out.buf])

    def tt(s, out, in0, in1, op, eng="dve"):
        o, a, b = out.ap, in0.ap, in1.ap
        s._rec(eng, lambda e: e.tensor_tensor(out=o, in0=a, in1=b, op=op), [in0.buf, in1.buf], [out.buf])

    def ts(s, out, in0, s1, s2=None, op0=ALU.mult, op1=None, eng="dve"):
        o, a = out.ap, in0.ap
        a1, a2 = s._a(s1), s._a(s2)
        rd = [in0.buf, s._b(s1), s._b(s2)]
        if op1 is None:
            s._rec(eng, lambda e: e.tensor_scalar(out=o, in0=a, scalar1=a1, scalar2=None, op0=op0), rd, [out.buf])
        else:
            s._rec(eng, lambda e: e.tensor_scalar(out=o, in0=a, scalar1=a1, scalar2=a2, op0=op0, op1=op1), rd,
                   [out.buf])

    def stt(s, out, in0, scalar, in1, op0, op1, eng="dve"):
        o, a, b, sc = out.ap, in0.ap, in1.ap, s._a(scalar)
        s._rec(eng, lambda e: e.scalar_tensor_tensor(out=o, in0=a, scalar=sc, in1=b, op0=op0, op1=op1),
               [in0.buf, in1.buf, s._b(scalar)], [out.buf])

    def copy(s, out, in_, eng="dve"):
        o, i = out.ap, in_.ap
        if eng == "act":
            s._rec("act", lambda e: e.activation(out=o, in_=i, func=AF.Copy), [in_.buf], [out.buf])
        else:
            s._rec(eng, lambda e: e.tensor_copy(out=o, in_=i), [in_.buf], [out.buf])

    def memset(s, out, val, eng="dve"):
        o = out.ap
        s._rec(eng, lambda e: e.memset(o, val), [], [out.buf])

    def rsum(s, out, in_, eng="dve"):
        o, i = out.ap, in_.ap
        s._rec(eng, lambda e: e.reduce_sum(out=o, in_=i, axis=AX.X), [in_.buf], [out.buf])

    def recip(s, out, in_):
        o, i = out.ap, in_.ap
        s._rec("dve", lambda e: e.reciprocal(out=o, in_=i), [in_.buf], [out.buf])

    def scan(s, out, d0, d1, init, op0=ALU.mult, op1=ALU.add):
        o, a, b, ini = out.ap, d0.ap, d1.ap, s._a(init)
        s._rec("dve", lambda e: e.tensor_tensor_scan(out=o, data0=a, data1=b, initial=ini, op0=op0, op1=op1),
               [d0.buf, d1.buf, s._b(init)], [out.buf])

    def dma(s, out, in_, q="sp", is_out=False, extra_reads=()):
        o, i = s._a(out), s._a(in_)
        s._rec(q, lambda e: e.dma_start(out=o, in_=i), [s._b(in_)] + list(extra_reads), [s._b(out)], is_dma=True,
               is_out=is_out)

    def emit(s):
        nc = s.nc
        for e in ENG_NAMES:
            for op in s.ops[e]:
                for d in op.deps:
                    if not d.is_dma:
                        if d.eng == "pe" and op.eng == "pe" and not op.is_dma:
                            continue
                        d.signal = True
        for e in ENG_NAMES:
            cnt = 0
            for op in s.ops[e]:
                if op.signal and not op.is_dma:
                    cnt += 1
                op.count = cnt
        with ExitStack() as st:
            esem = {e: st.enter_context(nc.semaphore(f"sem_{e}")) for e in ENG_NAMES}
            dsem = {e: [st.enter_context(nc.semaphore(f"dsem_{e}_{i}")) for i in range(N_DMA_SEM[e])]
                    for e in ("sp", "pool", "act")}
            block = st.enter_context(nc.Block())

            def run(ename, eng):
                waited = {}
                mine = s.ops[ename]

                def wait(key, sem, val):
                    if waited.get(key, 0) >= val:
                        return
                    eng.wait_ge(sem, val)
                    waited[key] = val

                for op in mine:
                    for d in op.deps:
                        if d.is_dma:
                            wait(("d", d.eng, d.dsem), dsem[d.eng][d.dsem], d.dval)
                        else:
                            if d.eng == "pe" and ename == "pe" and not op.is_dma:
                                continue
                            wait(("e", d.eng), esem[d.eng], d.count)
                    if op.is_dma:
                        if op.dprev > 0:
                            wait(("d", ename, op.dsem), dsem[ename][op.dsem], op.dprev)
                        ins = op.fn(eng)
                        ins.then_inc(dsem[ename][op.dsem], 16)
                    else:
                        ins = op.fn(eng)
                        if op.signal:
                            ins.then_inc(esem[ename], 1)
                if ename in ("sp", "pool", "act"):
                    n = s.dma_count[ename]
                    nds = N_DMA_SEM[ename]
                    for i in range(min(n, nds)):
                        last = 16 * ((n - 1 - i) // nds + 1)
                        wait(("d", ename, i), dsem[ename][i], last)

            @block.tensor
            def _(e):
                run("pe", e)

            @block.scalar
            def _(e):
                run("act", e)

            @block.vector
            def _(e):
                run("dve", e)

            @block.gpsimd
            def _(e):
                run("pool", e)

            @block.sync
            def _(e):
                run("sp", e)


def fm(vec):
    return np.ascontiguousarray(vec.reshape(-1, 128).T)


def const_layout(cfg):
    items = [("ident", 128), ("ones", 128), ("blockones", 128), ("f64", 64)]
    for C in cfg.CCS:
        items += [(f"mstrT{C}", 128), (f"minclT{C}", 128), (f"mstr{C}", 128)]
    items += [("hm_r", cfg.NJ * cfg.RH), ("hm_g", cfg.GQT * cfg.GH), ("hm_h", cfg.NJ * cfg.HH)]
    for C in cfg.CCS:
        items += [(f"cm_r{C}", cfg.NJ * 2), (f"cm_g{C}", cfg.GQT * 2), (f"cm_h{C}", cfg.NJ)]
    off = {}
    o = 0
    for k, n in items:
        off[k] = (o, n)
        o += n
    return off, o


def pack_consts(cfg):
    c = build_consts(cfg)
    c["hm_r"] = head_mask(cfg.NJ, 2, cfg.RH).reshape(128, -1)
    c["hm_g"] = head_mask(cfg.GQT, 2, cfg.GH).reshape(128, -1)
    c["hm_h"] = head_mask(cfg.NJ, 1, cfg.HH).reshape(128, -1)
    for C in cfg.CCS:
        c[f"cm_r{C}"] = col_mask(C, cfg.NJ, 2, cfg.RH).reshape(128, -1)
        c[f"cm_g{C}"] = col_mask(C, cfg.GQT, 2, cfg.GH).reshape(128, -1)
        c[f"cm_h{C}"] = col_mask(C, cfg.NJ, 1, cfg.HH).reshape(128, -1)
    off, n = const_layout(cfg)
    out = np.zeros((128, n), np.float32)
    for k, (o, w) in off.items():
        out[:, o:o + w] = c[k]
    return out


def pack_shared(cfg, inp):
    L, D, GW, NJ = cfg.L, cfg.D, cfg.GW, cfg.NJ
    sh = {}
    w = inp["w_in"]
    RC = 3 * GW + 256
    wt = np.zeros((L, D, cfg.NCT * 128), np.float32)
    wt[:, :, 0:RC] = w[:, :, 0:RC]
    o = RC
    wt[:, :, cfg.ct_s5 * 128: cfg.ct_s5 * 128 + GW] = w[:, :, o:o + GW]
    o += GW
    wt[:, :, cfg.ct_hq * 128: cfg.ct_hq * 128 + 4 * GW] = w[:, :, o:o + 4 * GW]
    o += 4 * GW
    GQK = GW // 2
    wt[:, :, cfg.ct_gq * 128: cfg.ct_gq * 128 + GQK] = w[:, :, o:o + GQK]
    o += GQK
    wt[:, :, cfg.ct_gk * 128: cfg.ct_gk * 128 + GQK] = w[:, :, o:o + GQK]
    o += GQK
    wt[:, :, cfg.ct_gv * 128: cfg.ct_gv * 128 + GW] = w[:, :, o:o + GW]
    o += GW
    wt[:, :, cfg.ct_glo * 128: cfg.ct_glo * 128 + 16] = w[:, :, o:o + 16]
    o += 16
    wt[:, :, cfg.ct_gg * 128: cfg.ct_gg * 128 + GW] = w[:, :, o:o + GW]
    o += GW
    assert o == w.shape[2]
    sh["w_in"] = wt
    sh["w_out"] = np.ascontiguousarray(inp["w_out"])
    sh["w_up"] = np.ascontiguousarray(inp["ffn_w_up"])
    sh["w_down"] = np.ascontiguousarray(inp["ffn_w_down"])
    vecs = np.zeros((L, 128, cfg.NVEC), np.float32)

    def put(name, arr):
        o, n = cfg.vec[name]
        for l in range(L):
            vecs[l, :, o:o + n] = fm(arr[l])
    put("norm_mix", inp["norm_mix"])
    put("norm_ffn", inp["norm_ffn"])
    mu = np.zeros((L, cfg.NR * 128), np.float32)
    mu[:, :RC] = inp["rwkv_mu"]
    put("mu", mu)
    put("w0", inp["rwkv_w0"])
    put("a0", inp["rwkv_a0"])
    put("k_k", inp["rwkv_k_k"])
    put("k_a", inp["rwkv_k_a"])
    put("r_k", inp["rwkv_r_k"].reshape(L, -1))
    put("s5_D", inp["s5_D"])
    put("s5_bglu", inp["s5_b_glu"])
    put("gk_b", inp["gla_gk_b"])
    for j in range(3):
        put(f"conv_w{j}", inp["ffn_conv_w"][:, j])
    put("conv_b", inp["ffn_conv_b"])
    put("A_re", inp["s5_A_re"].reshape(L, -1))
    put("A_im", inp["s5_A_im"].reshape(L, -1))
    put("logdt", np.repeat(inp["s5_log_dt"], 64, axis=1))
    put("hlb", inp["hgrn_lower_bounds"])
    sh["vecs"] = vecs
    sh["normf"] = fm(inp["norm_final"])
    sh["rw_wa"] = np.ascontiguousarray(np.concatenate([inp["rwkv_w_up"], inp["rwkv_a_up"]], axis=1))
    sh["rw_gup"] = np.ascontiguousarray(inp["rwkv_g_up"])
    for C in (cfg.C, TS):
        sh[f"rw_ln{C}"] = np.ascontiguousarray(np.repeat(inp["rwkv_ln"].reshape(L, cfg.RH, 1, 64), C, axis=2)
                                               .reshape(L, cfg.RH * C, 64))
    for C in cfg.CCS:
        sh[f"hg_n{C}"] = np.ascontiguousarray(np.repeat(inp["hgrn_norm"].reshape(L, cfg.HH, 1, 128), C, axis=2)
                                              .reshape(L, cfg.HH * C, 128))
        sh[f"gl_n{C}"] = np.ascontiguousarray(np.repeat(inp["gla_norm"].reshape(L, cfg.GH, 1, 128), C, axis=2)
                                              .reshape(L, cfg.GH * C, 128))
    sh["gk_up"] = np.ascontiguousarray(inp["gla_gk_up"])
    sh["s5_Bt_re"] = np.ascontiguousarray(inp["s5_B_re"].transpose(0, 1, 3, 2))
    sh["s5_Bt_im"] = np.ascontiguousarray(inp["s5_B_im"].transpose(0, 1, 3, 2))
    sh["s5_Ct_re"] = np.ascontiguousarray(inp["s5_C_re"].transpose(0, 1, 3, 2))
    sh["s5_Ct_im"] = np.ascontiguousarray(inp["s5_C_im"].transpose(0, 1, 3, 2))
    sh["s5_wglu"] = np.ascontiguousarray(inp["s5_w_glu"])
    sh["consts"] = pack_consts(cfg)
    ntile = len(cfg.tiles)
    sm = np.ones((ntile, 128, cfg.TMAX), np.float32)
    pos = np.zeros((ntile, 128, cfg.TMAX), np.float32)
    cmk = np.ones((ntile, 128, cfg.TMAX), np.float32)
    cmk2 = np.ones((ntile, 128, cfg.TMAX), np.float32)
    for i, (p0, pn, ns) in enumerate(cfg.tiles):
        sm[i, :, 0] = 0
        pos[i, :, :pn] = np.arange(1, pn + 1)
        cmk[i, :, 0:pn:cfg.C] = 0
        cmk2[i, :, 0:pn:cfg.CL] = 0
        for q in range(ns):
            sm[i, :, pn + q * TS] = 0
            pos[i, :, pn + q * TS: pn + (q + 1) * TS] = np.arange(1, TS + 1)
            cmk[i, :, pn + q * TS] = 0
            cmk2[i, :, pn + q * TS] = 0
    sh["tmask"] = np.ascontiguousarray(np.stack([sm, pos, cmk, cmk2], axis=1))
    return sh


def pack_core(cfg, inp, b, s0):
    L, NS = cfg.L, cfg.NS
    d = {}
    xs = inp["x_sample"][s0:s0 + NS].reshape(NS * TS, cfg.D)
    xall = np.concatenate([inp["meta_tokens"], inp["x_prompt"][b], xs], axis=0)
    d["xin"] = np.ascontiguousarray(xall.T)
    sl = slice(s0, s0 + NS)
    d["st_rwkv"] = np.ascontiguousarray(inp["state_rwkv"][:, sl].transpose(0, 1, 2, 4, 3).reshape(L, NS, cfg.GW, 64))
    sft = np.zeros((L, NS, cfg.NR * 128), np.float32)
    sft[:, :, :3 * cfg.GW + 256] = inp["state_rwkv_shift"][:, sl]
    d["st_shift"] = np.ascontiguousarray(sft.reshape(L, NS, cfg.NR, 128).transpose(0, 3, 2, 1))
    for nm, key in (("st_s5re", "state_s5_re"), ("st_s5im", "state_s5_im")):
        a = inp[key][:, sl].reshape(L, NS, cfg.S5C, 128)
        d[nm] = np.ascontiguousarray(a.transpose(0, 3, 2, 1))
    d["st_hgrn"] = np.ascontiguousarray(inp["state_hgrn"][:, sl].reshape(L, NS, cfg.GW, 128))
    d["st_gla"] = np.ascontiguousarray(inp["state_gla"][:, sl].reshape(L, NS, cfg.GW // 2, 128))
    a = inp["state_ffn_conv"][:, sl].reshape(L, NS, 2, cfg.NFT, 128)
    d["st_conv"] = np.ascontiguousarray(a.transpose(0, 4, 3, 1, 2))
    return d


OUT_SPECS = lambda cfg: {
    "yT": [cfg.D, cfg.NTOK],
    "o_rwkv_p": [cfg.L, cfg.GW, 64], "o_rwkv_s": [cfg.L, cfg.NS, cfg.GW, 64],
    "o_shift_p": [cfg.L, 128, cfg.NR], "o_shift_s": [cfg.L, 128, cfg.NR, cfg.NS],
    "o_s5re_p": [cfg.L, 128, cfg.S5C], "o_s5re_s": [cfg.L, 128, cfg.S5C, cfg.NS],
    "o_s5im_p": [cfg.L, 128, cfg.S5C], "o_s5im_s": [cfg.L, 128, cfg.S5C, cfg.NS],
    "o_hgrn_p": [cfg.L, cfg.GW, 128], "o_hgrn_s": [cfg.L, cfg.NS, cfg.GW, 128],
    "o_gla_p": [cfg.L, cfg.GW // 2, 128], "o_gla_s": [cfg.L, cfg.NS, cfg.GW // 2, 128],
    "o_conv_p": [cfg.L, 128, cfg.NFT, 2], "o_conv_s": [cfg.L, 128, cfg.NFT, cfg.NS, 2],
}


def unpack_core(cfg, r):
    L, NS, GW = cfg.L, cfg.NS, cfg.GW
    o = {}
    yT = r["yT"]
    o["y_p"] = np.ascontiguousarray(yT[:, N_META:cfg.NP].T)
    o["y_s"] = np.ascontiguousarray(yT[:, cfg.NP:].T).reshape(NS, TS, cfg.D)
    o["rwkv_p"] = r["o_rwkv_p"].reshape(L, cfg.RH, 64, 64).transpose(0, 1, 3, 2)
    o["rwkv_s"] = r["o_rwkv_s"].reshape(L, NS, cfg.RH, 64, 64).transpose(0, 1, 2, 4, 3)
    RC = 3 * GW + 256
    o["shift_p"] = r["o_shift_p"].transpose(0, 2, 1).reshape(L, -1)[:, :RC]
    o["shift_s"] = r["o_shift_s"].transpose(0, 3, 2, 1).reshape(L, NS, -1)[:, :, :RC]
    for k in ("s5re", "s5im"):
        o[k + "_p"] = r[f"o_{k}_p"].transpose(0, 2, 1).reshape(L, GW // 16, 64)
        o[k + "_s"] = r[f"o_{k}_s"].transpose(0, 3, 2, 1).reshape(L, NS, GW // 16, 64)
    o["hgrn_p"] = r["o_hgrn_p"].reshape(L, cfg.HH, 128, 128)
    o["hgrn_s"] = r["o_hgrn_s"].reshape(L, NS, cfg.HH, 128, 128)
    o["gla_p"] = r["o_gla_p"].reshape(L, cfg.GH, 64, 128)
    o["gla_s"] = r["o_gla_s"].reshape(L, NS, cfg.GH, 64, 128)
    o["conv_p"] = r["o_conv_p"].transpose(0, 3, 2, 1).reshape(L, 2, cfg.DFF)
    o["conv_s"] = r["o_conv_s"].transpose(0, 3, 4, 2, 1).reshape(L, NS, 2, cfg.DFF)
    return o


def weight_blocks(cfg):
    blocks = []
    for (c0, n) in cfg.mix_groups:
        for g0 in range(0, n, 4):
            blocks.append(("win", c0 + g0, min(4, n - g0)))
    for og in range(cfg.D // 256):
        blocks.append(("wout", og))
    for j0 in range(0, cfg.NFT, 2):
        blocks.append(("wup", j0, min(2, cfg.NFT - j0)))
    for og in range(cfg.D // 512):
        for k0 in range(0, cfg.NFT, 16):
            blocks.append(("wdown", og, k0, min(16, cfg.NFT - k0)))
    return blocks


NSLOT = 3
SLOT_ELEMS = 8192


class WStream:
    def __init__(s, P, cfg, di, order, nc):
        s.P, s.cfg, s.di = P, cfg, di
        s.slots = [P.alloc(f"wslot{i}", [SLOT_ELEMS], BF16) for i in range(NSLOT)]
        s.per = weight_blocks(cfg)
        s.specs = [(l, b) for (_, l) in order for b in s.per]
        s.issued = 0
        s.next = 0
        s.regs = []
        off = 0
        for b in s.per:
            k = b[0]
            if k == "win":
                rr = [(128, cfg.NDC * b[2] * 128)]
            elif k == "wout":
                rr = [(64, cfg.RH * 256), (128, 3 * cfg.NJ * 256)]
            elif k == "wup":
                rr = [(128, cfg.NDC * 2 * b[2] * 128)]
            else:
                rr = [(128, b[3] * 512)]
            lst = []
            for (np_, m) in rr:
                lst.append((off, np_, m))
                off += np_ * m
            s.regs.append(lst)
        s.total = off
        s.scr = [nc.dram_tensor(f"wscr_bf{l}", [off], BF16, kind="Internal").ap() for l in range(cfg.L)]
        s.dbuf = {(l, bi): Buf(f"wscr{l}_{bi}", None, [("dram", l, bi)], "dram")
                  for l in range(cfg.L) for bi in range(len(s.per))}

    def region(s, l, bi, ri):
        off, np_, m = s.regs[bi][ri]
        return V(s.scr[l][off:off + np_ * m].rearrange("(p m) -> p m", p=np_), s.dbuf[(l, bi)])

    def convert_layer(s, l):
        cfg, di, P = s.cfg, s.di, s.P
        for bi, b in enumerate(s.per):
            k = b[0]
            if k == "win":
                c0, n = b[1] * 128, b[2] * 128
                P.dma(s.region(l, bi, 0).re("p (c n) -> p c n", n=n),
                      di["w_in"][l, :, c0:c0 + n].rearrange("(c p) n -> p c n", p=128), q="pool")
            elif k == "wout":
                c0 = b[1] * 256
                P.dma(s.region(l, bi, 0).re("p (h n) -> p h n", n=256),
                      di["w_out"][l, 0:cfg.GW, c0:c0 + 256].rearrange("(h k) n -> k h n", k=64), q="pool")
                P.dma(s.region(l, bi, 1).re("p (c n) -> p c n", n=256),
                      di["w_out"][l, cfg.GW:4 * cfg.GW, c0:c0 + 256].rearrange("(c p) n -> p c n", p=128), q="pool")
            elif k == "wup":
                c0, n = b[1] * 128, b[2] * 128
                dst = s.region(l, bi, 0).re("p (c g n) -> p c g n", g=2, n=n)
                for g in range(2):
                    P.dma(dst[:, :, g, :],
                          di["w_up"][l, :, g * cfg.DFF + c0:g * cfg.DFF + c0 + n].rearrange("(c p) n -> p c n", p=128),
                          q="pool")
            else:
                og, k0, nk = b[1], b[2], b[3]
                P.dma(s.region(l, bi, 0).re("p (c n) -> p c n", n=512),
                      di["w_down"][l, k0 * 128:(k0 + nk) * 128, og * 512:(og + 1) * 512]
                      .rearrange("(c p) n -> p c n", p=128), q="pool")

    def views(s, slot, spec):
        cfg = s.cfg
        l, b = spec
        k = b[0]
        sv = slot.v()
        if k == "win":
            n = b[2] * 128
            return [sv[:, 0:cfg.NDC * n].re("p (c n) -> p c n", n=n)]
        if k == "wout":
            a = sv[:, 0:cfg.RH * 256].re("p (h n) -> p h n", n=256)
            bb = sv[:, cfg.RH * 256:(cfg.RH + 3 * cfg.NJ) * 256].re("p (c n) -> p c n", n=256)
            return [a, bb]
        if k == "wup":
            n = b[2] * 128
            return [sv[:, 0:cfg.NDC * 2 * n].re("p (c g n) -> p c g n", g=2, n=n)]
        if k == "wdown":
            return [sv[:, 0:b[3] * 512].re("p (c n) -> p c n", n=512)]

    def _load(s, i):
        cfg, P = s.cfg, s.P
        slot = s.slots[i % NSLOT]
        l, b = s.specs[i]
        bi = i % len(s.per)
        sv = slot.v()
        if b[0] == "wout":
            ma, mb = cfg.RH * 256, 3 * cfg.NJ * 256
            P.dma(sv[0:64, 0:ma], s.region(l, bi, 0), q="sp")
            P.dma(sv[:, ma:ma + mb], s.region(l, bi, 1), q="sp")
        else:
            m = s.regs[bi][0][2]
            P.dma(sv[:, 0:m], s.region(l, bi, 0), q="sp")

    def get(s, kind):
        i = s.next
        s.next += 1
        while s.issued < min(len(s.specs), i + NSLOT):
            s._load(s.issued)
            s.issued += 1
        l, b = s.specs[i]
        assert b[0] == kind, (b, kind)
        return s.views(s.slots[i % NSLOT], s.specs[i]), b


def la_chunk(P, G, S, Cc, qv, kv, vv, gamma, H, av=None, bv=None, prodv=None):
    NH, NJ, NJV, DK, DV, HPT = G["NH"], G["NJ"], G["NJV"], G["DK"], G["DV"], G["HPT"]
    R = NH * Cc
    delta = av is not None
    XX, VX, HS = S["XX"], S["VX"], S["HS"]
    ident = G["ident"]
    mstrT, minclT, mstr = G[f"mstrT{Cc}"], G[f"minclT{Cc}"], G[f"mstr{Cc}"]
    hm, hmv = G["hm"], G["hmv"]

    def expand(x, src):
        out = XX[:, :, x, 0:R].re("p j (h t) -> p j h t", t=Cc)
        P.tt(out, src.us(2).bc([128, NJ, NH, Cc]), hm.us(3).bc([128, NJ, NH, Cc]), ALU.mult)

    if delta:
        expand(0, bv)
        expand(2, av)
        if prodv is not None:
            expand(4, prodv)
    expand(1, qv)
    expand(3, kv)
    outv = VX[:, :, 0:R].re("p j (h t) -> p j h t", t=Cc)
    P.tt(outv, vv.us(2).bc([128, NJV, NH, Cc]), hmv.us(3).bc([128, NJV, NH, Cc]), ALU.mult)
    psf = P.ps()
    for j in range(NJ):
        P.mm(psf[0:R, 0:DK], XX[:, j, 3, 0:R], G["fold_k"], start=(j == 0), stop=(j == NJ - 1))
    if delta:
        for j in range(NJ):
            P.mm(psf[0:R, DK:2 * DK], XX[:, j, 2, 0:R], G["fold_k"], start=(j == 0), stop=(j == NJ - 1))
    for j in range(NJV):
        P.mm(psf[0:R, 2 * DK:2 * DK + DV], VX[:, j, 0:R], G["fold_v"], start=(j == 0), stop=(j == NJV - 1))
    if delta:
        P.copy(HS[0:R, 0:2 * DK + DV], psf[0:R, 0:2 * DK + DV], eng="act")
    else:
        P.copy(HS[0:R, 0:DK], psf[0:R, 0:DK], eng="act")
        P.copy(HS[0:R, 2 * DK:2 * DK + DV], psf[0:R, 2 * DK:2 * DK + DV], eng="act")
    K_hs, A_hs, V_hs = HS[0:R, 0:DK], HS[0:R, DK:2 * DK], HS[0:R, 2 * DK:2 * DK + DV]
    psK = P.ps()
    RKT, NT, MT, RAT, Mm = S["RKT"], S["NT"], S["MT"], S["RAT"], S["Mm"]
    if delta:
        pk = psK[0:R, 0:256].re("p (x r) -> p x r", x=2)[:, :, 0:R]
        if R == 128:
            for j in range(NJ):
                P.mm(psK[:, 0:256], XX[:, j, 3, :], XX[:, j, 0:2, :].re("p x r -> p (x r)"), start=(j == 0),
                     stop=(j == NJ - 1))
        else:
            for xi in range(2):
                for j in range(NJ):
                    P.mm(pk[:, xi, :], XX[:, j, 3, 0:R], XX[:, j, xi, 0:R], start=(j == 0), stop=(j == NJ - 1))
        P.tt(NT[0:R, 0:R], pk[:, 0, :], mstrT[0:R, 0:R], ALU.mult)
        P.tt(RKT[0:R, 0:R], pk[:, 1, :], minclT[0:R, 0:R], ALU.mult)
        psA = P.ps()
        pa = psA[0:R, 0:256].re("p (x r) -> p x r", x=2)[:, :, 0:R]
        if R == 128:
            for j in range(NJ):
                P.mm(psA[:, 0:256], XX[:, j, 2, :], XX[:, j, 0:2, :].re("p x r -> p (x r)"), start=(j == 0),
                     stop=(j == NJ - 1))
        else:
            for xi in range(2):
                for j in range(NJ):
                    P.mm(pa[:, xi, :], XX[:, j, 2, 0:R], XX[:, j, xi, 0:R], start=(j == 0), stop=(j == NJ - 1))
        P.tt(MT[0:R, 0:R], pa[:, 0, :], mstrT[0:R, 0:R], ALU.mult)
        P.tt(RAT[0:R, 0:R], pa[:, 1, :], minclT[0:R, 0:R], ALU.mult)
        psM = P.ps()
        for j in range(NJ):
            P.mm(psM[0:R, 0:R], XX[:, j, 0, 0:R], XX[:, j, 2, 0:R], start=(j == 0), stop=(j == NJ - 1))
        P.tt(Mm[0:R, 0:R], psM[0:R, 0:R], mstr[0:R, 0:R], ALU.mult)
        levels = int(round(math.log2(Cc))) - 1
        Z = S["Z"][0]
        P.tt(Z[0:R, 0:R], ident[0:R, 0:R], MT[0:R, 0:R], ALU.subtract)
        pn_ = P.ps()
        P.mm(pn_[0:R, 0:DV], NT[0:R, 0:R], V_hs)
        P.copy(S["NV"][0:R, 0:DV], pn_[0:R, 0:DV], eng="act")
        pw = P.ps()
        for j in range(NJ):
            P.mm(pw[0:R, 0:DV], XX[:, j, 0, 0:R], H[:, j, :], start=(j == 0), stop=(j == NJ - 1))
        P.tt(S["W"][0:R, 0:DV], pw[0:R, 0:DV], S["NV"][0:R, 0:DV], ALU.add)
        zstate = {"Z": Z, "n": 0}

        def zstep(k):
            Zc = zstate["Z"]
            pz = P.ps()
            P.mm(pz[0:R, 0:R], ident[0:R, 0:R], Zc[0:R, 0:R], start=True, stop=False)
            P.mm(pz[0:R, 0:R], S["Mp"][k][0:R, 0:R], Zc[0:R, 0:R], start=False, stop=True)
            zstate["n"] += 1
            Zn = S["Z"][zstate["n"] % 2]
            P.copy(Zn[0:R, 0:R], pz[0:R, 0:R], eng="act")
            zstate["Z"] = Zn

        cur, curT = Mm, MT
        for lev in range(levels):
            p1 = P.ps()
            P.mm(p1[0:R, 0:R], curT[0:R, 0:R], cur[0:R, 0:R])
            P.copy(S["Mp"][lev][0:R, 0:R], p1[0:R, 0:R], eng="act")
            if lev < levels - 1:
                p2 = P.ps()
                P.mm(p2[0:R, 0:R], cur[0:R, 0:R], curT[0:R, 0:R])
                P.copy(S["MpT"][lev][0:R, 0:R], p2[0:R, 0:R], eng="act")
            if lev >= 1:
                zstep(lev - 1)
            cur, curT = S["Mp"][lev], S["MpT"][lev]
        zstep(levels - 1)
        TinvT = zstate["Z"]
        pu = P.ps()
        P.mm(pu[0:R, 0:DV], TinvT[0:R, 0:R], S["W"][0:R, 0:DV])
        P.act(S["nU"][0:R, 0:DV], pu[0:R, 0:DV], AF.Copy, scale=-1.0)
        nU = S["nU"][0:R, 0:DV]
    else:
        for j in range(NJ):
            P.mm(psK[0:R, 0:R], XX[:, j, 3, 0:R], XX[:, j, 1, 0:R], start=(j == 0), stop=(j == NJ - 1))
        P.tt(RKT[0:R, 0:R], psK[0:R, 0:R], minclT[0:R, 0:R], ALU.mult)
    psY = P.ps()
    for j in range(NJ):
        P.mm(psY[0:R, 0:DV], XX[:, j, 1, 0:R], H[:, j, :], start=(j == 0), stop=False)
    if delta:
        P.mm(psY[0:R, 0:DV], RAT[0:R, 0:R], nU, start=False, stop=False)
    P.mm(psY[0:R, 0:DV], RKT[0:R, 0:R], V_hs, start=False, stop=True)
    psS = None
    if prodv is not None:
        psS = P.ps()
        for j in range(NJ):
            P.mm(psS[0:R, 0:1], XX[:, j, 4, 0:R], G["ones"][:, 0:1], start=(j == 0), stop=(j == NJ - 1))
    cm = G[f"cm{Cc}"]
    K2, A2 = S["K2"], S["A2"]
    P.tt(K2[0:R], K_hs.us(1).us(1).bc([R, NJ, HPT, DK]), cm[0:R].us(3).bc([R, NJ, HPT, DK]), ALU.mult)
    if delta:
        P.tt(A2[0:R], A_hs.us(1).us(1).bc([R, NJ, HPT, DK]), cm[0:R].us(3).bc([R, NJ, HPT, DK]), ALU.mult)
    psH = P.ps()
    for j in range(NJ):
        o = psH[:, j * DV:(j + 1) * DV]
        if delta:
            P.mm(o, A2[0:R, j].re("p h k -> p (h k)"), nU, start=True, stop=False)
        P.mm(o, K2[0:R, j].re("p h k -> p (h k)"), V_hs, start=(not delta), stop=True)
    P.tt(S["tH"].v(), psH[:, 0:NJ * DV].re("p (j v) -> p j v", v=DV), H.v(), ALU.add)
    P.tt(H.v(), S["tH"].v(), gamma.us(2).bc([128, NJ, DV]), ALU.mult)
    return psY[0:R, 0:DV], V_hs, psS


MAGIC = 12582912.0
TWO_PI = 2.0 * math.pi


def build_program(cfg, debug=False):
    nc = bass.Bass("TRN2", target_bir_lowering=False)
    L, D, GW, NJ, NDC, NR, RH, HH, GH, GQT, S5C, NFT, NS, C = (cfg.L, cfg.D, cfg.GW, cfg.NJ, cfg.NDC, cfg.NR,
                                                             cfg.RH, cfg.HH, cfg.GH, cfg.GQT, cfg.S5C, cfg.NFT,
                                                             cfg.NS, cfg.C)
    TMAX = cfg.TMAX
    di = {}

    def din(name, shape):
        di[name] = nc.dram_tensor(name, [int(x) for x in shape], F32, kind="ExternalInput").ap()

    coff, NCONST = const_layout(cfg)
    ntile = len(cfg.tiles)
    din("xin", [D, cfg.NTOK])
    din("w_in", [L, D, cfg.NCT * 128])
    din("w_out", [L, D, D])
    din("w_up", [L, D, 2 * cfg.DFF])
    din("w_down", [L, cfg.DFF, D])
    din("vecs", [L, 128, cfg.NVEC])
    din("normf", [128, NDC])
    din("rw_wa", [L, 128, GW])
    din("rw_gup", [L, 128, GW])
    for Cc in (C, TS):
        din(f"rw_ln{Cc}", [L, RH * Cc, 64])
    for Cc in cfg.CCS:
        din(f"hg_n{Cc}", [L, HH * Cc, 128])
        din(f"gl_n{Cc}", [L, GH * Cc, 128])
    din("gk_up", [L, 16, GW // 2])
    G5 = GW // 16
    din("s5_Bt_re", [L, G5, 16, 64])
    din("s5_Bt_im", [L, G5, 16, 64])
    din("s5_Ct_re", [L, G5, 64, 16])
    din("s5_Ct_im", [L, G5, 64, 16])
    din("s5_wglu", [L, GW, GW])
    din("consts", [128, NCONST])
    din("tmask", [ntile, 4, 128, TMAX])
    din("st_rwkv", [L, NS, GW, 64])
    din("st_shift", [L, 128, NR, NS])
    din("st_s5re", [L, 128, S5C, NS])
    din("st_s5im", [L, 128, S5C, NS])
    din("st_hgrn", [L, NS, GW, 128])
    din("st_gla", [L, NS, GW // 2, 128])
    din("st_conv", [L, 128, NFT, NS, 2])
    do = {}
    for k, shp in OUT_SPECS(cfg).items():
        do[k] = nc.dram_tensor(k, [int(x) for x in shp], F32, kind="ExternalOutput").ap()

    words = nc.sbuf_bytes_remaining // 4 - 64
    P = Prog(nc, words)
    consts = P.alloc("consts", [NCONST])
    cv = lambda k: consts[:, coff[k][0]:coff[k][0] + coff[k][1]]
    ident, ones_f, blockones, f64 = cv("ident"), cv("ones"), cv("blockones"), cv("f64")
    ones_bf = P.alloc("ones_bf", [128], BF16)
    vecs = P.alloc("vecs", [L, cfg.NVEC])
    normf = P.alloc("normf", [NDC])
    lb = P.alloc("lb", [L, NJ])
    oml = P.alloc("oml", [L, NJ])
    tmask = P.alloc("tmask", [4, TMAX])
    x = P.alloc("x", [NDC, TMAX])
    h = P.alloc("h", [NDC, TMAX], BF16)
    Hr = [P.alloc(f"Hr{l}", [NJ, 64]) for l in range(L)]
    Hh = [P.alloc(f"Hh{l}", [NJ, 128]) for l in range(L)]
    Hg = [P.alloc(f"Hg{l}", [GQT, 128]) for l in range(L)]
    s5c = [[P.alloc(f"s5c{l}{i}", [S5C]) for i in range(2)] for l in range(L)]
    convc = [P.alloc(f"convc{l}", [NFT, 2]) for l in range(L)]
    shiftc = [P.alloc(f"shiftc{l}", [NR]) for l in range(L)]
    order = [(ti, l) for ti in range(ntile) for l in range(L)]
    WS = WStream(P, cfg, di, order, nc)
    phase_base = P.off

    def vc(l, name, j=None):
        o, n = cfg.vec[name]
        if j is None:
            return vecs[:, l, o:o + n]
        return vecs[:, l, o + j:o + j + 1]

    def geom(NH, NJk, HPT, NJV, DV, hmk, hmv, cmk):
        g = {"NH": NH, "NJ": NJk, "NJV": NJV, "DK": 128 // HPT, "DV": DV, "HPT": HPT, "ident": ident,
             "ones": ones_f,
             "hm": cv(hmk).re("p (j h) -> p j h", h=NH), "hmv": cv(hmv).re("p (j h) -> p j h", h=NH),
             "fold_k": f64 if HPT == 2 else ident, "fold_v": f64 if DV == 64 else ident}
        for Cc in cfg.CCS:
            g[f"mstrT{Cc}"], g[f"minclT{Cc}"], g[f"mstr{Cc}"] = cv(f"mstrT{Cc}"), cv(f"minclT{Cc}"), cv(f"mstr{Cc}")
            g[f"cm{Cc}"] = cv(f"{cmk}{Cc}").re("p (j h) -> p j h", h=HPT)
        return g
    G_r = geom(RH, NJ, 2, NJ, 64, "hm_r", "hm_r", "cm_r")
    G_g = geom(GH, GQT, 2, NJ, 128, "hm_g", "hm_h", "cm_g")
    G_h = geom(HH, NJ, 1, NJ, 128, "hm_h", "hm_h", "cm_h")

    def la_scratch(G, delta):
        NJk, NJV, DK, DV, HPT = G["NJ"], G["NJV"], G["DK"], G["DV"], G["HPT"]
        S = {"XX": P.alloc("XX", [NJk, 5 if delta else 4, 128]), "VX": P.alloc("VX", [NJV, 128]),
             "HS": P.alloc("HS", [2 * DK + DV]), "RKT": P.alloc("RKT", [128]),
             "K2": P.alloc("K2", [NJk, HPT, DK]), "tH": P.alloc("tH", [NJk, DV]), "NT": None, "MT": None, "RAT": None, "Mm": None, "A2": None}
        if delta:
            for k in ("NT", "MT", "RAT", "Mm"):
                S[k] = P.alloc(k, [128])
            S["A2"] = P.alloc("A2", [NJk, HPT, DK])
            S["Mp"] = [P.alloc(f"Mp{i}", [128]) for i in range(3)]
            S["MpT"] = [P.alloc(f"MpT{i}", [128]) for i in range(3)]
            S["Z"] = [P.alloc(f"Z{i}", [128]) for i in range(2)]
            S["NV"], S["W"], S["nU"] = P.alloc("NV", [DV]), P.alloc("W", [DV]), P.alloc("nU", [DV])
        return S

    P.dma(consts.v(), di["consts"][:, :])
    P.dma(vecs.v(), di["vecs"].rearrange("l p n -> p l n"))
    P.dma(normf.v(), di["normf"][:, :])
    P.memset(ones_bf.v(), 1.0)
    for l_ in range(L):
        WS.convert_layer(l_)
    for l in range(L):
        for b in (Hr[l], Hh[l], Hg[l], s5c[l][0], s5c[l][1], convc[l], shiftc[l]):
            P.memset(b.v(), 0.0)
    P.off = phase_base
    ee = P.alloc("lb_e", [L, NJ])
    se = P.alloc("lb_s", [NJ])
    for l in range(L):
        P.act(ee[:, l, :], vc(l, "hlb"), AF.Exp)
    P.copy(se.v(), ee[:, 0, :])
    for l in range(1, L):
        P.tt(se.v(), se.v(), ee[:, l, :], ALU.add)
    P.recip(se.v(), se.v())
    P.memset(lb[:, 0, :], 0.0)
    for l in range(1, L):
        P.tt(ee[:, l, :], ee[:, l, :], se.v(), ALU.mult)
        P.tt(lb[:, l, :], lb[:, l - 1, :], ee[:, l, :], ALU.add)
    P.ts(oml.v(), lb.v(), -1.0, 1.0, ALU.mult, ALU.add)

    def rmsnorm(T, gname, l, out_bf, gvec=None):
        sq = [P.alloc(f"nsq{i}", [TMAX], BF16) for i in range(2)]
        rstd = P.alloc("rstd", [TMAX])
        ps = P.ps()
        for c in range(NDC):
            s_ = sq[c % 2]
            P.act(s_[:, 0:T], x[:, c, 0:T], AF.Square)
            P.mm(ps[:, 0:T], ones_bf.v(), s_[:, 0:T], start=(c == 0), stop=(c == NDC - 1))
        P.act(rstd[:, 0:T], ps[:, 0:T], AF.Sqrt, bias=EPS, scale=1.0 / D)
        P.recip(rstd[:, 0:T], rstd[:, 0:T])
        for c in range(NDC):
            g = gvec[:, c:c + 1] if gvec is not None else vc(l, gname, c)
            P.stt(out_bf[:, c, 0:T], x[:, c, 0:T], g, rstd[:, 0:T], ALU.mult, ALU.mult)

    def project(T, ct0, nct, pt):
        done = 0
        while done < nct:
            (wv,), b = WS.get("win")
            assert b[1] == ct0 + done
            for i in range(b[2]):
                ps = P.ps()
                for c in range(NDC):
                    P.mm(ps[:, 0:T], wv[:, c, i * 128:(i + 1) * 128], h[:, c, 0:T], start=(c == 0),
                         stop=(c == NDC - 1))
                P.copy(pt[:, done + i, 0:T], ps[:, 0:T], eng="act")
            done += b[2]

    def chunks_of(pn, ns, CC=None):
        CC = CC or C
        ch = [("p", c0, min(CC, pn - c0), None) for c0 in range(0, pn, CC)]
        ch += [("s", pn + q * TS, TS, q) for q in range(ns)]
        return ch

    def range_reduce(out, in_, tmp):
        P.ts(tmp, in_, 1.0 / TWO_PI, MAGIC, ALU.mult, ALU.add)
        P.ts(tmp, tmp, -MAGIC, None, ALU.add)
        P.stt(out, tmp, -TWO_PI, in_, ALU.mult, ALU.add)

    cmk = lambda T: tmask[:, 2, 0:T]
    cmk2 = lambda T: tmask[:, 3, 0:T]
    smk = lambda T: tmask[:, 0, 0:T]
    posv = lambda T: tmask[:, 1, 0:T]

    def rwkv(ti, l, T, pn, ns, yTr):
        last = ti == ntile - 1
        P.off = mix_base
        pt = P.alloc("pt_r", [NR, TMAX])
        project(T, 0, NR, pt)
        wa = P.alloc("rw_wa", [GW])
        gup = P.alloc("rw_gup", [GW])
        ln = {C: P.alloc("rw_ln", [64])}
        P.dma(wa.v(), di["rw_wa"][l])
        P.dma(gup.v(), di["rw_gup"][l])
        P.dma(ln[C][0:RH * C], di[f"rw_ln{C}"][l])
        if ns:
            ln[TS] = P.alloc("rw_ln4", [64])
            P.dma(ln[TS][0:RH * TS], di[f"rw_ln{TS}"][l])
            stsh = P.alloc("stsh", [NR, NS])
            osh = P.alloc("osh", [NR, NS])
            P.dma(stsh.v(), di["st_shift"][l])
        d = P.alloc("rw_d", [TMAX])
        for ct in range(NR):
            p = pt[:, ct, :]
            if pn > 1:
                P.tt(d[:, 1:pn], p[:, 0:pn - 1], p[:, 1:pn], ALU.subtract)
            P.tt(d[:, 0:1], shiftc[l][:, ct:ct + 1], p[:, 0:1], ALU.subtract)
            if ns:
                pv = p[:, pn:T].re("p (s t) -> p s t", t=TS)
                dv = d[:, pn:T].re("p (s t) -> p s t", t=TS)
                P.tt(dv[:, :, 1:TS], pv[:, :, 0:TS - 1], pv[:, :, 1:TS], ALU.subtract)
                P.tt(dv[:, :, 0:1], stsh[:, ct, :].us(2), pv[:, :, 0:1], ALU.subtract)
                P.copy(osh[:, ct, :].us(2), pv[:, :, TS - 1:TS])
            P.copy(shiftc[l][:, ct:ct + 1], p[:, pn - 1:pn])
            P.stt(p[:, 0:T], d[:, 0:T], vc(l, "mu", ct), p[:, 0:T], ALU.mult, ALU.add)
        if ns:
            P.dma(do["o_shift_s"][l], osh.v(), is_out=True)
        if last:
            P.dma(do["o_shift_p"][l], shiftc[l].v(), is_out=True)
        rT = pt[:, cfg.ct_r:cfg.ct_r + NJ, :]
        kT = pt[:, cfg.ct_k:cfg.ct_k + NJ, :]
        vT = pt[:, cfg.ct_v:cfg.ct_v + NJ, :]
        pwa = pt[:, cfg.ct_wa, :]
        B = [P.alloc(f"rwB{i}", [NJ, TMAX]) for i in range(6)]
        tw = P.alloc("rw_tw", [TMAX])
        sgl = P.alloc("rw_sgl", [TMAX])
        gT = P.alloc("rw_gT", [RH, TMAX])
        P.act(tw[0:64, 0:T], pwa[0:64, 0:T], AF.Tanh)
        P.act(sgl[:, 0:T], pt[:, cfg.ct_gl, 0:T], AF.Sigmoid)
        sg, cs, t3, a_, kk, ka = B
        for j in range(NJ):
            ps = P.ps()
            P.mm(ps[:, 0:T], wa[0:64, j * 128:(j + 1) * 128], tw[0:64, 0:T])
            P.act(sg[:, j, 0:T], ps[:, 0:T], AF.Sigmoid, bias=vc(l, "w0", j))
            ps2 = P.ps()
            P.mm(ps2[:, 0:T], wa[64:128, j * 128:(j + 1) * 128], pwa[64:128, 0:T])
            P.act(a_[:, j, 0:T], ps2[:, 0:T], AF.Sigmoid, bias=vc(l, "a0", j))
            P.scan(cs[:, j, 0:T], cmk(T), sg[:, j, 0:T], 0.0)
        for hd in range(RH):
            ps = P.ps()
            P.mm(ps[0:64, 0:T], gup[:, hd * 64:(hd + 1) * 64], sgl[:, 0:T])
            P.copy(gT[0:64, hd, 0:T], ps[0:64, 0:T], eng="act")
        P.tt(t3[:, :, 0:T], cs[:, :, 0:T], sg[:, :, 0:T], ALU.subtract)
        P.act(t3[:, :, 0:T], t3[:, :, 0:T], AF.Exp, scale=-RWKV_DECAY_SCALE)
        eb, enb = sg, P.alloc("rw_enb", [NJ, TMAX])
        P.act(eb[:, :, 0:T], cs[:, :, 0:T], AF.Exp, scale=-RWKV_DECAY_SCALE)
        P.act(enb[:, :, 0:T], cs[:, :, 0:T], AF.Exp, scale=RWKV_DECAY_SCALE)
        prod = cs
        sq = P.alloc("rw_sq", [TMAX])
        nrm = P.alloc("rw_nrm", [TMAX])
        for j in range(NJ):
            P.ts(kk[:, j, 0:T], kT[:, j, 0:T], vc(l, "k_k", j), None, ALU.mult)
            P.act(sq[:, 0:T], kk[:, j, 0:T], AF.Square)
            ps = P.ps()
            P.mm(ps[:, 0:T], blockones, sq[:, 0:T])
            P.act(nrm[:, 0:T], ps[:, 0:T], AF.Sqrt)
            P.ts(nrm[:, 0:T], nrm[:, 0:T], 1e-12, None, ALU.max)
            P.recip(nrm[:, 0:T], nrm[:, 0:T])
            P.tt(kk[:, j, 0:T], kk[:, j, 0:T], nrm[:, 0:T], ALU.mult)
            P.ts(sq[:, 0:T], a_[:, j, 0:T], -1.0, vc(l, "k_a", j), ALU.add, ALU.mult)
            P.stt(kT[:, j, 0:T], sq[:, 0:T], 1.0, kT[:, j, 0:T], ALU.add, ALU.mult)
            P.tt(ka[:, j, 0:T], kk[:, j, 0:T], a_[:, j, 0:T], ALU.mult)
            P.stt(prod[:, j, 0:T], rT[:, j, 0:T], vc(l, "r_k", j), kT[:, j, 0:T], ALU.mult, ALU.mult)
        P.tt(ka[:, :, 0:T], ka[:, :, 0:T], enb[:, :, 0:T], ALU.mult)
        P.tt(kT[:, :, 0:T], kT[:, :, 0:T], enb[:, :, 0:T], ALU.mult)
        P.tt(kk[:, :, 0:T], kk[:, :, 0:T], t3[:, :, 0:T], ALU.mult)
        P.tt(rT[:, :, 0:T], rT[:, :, 0:T], eb[:, :, 0:T], ALU.mult)
        S = la_scratch(G_r, True)
        ysb = P.alloc("rw_y", [64])
        yc = P.alloc("rw_yc", [64])
        st1 = P.alloc("rw_st", [8])
        Hs = [P.alloc(f"rw_Hs{i}", [NJ, 64]) for i in range(2)]
        for (kind, c0, Cc, q) in chunks_of(pn, ns):
            R = RH * Cc
            if kind == "p":
                Hst = Hr[l]
            else:
                Hst = Hs[q % 2]
                P.dma(Hst.v(), di["st_rwkv"][l, q].rearrange("(j p) v -> p j v", p=128))
            sl = slice(c0, c0 + Cc)
            psY, V_hs, psS = la_chunk(P, G_r, S, Cc, rT[:, :, sl], kT[:, :, sl], vT[:, :, sl],
                                      eb[:, :, c0 + Cc - 1], Hst, av=ka[:, :, sl], bv=kk[:, :, sl],
                                      prodv=prod[:, :, sl])
            if kind == "s":
                P.dma(do["o_rwkv_s"][l, q].rearrange("(j p) v -> p j v", p=128), Hst.v(), is_out=True)
            P.copy(ysb[0:R, :], psY, eng="act")
            P.rsum(st1[0:R, 0:1], ysb[0:R, :])
            P.ts(st1[0:R, 0:1], st1[0:R, 0:1], -1.0 / 64, None, ALU.mult)
            P.ts(yc[0:R, :], ysb[0:R, :], st1[0:R, 0:1], None, ALU.add)
            P.tt(ysb[0:R, :], yc[0:R, :], yc[0:R, :], ALU.mult)
            P.rsum(st1[0:R, 1:2], ysb[0:R, :])
            P.act(st1[0:R, 1:2], st1[0:R, 1:2], AF.Sqrt, bias=RWKV_GN_EPS, scale=1.0 / 64)
            P.recip(st1[0:R, 1:2], st1[0:R, 1:2])
            P.stt(yc[0:R, :], yc[0:R, :], st1[0:R, 1:2], ln[Cc][0:R, :], ALU.mult, ALU.mult)
            P.copy(st1[0:R, 2:3], psS[0:R, 0:1], eng="act")
            P.stt(yc[0:R, :], V_hs, st1[0:R, 2:3], yc[0:R, :], ALU.mult, ALU.add)
            pT = P.ps()
            P.transpose(pT[0:64, 0:R], yc[0:R, :], ident[0:R, 0:R])
            P.tt(yTr[0:64, :, sl], pT[0:64, 0:R].re("p (h t) -> p h t", t=Cc), gT[0:64, :, sl], ALU.mult)
        if last:
            P.dma(do["o_rwkv_p"][l].rearrange("(j p) v -> p j v", p=128), Hr[l].v(), is_out=True)

    def la_tail(G, S, l, T, pn, ns, qT, kT, vT, eb, Hp, st_key, out_s, out_p, ng, sgate, oT, last):
        NH = G["NH"]
        NJk = G["NJ"]
        Hs = [P.alloc(f"la_Hs{i}", [NJk, 128]) for i in range(2)]
        ysb = P.alloc("la_y", [128])
        st1 = P.alloc("la_st", [4])
        for (kind, c0, Cc, q) in chunks_of(pn, ns, cfg.CL):
            R = NH * Cc
            if kind == "p":
                Hst = Hp
            else:
                Hst = Hs[q % 2]
                P.dma(Hst.v(), di[st_key][l, q].rearrange("(j p) v -> p j v", p=128))
            sl = slice(c0, c0 + Cc)
            psY, V_hs, _ = la_chunk(P, G, S, Cc, qT[:, :, sl], kT[:, :, sl], vT[:, :, sl], eb[:, :, c0 + Cc - 1], Hst)
            if kind == "s":
                P.dma(do[out_s][l, q].rearrange("(j p) v -> p j v", p=128), Hst.v(), is_out=True)
            P.act(ysb[0:R, :], psY, AF.Square)
            P.rsum(st1[0:R, 0:1], ysb[0:R, :])
            P.act(st1[0:R, 0:1], st1[0:R, 0:1], AF.Sqrt, bias=EPS, scale=1.0 / 128)
            P.recip(st1[0:R, 0:1], st1[0:R, 0:1])
            P.stt(ysb[0:R, :], psY, st1[0:R, 0:1], ng[Cc][0:R, :], ALU.mult, ALU.mult)
            pT = P.ps()
            P.transpose(pT[:, 0:R], ysb[0:R, :], ident[0:R, 0:R])
            P.tt(oT[:, :, sl], pT[:, 0:R].re("p (h t) -> p h t", t=Cc), sgate[:, :, sl], ALU.mult)
        if last:
            P.dma(do[out_p][l].rearrange("(j p) v -> p j v", p=128), Hp.v(), is_out=True)

    def hgrn(ti, l, T, pn, ns, oT):
        last = ti == ntile - 1
        P.off = mix_base
        pt = P.alloc("pt_h", [4 * NJ, TMAX])
        project(T, cfg.ct_hq, 4 * NJ, pt)
        ng = {}
        for Cc in sorted(set(c_[2] for c_ in chunks_of(pn, ns, cfg.CL))):
            ng[Cc] = P.alloc(f"hg_n{Cc}", [128])
            P.dma(ng[Cc][0:HH * Cc], di[f"hg_n{Cc}"][l])
        qT, fT, iT, gT_ = (pt[:, k * NJ:(k + 1) * NJ, :] for k in range(4))
        lf = P.alloc("hg_lf", [NJ, TMAX])
        cs = P.alloc("hg_cs", [NJ, TMAX])
        eb = P.alloc("hg_eb", [NJ, TMAX])
        P.act(qT[:, :, 0:T], qT[:, :, 0:T], AF.Silu)
        P.act(gT_[:, :, 0:T], gT_[:, :, 0:T], AF.Silu)
        P.act(fT[:, :, 0:T], fT[:, :, 0:T], AF.Sigmoid, scale=-1.0)
        for j in range(NJ):
            P.ts(fT[:, j, 0:T], fT[:, j, 0:T], oml[:, l, j:j + 1], HGRN_MAX_INPUT, ALU.mult, ALU.min)
            P.act(lf[:, j, 0:T], fT[:, j, 0:T], AF.Ln, bias=1.0, scale=-1.0)
            P.scan(cs[:, j, 0:T], cmk2(T), lf[:, j, 0:T], 0.0)
        P.act(eb[:, :, 0:T], cs[:, :, 0:T], AF.Exp)
        P.act(lf[:, :, 0:T], cs[:, :, 0:T], AF.Exp, scale=-1.0)
        P.tt(qT[:, :, 0:T], qT[:, :, 0:T], eb[:, :, 0:T], ALU.mult)
        P.tt(fT[:, :, 0:T], fT[:, :, 0:T], lf[:, :, 0:T], ALU.mult)
        S = la_scratch(G_h, False)
        la_tail(G_h, S, l, T, pn, ns, qT, fT, iT, eb, Hh[l], "st_hgrn", "o_hgrn_s", "o_hgrn_p", ng, gT_, oT, last)

    def gla(ti, l, T, pn, ns, oT):
        last = ti == ntile - 1
        P.off = mix_base
        nct = cfg.NCT - cfg.ct_gq
        pt = P.alloc("pt_g", [nct, TMAX])
        project(T, cfg.ct_gq, nct, pt)
        ng = {}
        for Cc in sorted(set(c_[2] for c_ in chunks_of(pn, ns, cfg.CL))):
            ng[Cc] = P.alloc(f"gl_n{Cc}", [128])
            P.dma(ng[Cc][0:GH * Cc], di[f"gl_n{Cc}"][l])
        gku = P.alloc("gk_up", [GW // 2])
        P.dma(gku[0:16, :], di["gk_up"][l])
        negb = P.alloc("gl_negb", [GQT])
        P.ts(negb.v(), vc(l, "gk_b"), -1.0, None, ALU.mult)
        o = 0
        qT = pt[:, o:o + GQT, :]; o += GQT
        kT = pt[:, o:o + GQT, :]; o += GQT
        vT = pt[:, o:o + NJ, :]; o += NJ
        glo = pt[:, o, :]; o += 1
        gT_ = pt[:, o:o + NJ, :]
        ll = P.alloc("gl_l", [GQT, TMAX])
        cs = P.alloc("gl_cs", [GQT, TMAX])
        eb = P.alloc("gl_eb", [GQT, TMAX])
        for j in range(GQT):
            ps = P.ps()
            P.mm(ps[:, 0:T], gku[0:16, j * 128:(j + 1) * 128], glo[0:16, 0:T])
            P.act(ll[:, j, 0:T], ps[:, 0:T], AF.Exp, bias=negb[:, j:j + 1], scale=-1.0)
            P.act(ll[:, j, 0:T], ll[:, j, 0:T], AF.Ln, bias=1.0)
            P.scan(cs[:, j, 0:T], cmk2(T), ll[:, j, 0:T], 0.0)
        P.act(gT_[:, :, 0:T], gT_[:, :, 0:T], AF.Silu)
        P.act(eb[:, :, 0:T], cs[:, :, 0:T], AF.Exp, scale=-1.0 / 16)
        P.act(ll[:, :, 0:T], cs[:, :, 0:T], AF.Exp, scale=1.0 / 16)
        P.stt(qT[:, :, 0:T], qT[:, :, 0:T], 0.125, eb[:, :, 0:T], ALU.mult, ALU.mult)
        P.tt(kT[:, :, 0:T], kT[:, :, 0:T], ll[:, :, 0:T], ALU.mult)
        S = la_scratch(G_g, False)
        la_tail(G_g, S, l, T, pn, ns, qT, kT, vT, eb, Hg[l], "st_gla", "o_gla_s", "o_gla_p", ng, gT_, oT, last)

    def s5(ti, l, T, pn, ns, oT):
        last = ti == ntile - 1
        P.off = mix_base
        pt = P.alloc("pt_s", [NJ, TMAX])
        project(T, cfg.ct_s5, NJ, pt)
        wgl = P.alloc("s5_wglu", [NJ, GW])
        P.dma(wgl.v(), di["s5_wglu"][l].rearrange("(c p) n -> p c n", p=128))
        BXr, BXi = P.alloc("BXr", [S5C, 128]), P.alloc("BXi", [S5C, 128])
        bbr, bbi = P.alloc("bbr", [S5C, 128]), P.alloc("bbi", [S5C, 128])
        CXr, CXi = P.alloc("CXr", [S5C, 128]), P.alloc("CXi", [S5C, 128])
        for b_ in (BXr, BXi, CXr, CXi):
            P.memset(b_.v(), 0.0)
        for b8 in range(8):
            for (dst, key) in ((BXr, "s5_Bt_re"), (BXi, "s5_Bt_im")):
                P.dma(dst[b8 * 16:(b8 + 1) * 16, (b8 // 2)::4, (b8 % 2) * 64:(b8 % 2) * 64 + 64],
                      di[key][l].rearrange("(a b) c p -> b c a p", b=8)[b8])
            for (dst, key) in ((CXr, "s5_Ct_re"), (CXi, "s5_Ct_im")):
                P.dma(dst[(b8 % 2) * 64:(b8 % 2) * 64 + 64, (b8 // 2)::4, b8 * 16:(b8 + 1) * 16],
                      di[key][l].rearrange("(a b) p c -> b p a c", b=8)[b8])
        P.ts(CXi.v(), CXi.v(), -1.0, None, ALU.mult)
        sc = [P.alloc(f"s5c_{i}", [S5C]) for i in range(12)]
        dtv, ar, th, rho, sn, cs_, abre, abim, den, core, coim, tmp = sc
        P.act(dtv.v(), vc(l, "logdt"), AF.Exp)
        P.tt(ar.v(), vc(l, "A_re"), dtv.v(), ALU.mult)
        P.tt(th.v(), vc(l, "A_im"), dtv.v(), ALU.mult)
        P.act(rho.v(), ar.v(), AF.Exp)
        range_reduce(sn.v(), th.v(), tmp.v())
        P.act(sn.v(), sn.v(), AF.Sin)
        P.ts(cs_.v(), th.v(), math.pi / 2, None, ALU.add)
        range_reduce(cs_.v(), cs_.v(), tmp.v())
        P.act(cs_.v(), cs_.v(), AF.Sin)
        P.tt(abre.v(), rho.v(), cs_.v(), ALU.mult)
        P.tt(abim.v(), rho.v(), sn.v(), ALU.mult)
        P.tt(den.v(), vc(l, "A_re"), vc(l, "A_re"), ALU.mult)
        P.tt(tmp.v(), vc(l, "A_im"), vc(l, "A_im"), ALU.mult)
        P.tt(den.v(), den.v(), tmp.v(), ALU.add)
        P.recip(den.v(), den.v())
        P.ts(abre.v(), abre.v(), -1.0, None, ALU.add)
        P.tt(core.v(), abre.v(), vc(l, "A_re"), ALU.mult)
        P.tt(tmp.v(), abim.v(), vc(l, "A_im"), ALU.mult)
        P.tt(core.v(), core.v(), tmp.v(), ALU.add)
        P.tt(core.v(), core.v(), den.v(), ALU.mult)
        P.tt(coim.v(), abim.v(), vc(l, "A_re"), ALU.mult)
        P.tt(tmp.v(), abre.v(), vc(l, "A_im"), ALU.mult)
        P.tt(coim.v(), coim.v(), tmp.v(), ALU.subtract)
        P.tt(coim.v(), coim.v(), den.v(), ALU.mult)
        dg = [P.alloc(f"s5dg{i}", [128]) for i in range(2)]
        t1, t2 = P.alloc("s5t1", [128]), P.alloc("s5t2", [128])
        for oc in range(S5C):
            P.ts(dg[0].v(), ident, core[:, oc:oc + 1], None, ALU.mult)
            P.ts(dg[1].v(), ident, coim[:, oc:oc + 1], None, ALU.mult)
            pr, pi_ = P.ps(), P.ps()
            P.mm(pr[:, 0:128], ones_f, dg[0].v())
            P.mm(pi_[:, 0:128], ones_f, dg[1].v())
            P.tt(t1.v(), BXr[:, oc, :], pr[:, 0:128], ALU.mult)
            P.tt(t2.v(), BXi[:, oc, :], pi_[:, 0:128], ALU.mult)
            P.tt(bbr[:, oc, :], t1.v(), t2.v(), ALU.subtract)
            P.tt(t1.v(), BXi[:, oc, :], pr[:, 0:128], ALU.mult)
            P.tt(t2.v(), BXr[:, oc, :], pi_[:, 0:128], ALU.mult)
            P.tt(bbi[:, oc, :], t1.v(), t2.v(), ALU.add)
        if ns:
            sts = [P.alloc(f"s5st{i}", [S5C, NS]) for i in range(2)]
            osts = [P.alloc(f"s5ost{i}", [S5C, NS]) for i in range(2)]
            P.dma(sts[0].v(), di["st_s5re"][l])
            P.dma(sts[1].v(), di["st_s5im"][l])
        W = [P.alloc(f"s5w{i}", [TMAX]) for i in range(9)]
        a1, snt, cst, zre, zim, dec, u1, u2, u3 = W
        hb = P.alloc("s5hb", [4, 2, TMAX])
        ygl = P.alloc("s5ygl", [NJ, TMAX])
        for oc in range(S5C):
            u = pt[:, oc // 4, 0:T]
            pbr, pbi = P.ps(), P.ps()
            P.mm(pbr[:, 0:T], bbr[:, oc, :], u)
            P.mm(pbi[:, 0:T], bbi[:, oc, :], u)
            P.ts(a1[:, 0:T], posv(T), th[:, oc:oc + 1], None, ALU.mult)
            range_reduce(snt[:, 0:T], a1[:, 0:T], u1[:, 0:T])
            P.act(snt[:, 0:T], snt[:, 0:T], AF.Sin)
            P.ts(a1[:, 0:T], a1[:, 0:T], math.pi / 2, None, ALU.add)
            range_reduce(cst[:, 0:T], a1[:, 0:T], u1[:, 0:T])
            P.act(cst[:, 0:T], cst[:, 0:T], AF.Sin)
            P.tt(u1[:, 0:T], cst[:, 0:T], pbr[:, 0:T], ALU.mult)
            P.tt(u2[:, 0:T], snt[:, 0:T], pbi[:, 0:T], ALU.mult)
            P.tt(zre[:, 0:T], u1[:, 0:T], u2[:, 0:T], ALU.add)
            P.tt(u1[:, 0:T], cst[:, 0:T], pbi[:, 0:T], ALU.mult)
            P.tt(u2[:, 0:T], snt[:, 0:T], pbr[:, 0:T], ALU.mult)
            P.tt(zim[:, 0:T], u1[:, 0:T], u2[:, 0:T], ALU.subtract)
            P.ts(dec[:, 0:T], smk(T), rho[:, oc:oc + 1], None, ALU.mult)
            for i, z in enumerate((zre, zim)):
                P.stt(z[:, 0:1], s5c[l][i][:, oc:oc + 1], rho[:, oc:oc + 1], z[:, 0:1], ALU.mult, ALU.add)
                if ns:
                    zv = z[:, pn:T].re("p (s t) -> p s t", t=TS)[:, :, 0:1]
                    P.stt(zv, sts[i][:, oc, :].us(2), rho[:, oc:oc + 1], zv, ALU.mult, ALU.add)
            P.scan(u1[:, 0:T], dec[:, 0:T], zre[:, 0:T], 0.0)
            P.scan(u2[:, 0:T], dec[:, 0:T], zim[:, 0:T], 0.0)
            hre, him = hb[:, oc % 4, 0, :], hb[:, oc % 4, 1, :]
            P.tt(zre[:, 0:T], cst[:, 0:T], u1[:, 0:T], ALU.mult)
            P.tt(zim[:, 0:T], snt[:, 0:T], u2[:, 0:T], ALU.mult)
            P.tt(hre[:, 0:T], zre[:, 0:T], zim[:, 0:T], ALU.subtract)
            P.tt(zre[:, 0:T], cst[:, 0:T], u2[:, 0:T], ALU.mult)
            P.tt(zim[:, 0:T], snt[:, 0:T], u1[:, 0:T], ALU.mult)
            P.tt(him[:, 0:T], zre[:, 0:T], zim[:, 0:T], ALU.add)
            for i, hh_ in enumerate((hre, him)):
                P.copy(s5c[l][i][:, oc:oc + 1], hh_[:, pn - 1:pn])
                if ns:
                    P.copy(osts[i][:, oc, :].us(2), hh_[:, pn:T].re("p (s t) -> p s t", t=TS)[:, :, TS - 1:TS])
            if oc % 4 == 3:
                ot = oc // 4
                py = P.ps()
                for i in range(4):
                    P.mm(py[:, 0:T], CXr[:, 4 * ot + i, :], hb[:, i, 0, 0:T], start=(i == 0), stop=False)
                    P.mm(py[:, 0:T], CXi[:, 4 * ot + i, :], hb[:, i, 1, 0:T], start=False, stop=(i == 3))
                P.stt(ygl[:, ot, 0:T], pt[:, ot, 0:T], vc(l, "s5_D", ot), py[:, 0:T], ALU.mult, ALU.add)
                P.act(ygl[:, ot, 0:T], ygl[:, ot, 0:T], AF.Gelu_apprx_tanh)
        for ot in range(NJ):
            ps = P.ps()
            for kt in range(NJ):
                P.mm(ps[:, 0:T], wgl[:, kt, ot * 128:(ot + 1) * 128], ygl[:, kt, 0:T], start=(kt == 0),
                     stop=(kt == NJ - 1))
            P.act(u3[:, 0:T], ps[:, 0:T], AF.Sigmoid, bias=vc(l, "s5_bglu", ot))
            P.tt(oT[:, ot, 0:T], ygl[:, ot, 0:T], u3[:, 0:T], ALU.mult)
        if ns:
            P.dma(do["o_s5re_s"][l], osts[0].v(), is_out=True)
            P.dma(do["o_s5im_s"][l], osts[1].v(), is_out=True)
        if last:
            P.dma(do["o_s5re_p"][l], s5c[l][0].v(), is_out=True)
            P.dma(do["o_s5im_p"][l], s5c[l][1].v(), is_out=True)

    def layer(ti, l, T, pn, ns):
        nonlocal mix_base
        last = ti == ntile - 1
        P.off = phase_base
        yTr = P.alloc("yTr", [RH, TMAX], BF16)
        mixB = [P.alloc(f"mixB{i}", [NJ, TMAX], BF16) for i in range(3)]
        mix_base = P.off
        rmsnorm(T, "norm_mix", l, h)
        mix_base = P.off
        rwkv(ti, l, T, pn, ns, yTr)
        s5(ti, l, T, pn, ns, mixB[0])
        hgrn(ti, l, T, pn, ns, mixB[1])
        gla(ti, l, T, pn, ns, mixB[2])
        for og in range(D // 256):
            (wa_, wb_), b = WS.get("wout")
            for i in range(2):
                oc = og * 2 + i
                ps = P.ps()
                for hd in range(RH):
                    P.mm(ps[:, 0:T], wa_[0:64, hd, i * 128:(i + 1) * 128], yTr[0:64, hd, 0:T], start=(hd == 0),
                         stop=False)
                for k in range(3 * NJ):
                    P.mm(ps[:, 0:T], wb_[:, k, i * 128:(i + 1) * 128], mixB[k // NJ][:, k % NJ, 0:T], start=False,
                         stop=(k == 3 * NJ - 1))
                P.tt(x[:, oc, 0:T], x[:, oc, 0:T], ps[:, 0:T], ALU.add)
        P.off = mix_base
        rmsnorm(T, "norm_ffn", l, h)
        aT = P.alloc("aT", [NFT, TMAX], BF16)
        uxp = [P.alloc(f"uxp{i}", [TMAX + 2]) for i in range(2)]
        acc = [P.alloc(f"acc{i}", [TMAX]) for i in range(2)]
        if ns:
            stc = P.alloc("stc", [NFT, NS, 2])
            ostc = P.alloc("ostc", [NFT, NS, 2])
            uxs = [P.alloc(f"uxs{i}", [NS, TS + 2]) for i in range(2)]
            P.dma(stc.v(), di["st_conv"][l])
        j = 0
        while j < NFT:
            (wv,), b = WS.get("wup")
            for jj in range(b[2]):
                pu, pg = P.ps(), P.ps()
                for c in range(NDC):
                    P.mm(pu[:, 0:T], wv[:, c, 0, jj * 128:(jj + 1) * 128], h[:, c, 0:T], start=(c == 0),
                         stop=(c == NDC - 1))
                for c in range(NDC):
                    P.mm(pg[:, 0:T], wv[:, c, 1, jj * 128:(jj + 1) * 128], h[:, c, 0:T], start=(c == 0),
                         stop=(c == NDC - 1))
                ux, ac = uxp[j % 2], acc[j % 2]
                w0, w1, w2, cb = (vc(l, f"conv_w{i}", j) for i in range(3)), None, None, vc(l, "conv_b", j)
                w0, w1, w2 = list(w0)
                P.copy(ux[:, 0:2], convc[l][:, j, :])
                P.copy(ux[:, 2:2 + pn], pu[:, 0:pn], eng="act")
                P.copy(convc[l][:, j, :], ux[:, pn:pn + 2])
                P.ts(ac[:, 0:pn], ux[:, 0:pn], w0, cb, ALU.mult, ALU.add)
                P.stt(ac[:, 0:pn], ux[:, 1:pn + 1], w1, ac[:, 0:pn], ALU.mult, ALU.add)
                P.stt(ac[:, 0:pn], ux[:, 2:pn + 2], w2, ac[:, 0:pn], ALU.mult, ALU.add)
                if ns:
                    us_ = uxs[j % 2]
                    acs = ac[:, pn:T].re("p (s t) -> p s t", t=TS)
                    P.copy(us_[:, :, 0:2], stc[:, j, :, :])
                    P.copy(us_[:, :, 2:2 + TS], pu[:, pn:T].re("p (s t) -> p s t", t=TS), eng="act")
                    P.copy(ostc[:, j, :, :], us_[:, :, TS:TS + 2])
                    P.ts(acs, us_[:, :, 0:TS], w0, cb, ALU.mult, ALU.add)
                    P.stt(acs, us_[:, :, 1:TS + 1], w1, acs, ALU.mult, ALU.add)
                    P.stt(acs, us_[:, :, 2:TS + 2], w2, acs, ALU.mult, ALU.add)
                P.act(ac[:, 0:T], ac[:, 0:T], AF.Gelu_apprx_tanh)
                P.tt(aT[:, j, 0:T], ac[:, 0:T], pg[:, 0:T], ALU.mult)
                j += 1
        if ns:
            P.dma(do["o_conv_s"][l], ostc.v(), is_out=True)
        if last:
            P.dma(do["o_conv_p"][l], convc[l].v(), is_out=True)
        for og in range(D // 512):
            pss = [P.ps() for _ in range(4)]
            k0 = 0
            while k0 < NFT:
                (wv,), b = WS.get("wdown")
                assert b[1] == og and b[2] == k0
                for kk_ in range(b[3]):
                    kt = k0 + kk_
                    for i in range(4):
                        P.mm(pss[i][:, 0:T], wv[:, kk_, i * 128:(i + 1) * 128], aT[:, kt, 0:T], start=(kt == 0),
                             stop=(kt == NFT - 1))
                k0 += b[3]
            for i in range(4):
                oc = og * 4 + i
                P.tt(x[:, oc, 0:T], x[:, oc, 0:T], pss[i][:, 0:T], ALU.add)

    mix_base = phase_base
    for ti, (p0, pn, ns) in enumerate(cfg.tiles):
        T = pn + ns * TS
        P.dma(x[:, :, 0:pn], di["xin"][:, p0:p0 + pn].rearrange("(c p) t -> p c t", p=128))
        if ns:
            P.dma(x[:, :, pn:T], di["xin"][:, cfg.NP:cfg.NP + ns * TS].rearrange("(c p) t -> p c t", p=128))
        P.dma(tmask.v(), di["tmask"][ti].rearrange("k p t -> p k t"))
        for l in range(L):
            layer(ti, l, T, pn, ns)
        P.off = mix_base
        yo = P.alloc("yo", [NDC, TMAX])
        sq = [P.alloc(f"fsq{i}", [TMAX], BF16) for i in range(2)]
        rstd = P.alloc("frstd", [TMAX])
        ps = P.ps()
        for c in range(NDC):
            s_ = sq[c % 2]
            P.act(s_[:, 0:T], x[:, c, 0:T], AF.Square)
            P.mm(ps[:, 0:T], ones_bf.v(), s_[:, 0:T], start=(c == 0), stop=(c == NDC - 1))
        P.act(rstd[:, 0:T], ps[:, 0:T], AF.Sqrt, bias=EPS, scale=1.0 / D)
        P.recip(rstd[:, 0:T], rstd[:, 0:T])
        for c in range(NDC):
            P.stt(yo[:, c, 0:T], x[:, c, 0:T], normf[:, c:c + 1], rstd[:, 0:T], ALU.mult, ALU.mult)
        P.dma(do["yT"][:, p0:p0 + pn].rearrange("(c p) t -> p c t", p=128), yo[:, :, 0:pn], is_out=True)
        if ns:
            P.dma(do["yT"][:, cfg.NP:cfg.NP + ns * TS].rearrange("(c p) t -> p c t", p=128), yo[:, :, pn:T],
                  is_out=True)
    P.emit()
    return nc, P


REAL = dict(D=2048, L=4, NP=2064, NS=16, TILE=256, C=16)
_cache = {}


def kernel(**inputs):
    cfg = Cfg(**REAL)
    inp = {k: np.asarray(v) for k, v in inputs.items()}
    B = inp["x_prompt"].shape[0]
    ncore = 8
    sh = pack_shared(cfg, inp)
    in_maps = []
    for c in range(ncore):
        m = dict(sh)
        m.update(pack_core(cfg, inp, c % B, c * cfg.NS))
        in_maps.append(m)
    nc, _ = build_program(cfg)
    res = run_bass_kernel_spmd(nc, in_maps, core_ids=list(range(ncore)))
    outs = [unpack_core(cfg, r) for r in res.results]
    cat = lambda k: np.concatenate([o[k] for o in outs], axis=1)
    stack_p = lambda k: np.stack([outs[b][k] for b in range(B)], axis=1)
    f = lambda a: np.ascontiguousarray(a, dtype=np.float32)
    return (f(np.stack([outs[b]["y_p"] for b in range(B)], axis=0)),
            f(np.concatenate([o["y_s"] for o in outs], axis=0)),
            f(stack_p("rwkv_p")), f(cat("rwkv_s")), f(stack_p("shift_p")), f(cat("shift_s")),
            f(stack_p("s5re_p")), f(cat("s5re_s")), f(stack_p("s5im_p")), f(cat("s5im_s")),
            f(stack_p("hgrn_p")), f(cat("hgrn_s")), f(stack_p("gla_p")), f(cat("gla_s")),
            f(stack_p("conv_p")), f(cat("conv_s")))
```
